# Optimizing a Trainium2 kernel written in Bass

```python
import jax, jax.numpy as jnp
from jax import lax
import numpy as np

D_MODEL = 1024
BATCH = 16
SEQ = 2048
DEPTH = 4

N_MIXERS = 2
N_MLA_LAYERS = (DEPTH + 1) // 2
N_GMLP_LAYERS = DEPTH // 2
MLA_HEADS = 8
QK_NOPE_DIM = 128
QK_ROPE_DIM = 64
V_HEAD_DIM = 128
Q_LORA_RANK = 384
KV_LORA_RANK = 256
ROPE_BASE = 10000.0
Q_BLOCK = 128
GMLP_CHUNK = 128
GMLP_HALF = 2 * D_MODEL
GMLP_GROUPS = 8
GMLP_GROUP_DIM = GMLP_HALF // GMLP_GROUPS
D_FF = 4 * D_MODEL
PLE_DIM = 256
NORM_EPS = 1e-6
MAX_POS_OFFSET = 4096

kernel_name = "hybrid_mla_chunked_gmlp_trunk"


def rms_norm(x, g):
    xf = x.astype(jnp.float32)
    y = xf * lax.rsqrt(jnp.mean(xf * xf, axis=-1, keepdims=True) + NORM_EPS)
    return (y * g.astype(jnp.float32)).astype(x.dtype)


def layer_norm(x, g, b):
    xf = x.astype(jnp.float32)
    mu = jnp.mean(xf, axis=-1, keepdims=True)
    xc = xf - mu
    y = xc * lax.rsqrt(jnp.mean(xc * xc, axis=-1, keepdims=True) + NORM_EPS)
    return (y * g.astype(jnp.float32) + b.astype(jnp.float32)).astype(x.dtype)


def rope_cos_sin(positions):
    inv_freq = ROPE_BASE ** (-(jnp.arange(0, QK_ROPE_DIM, 2, dtype=jnp.float32) / QK_ROPE_DIM))
    ang = positions.astype(jnp.float32)[..., None] * inv_freq
    return jnp.cos(ang), jnp.sin(ang)


def apply_rope(x, cos, sin):
    x1, x2 = jnp.split(x.astype(jnp.float32), 2, axis=-1)
    return jnp.concatenate([x1 * cos - x2 * sin, x2 * cos + x1 * sin], axis=-1).astype(x.dtype)


def mla_mixer(hn, cos, sin, w_down, q_lora_g, kv_lora_g, w_uq, w_ukv,
              q_nope_g, q_rope_g, k_nope_g, k_rope_g, w_out):
    B, S, _ = hn.shape
    H = MLA_HEADS
    lat = hn @ w_down
    c_q, c_kv, k_rope = jnp.split(lat, [Q_LORA_RANK, Q_LORA_RANK + KV_LORA_RANK], axis=-1)
    c_q = rms_norm(c_q, q_lora_g)
    c_kv = rms_norm(c_kv, kv_lora_g)
    q = (c_q @ w_uq).reshape(B, S, H, QK_NOPE_DIM + QK_ROPE_DIM)
    q_nope, q_rope = jnp.split(q, [QK_NOPE_DIM], axis=-1)
    kv = (c_kv @ w_ukv).reshape(B, S, H, QK_NOPE_DIM + V_HEAD_DIM)
    k_nope, v = jnp.split(kv, [QK_NOPE_DIM], axis=-1)
    q_nope = rms_norm(q_nope, q_nope_g)
    q_rope = apply_rope(rms_norm(q_rope, q_rope_g), cos[:, :, None], sin[:, :, None])
    k_nope = rms_norm(k_nope, k_nope_g)
    k_rope = apply_rope(rms_norm(k_rope, k_rope_g), cos, sin)
    scale = (QK_NOPE_DIM + QK_ROPE_DIM) ** -0.5
    outs = []
    for j in range(S // Q_BLOCK):
        q0 = j * Q_BLOCK
        kend = q0 + Q_BLOCK
        s = (jnp.einsum('bqhd,bkhd->bhqk', q_nope[:, q0:kend], k_nope[:, :kend])
             + jnp.einsum('bqhr,bkr->bhqk', q_rope[:, q0:kend], k_rope[:, :kend]))
        s = s.astype(jnp.float32) * scale
        causal = jnp.arange(kend)[None, :] <= (q0 + jnp.arange(Q_BLOCK))[:, None]
        s = jnp.where(causal, s, -jnp.inf)
        pr = jax.nn.softmax(s, axis=-1).astype(v.dtype)
        outs.append(jnp.einsum('bhqk,bkhd->bqhd', pr, v[:, :kend]))
    o = jnp.concatenate(outs, axis=1).reshape(B, S, H * V_HEAD_DIM)
    return o @ w_out


def gmlp_mixer(hn, w_in, ln_g, ln_b, w_s, b_s, w_out):
    B, S, _ = hn.shape
    z = jax.nn.gelu(hn @ w_in)
    u, v = jnp.split(z, 2, axis=-1)
    v = layer_norm(v, ln_g, ln_b)
    v = v.reshape(B, S // GMLP_CHUNK, GMLP_CHUNK, GMLP_GROUPS, GMLP_GROUP_DIM)
    mask = jnp.tril(jnp.ones((GMLP_CHUNK, GMLP_CHUNK), dtype=w_s.dtype))
    ws = w_s * mask
    sv = jnp.einsum('gts,bnsgd->bntgd', ws, v) + b_s.T[None, None, :, :, None]
    y = u * sv.reshape(B, S, GMLP_HALF)
    return y @ w_out


def setup_inputs(seed: int = 0) -> dict:
    key = jax.random.key(seed)
    ks = iter(jax.random.split(key, 40))

    def nrm(shape, scale):
        return jax.random.normal(next(ks), shape, dtype=jnp.float32) * scale

    def gain(shape):
        return 1.0 + nrm(shape, 0.02)

    nA, nB, D = N_MLA_LAYERS, N_GMLP_LAYERS, D_MODEL
    x = nrm((BATCH, SEQ, D), 1.0)
    p = nrm((DEPTH, BATCH, SEQ, PLE_DIM), 1.0)
    offs = jax.random.randint(next(ks), (BATCH, 1), 0, MAX_POS_OFFSET, dtype=jnp.int32)
    positions = offs + jnp.arange(SEQ, dtype=jnp.int32)[None, :]
    down_w = Q_LORA_RANK + KV_LORA_RANK + QK_ROPE_DIM
    return {
        "x": x,
        "p": p,
        "positions": positions,
        "norm_mix": gain((DEPTH, D)),
        "norm_ffn": gain((DEPTH, D)),
        "norm_ple": gain((DEPTH, D)),
        "mla_w_down": nrm((nA, D, down_w), D ** -0.5),
        "mla_q_lora_g": gain((nA, Q_LORA_RANK)),
        "mla_kv_lora_g": gain((nA, KV_LORA_RANK)),
        "mla_w_uq": nrm((nA, Q_LORA_RANK, MLA_HEADS * (QK_NOPE_DIM + QK_ROPE_DIM)), Q_LORA_RANK ** -0.5),
        "mla_w_ukv": nrm((nA, KV_LORA_RANK, MLA_HEADS * (QK_NOPE_DIM + V_HEAD_DIM)), KV_LORA_RANK ** -0.5),
        "mla_q_nope_g": gain((nA, QK_NOPE_DIM)),
        "mla_q_rope_g": gain((nA, QK_ROPE_DIM)),
        "mla_k_nope_g": gain((nA, QK_NOPE_DIM)),
        "mla_k_rope_g": gain((nA, QK_ROPE_DIM)),
        "mla_w_out": nrm((nA, MLA_HEADS * V_HEAD_DIM, D), 0.5 * (MLA_HEADS * V_HEAD_DIM) ** -0.5),
        "gmlp_w_in": nrm((nB, D, 2 * GMLP_HALF), D ** -0.5),
        "gmlp_ln_g": gain((nB, GMLP_HALF)),
        "gmlp_ln_b": nrm((nB, GMLP_HALF), 0.02),
        "gmlp_w_s": nrm((nB, GMLP_GROUPS, GMLP_CHUNK, GMLP_CHUNK), GMLP_CHUNK ** -0.5),
        "gmlp_b_s": 1.0 + nrm((nB, GMLP_GROUPS, GMLP_CHUNK), 0.1),
        "gmlp_w_out": nrm((nB, GMLP_HALF, D), 0.5 * GMLP_HALF ** -0.5),
        "ffn_w_up": nrm((DEPTH, D, D_FF), D ** -0.5),
        "ffn_w_down": nrm((DEPTH, D_FF, D), 0.5 * D_FF ** -0.5),
        "ple_w_gate": nrm((DEPTH, D, D), D ** -0.5),
        "ple_w_proj": nrm((DEPTH, PLE_DIM, D), PLE_DIM ** -0.5),
    }


def reference(x, p, positions, norm_mix, norm_ffn, norm_ple,
              mla_w_down, mla_q_lora_g, mla_kv_lora_g, mla_w_uq, mla_w_ukv,
              mla_q_nope_g, mla_q_rope_g, mla_k_nope_g, mla_k_rope_g, mla_w_out,
              gmlp_w_in, gmlp_ln_g, gmlp_ln_b, gmlp_w_s, gmlp_b_s, gmlp_w_out,
              ffn_w_up, ffn_w_down, ple_w_gate, ple_w_proj):
    cos, sin = rope_cos_sin(positions)
    h = x
    for i in range(DEPTH):
        hn = rms_norm(h, norm_mix[i])
        j = i // N_MIXERS
        if i % N_MIXERS == 0:
            mix = mla_mixer(hn, cos, sin, mla_w_down[j], mla_q_lora_g[j], mla_kv_lora_g[j],
                            mla_w_uq[j], mla_w_ukv[j], mla_q_nope_g[j], mla_q_rope_g[j],
                            mla_k_nope_g[j], mla_k_rope_g[j], mla_w_out[j])
        else:
            mix = gmlp_mixer(hn, gmlp_w_in[j], gmlp_ln_g[j], gmlp_ln_b[j],
                             gmlp_w_s[j], gmlp_b_s[j], gmlp_w_out[j])
        h = h + mix
        hn = rms_norm(h, norm_ffn[i])
        h = h + jnp.square(jax.nn.relu(hn @ ffn_w_up[i])) @ ffn_w_down[i]
        hn = rms_norm(h, norm_ple[i])
        h = h + jax.nn.sigmoid(hn @ ple_w_gate[i]) * (p[i] @ ple_w_proj[i])
    return h
```

```python
import contextlib
import math

import numpy as np
import concourse.bass as bass
import concourse.mybir as mybir
from concourse.bass_utils import run_bass_kernel_spmd

F32 = mybir.dt.float32
BF16 = mybir.dt.bfloat16
I32 = mybir.dt.int32
AF = mybir.ActivationFunctionType
ALU = mybir.AluOpType

N_CORES = 8
D = 1024
SEQ = 2048
BATCH = 16
DEPTH = 4
T = 512
NB = SEQ // T
KD = D // 128
NSEQ = BATCH // N_CORES
H = 8
QL, KVL, RD = 384, 256, 64
DFF = 4096
GH = 2048
PLE = 256
EPS = 1e-6
SM_SCALE = (128 + 64) ** -0.5
SM_SHIFT = 16.0

SAME_ENGINE_SYNC = True
CACHE_KINDS = ("gv", "gu", "go")
FUSED = True

C_MIX, C_FFN, C_PLE = 0, 32, 64
C_QL, C_KVL = 96, 102
C_QN, C_KN = 106, 108
C_QR, C_KR = 110, 114
C_LNG, C_LNB = 118, 150
C_INVF = 182
NCP = 184


class Res:
    __slots__ = ("name", "w", "r", "dsem", "dcnt")

    def __init__(self, name):
        self.name = name
        self.w = None
        self.r = {}
        self.dsem = None
        self.dcnt = 0


class Eng:
    def __init__(self, name, h, sem):
        self.name, self.h, self.sem = name, h, sem
        self.cnt = 0
        self.waited = {}


class K:
    def __init__(self, nc, es):
        self.nc, self.es = nc, es
        self.sems = []
        self.eng = {}
        for n, h in (("pe", nc.tensor), ("act", nc.scalar), ("dve", nc.vector),
                     ("pool", nc.gpsimd), ("sp", nc.sync)):
            self.eng[n] = Eng(n, h, self.new_sem("e_" + n))
        self.dsem_by_name = {}
        self.dcount = {}
        self.n_inst = 0

    def new_sem(self, name):
        self.sems.append(self.es.enter_context(self.nc.semaphore(name)))
        return len(self.sems) - 1

    def sb(self, name, shape, dt):
        return self.es.enter_context(self.nc.sbuf_tensor("sb_" + name, shape, dt))

    @staticmethod
    def _deps(reads, writes):
        deps = {}
        for r in reads:
            if r.w is not None and deps.get(r.w[0], 0) < r.w[1]:
                deps[r.w[0]] = r.w[1]
        for w in writes:
            if w.w is not None and deps.get(w.w[0], 0) < w.w[1]:
                deps[w.w[0]] = w.w[1]
            for s, v in w.r.items():
                if deps.get(s, 0) < v:
                    deps[s] = v
        return deps

    def _wait(self, E, deps):
        for s, v in deps.items():
            if s == E.sem and (E.name == "pe" or not SAME_ENGINE_SYNC):
                continue
            if E.waited.get(s, 0) >= v:
                continue
            E.h.wait_ge(self.sems[s], v)
            E.waited[s] = v
            self.n_inst += 1

    def op(self, en, reads, writes, fn):
        E = self.eng[en]
        self._wait(E, self._deps(reads, writes))
        last = fn(E.h)
        E.cnt += 1
        assert E.cnt < 60000
        last.then_inc(self.sems[E.sem], 1)
        ev = (E.sem, E.cnt)
        for r in reads:
            if r.r.get(E.sem, 0) < E.cnt:
                r.r[E.sem] = E.cnt
        for w in writes:
            w.w = ev
            w.r = {}
        self.n_inst += 1
        return ev

    def dma(self, reads, writes, transfers, q="sp"):
        E = self.eng[q]
        self._wait(E, self._deps(reads, writes))
        R = writes[0] if writes else reads[0]
        if R.dsem is None:
            if R.name not in self.dsem_by_name:
                self.dsem_by_name[R.name] = self.new_sem("d_" + R.name)
                self.dcount[self.dsem_by_name[R.name]] = 0
            R.dsem = self.dsem_by_name[R.name]
        for out_ap, in_ap in transfers:
            E.h.dma_start(out=out_ap, in_=in_ap).then_inc(self.sems[R.dsem], 16)
            self.dcount[R.dsem] += 16
            self.n_inst += 1
        cnt = self.dcount[R.dsem]
        assert cnt < 60000
        R.dcnt = cnt
        ev = (R.dsem, cnt)
        for r in reads:
            if r.r.get(R.dsem, 0) < cnt:
                r.r[R.dsem] = cnt
        for w in writes:
            w.w = ev
            w.r = {}
        return ev

    def barrier(self):
        tgt = {E.sem: E.cnt for E in self.eng.values() if E.cnt > 0}
        for sidx, cnt in self.dcount.items():
            if cnt > 0:
                tgt[sidx] = cnt
        for E in self.eng.values():
            d = {s: v for s, v in tgt.items() if s != E.sem}
            self._wait(E, d)


def build_program(layers, nseq=NSEQ, do_mix=True, do_ffn=True, do_ple=True, debug=False):
    nc = bass.Bass("TRN2", target_bir_lowering=False, dynamic_dma_scratch_size=2048)
    NT = nseq * SEQ

    def din(name, shape, dt=F32):
        return nc.dram_tensor(name, shape, dt, kind="ExternalInput").ap()

    xT = din("xT", [D, NT])
    pT = din("pT", [DEPTH * PLE, NT])
    pos = din("pos", [1, NT], I32)
    cp_d = din("cp", [128, NCP])
    bs_d = din("b_s", [2, 1024])
    wsT_d = din("w_sT", [2, 128, 1024])
    w_mdown = din("mla_w_down", [2, D, 704])
    w_muq = din("mla_w_uq", [2, QL, 1536])
    w_mukv = din("mla_w_ukv", [2, KVL, 2048])
    w_mout = din("mla_w_out", [2, D, D])
    w_gin = din("gmlp_w_in", [2, D, 2 * GH])
    w_gout = din("gmlp_w_out", [2, GH, D])
    w_fup = din("ffn_w_up", [DEPTH, D, DFF])
    w_fdn = din("ffn_w_down", [DEPTH, DFF, D])
    w_pg = din("ple_w_gate", [DEPTH, D, D])
    w_pp = din("ple_w_proj", [DEPTH, PLE, D])
    outT = nc.dram_tensor("outT", [D, NT], F32, kind="ExternalOutput").ap()

    es = contextlib.ExitStack()
    with es:
        k = K(nc, es)
        hT = k.sb("hT", [128, KD, SEQ], F32)
        hres = [[Res(f"h{c}_{b}") for b in range(NB)] for c in range(KD)]
        stage = [k.sb(f"stage{i}", [128, 4096], F32) for i in range(2)]
        stres = [Res(f"st{i}") for i in range(2)]
        NSLOT = 3
        wsl = [k.sb(f"wsl{i}", [128, 4096], BF16) for i in range(NSLOT)]
        wres = [Res(f"w{i}") for i in range(NSLOT)]
        NF = 6
        ft = [k.sb(f"ft{i}", [128, T], F32) for i in range(NF)]
        fres = [Res(f"ft{i}") for i in range(NF)]
        NBT = 4
        bt = [k.sb(f"bt{i}", [128, T], BF16) for i in range(NBT)]
        bres = [Res(f"bt{i}") for i in range(NBT)]
        cp = k.sb("cp", [128, NCP], F32)
        cpd = k.sb("cpd", [128, 16], F32)
        ones = k.sb("ones", [128, 128], BF16)
        onesf = k.sb("onesf", [128, 64], F32)
        fold = k.sb("fold", [128, 64], F32)
        fold2 = k.sb("fold2", [128, 64], F32)
        st1 = k.sb("st1", [128, 16], F32)
        st2 = k.sb("st2", [128, 16], F32)
        stt = k.sb("stt", [128, 32], F32)
        cres = Res("const")
        sres = Res("stats")
        ARENA = 80 * 1024 // 2
        arena = k.sb("arena", [128, ARENA], BF16)
        pbank = [es.enter_context(nc.psum_tensor(f"pb{i}", [128, T], F32)) for i in range(8)]
        pres = [Res(f"pb{i}") for i in range(8)]

        rr = {"f": 0, "b": 0, "p": 0, "st": 0, "w": 0}

        def ftmp():
            i = rr["f"] % NF
            rr["f"] += 1
            return ft[i], fres[i]

        def btmp():
            i = rr["b"] % NBT
            rr["b"] += 1
            return bt[i], bres[i]

        bankset = [list(range(8))]

        def bank():
            s = bankset[0]
            i = s[rr["p"] % len(s)]
            rr["p"] += 1
            return pbank[i], pres[i]

        def av(off, n, dt=BF16):
            if dt == BF16:
                return arena[:, off:off + n]
            return arena[:, off:off + 2 * n].bitcast(F32)

        k.dma([], [cres], [(cp[:], cp_d[:, :])])
        k.op("dve", [], [cres], lambda h: h.memset(ones[:], 1.0))
        for i_ in range(2):
            k.op("pool", [], [stres[i_]], lambda h, i_=i_: h.memset(stage[i_][:], 0.0))
        k.op("dve", [], [cres], lambda h: h.memset(onesf[:], 1.0))
        k.op("dve", [], [cres], lambda h: h.memset(cpd[:], -SM_SHIFT))
        k.op("pool", [cres], [cres], lambda h: h.affine_select(
            out=fold[:], in_=onesf[:], pattern=[[-1, 64]], compare_op=ALU.is_equal, fill=0.0, base=0, channel_multiplier=1))
        k.op("pool", [cres], [cres], lambda h: h.affine_select(
            out=fold2[:], in_=onesf[:], pattern=[[-1, 64]], compare_op=ALU.is_equal, fill=0.0, base=-64, channel_multiplier=1))
        k.op("pool", [cres], [cres], lambda h: h.tensor_tensor(out=fold[:], in0=fold[:], in1=fold2[:], op=ALU.add))
        for j in range(2):
            k.op("dve", [cres], [cres], lambda h, j=j: h.tensor_scalar(
                out=cpd[:, j:j + 1], in0=cp[:, C_QN + j:C_QN + j + 1], scalar1=SM_SCALE, scalar2=None, op0=ALU.mult))
            k.op("dve", [cres], [cres], lambda h, j=j: h.tensor_scalar(
                out=cpd[:, 2 + j:3 + j], in0=cp[:, C_QR + j:C_QR + j + 1], scalar1=SM_SCALE, scalar2=None, op0=ALU.mult))

        specs = []
        keyid = {}

        def piece(tag, parts, post=None, key=None):
            key = tag if key is None else key
            if key[0] in CACHE_KINDS:
                keyid.setdefault(key, len(keyid))
            off = 0
            tr = []
            for part in parts:
                ap_, nk, width = part[:3]
                padw = part[3] if len(part) > 3 else width
                tr.append((off, nk, width, padw, ap_))
                off += nk * padw
            specs.append((tag, key, tr, off, post))

        def ffn_specs(l):
            for fg in range(8):
                piece(("fup", l, fg), [(w_fup[l, :, fg * 512:(fg + 1) * 512].rearrange("(k p) n -> p k n", p=128), 8, 512)])
                piece(("fdn", l, fg), [(w_fdn[l, fg * 512:(fg + 1) * 512, :].rearrange("(k p) n -> p k n", p=128), 4, 1024)])

        def ple_specs(l):
            for hf in range(2):
                piece(("pg", l, hf), [(w_pg[l, :, hf * 512:(hf + 1) * 512].rearrange("(k p) n -> p k n", p=128), 8, 512)])
                piece(("pp", l, hf), [(w_pp[l, :, hf * 512:(hf + 1) * 512].rearrange("(k p) n -> p k n", p=128), 2, 512)])

        def gmlp_specs(j):
            for tb in range(NB):
                for vp in range(4):
                    piece(("gv", j, tb, vp), [(w_gin[j, :, GH + vp * 512:GH + (vp + 1) * 512].rearrange("(k p) n -> p k n", p=128), 8, 512)], key=("gv", j, vp))
                for up in range(4):
                    piece(("gu", j, tb, up), [(w_gin[j, :, up * 512:(up + 1) * 512].rearrange("(k p) n -> p k n", p=128), 8, 512)], key=("gu", j, up))
                    piece(("go", j, tb, up), [(w_gout[j, up * 512:(up + 1) * 512, :].rearrange("(k p) n -> p k n", p=128), 4, 1024)], key=("go", j, up))

        def rot_post(off, nk, padw, r0):
            def post(slot, res):
                v = slot[:, off:off + nk * padw].rearrange("p (k n) -> p k n", k=nk)
                k.op("pool", [res], [res], lambda h: h.tensor_scalar(
                    out=v[:, :, r0 + 64:r0 + 96], in0=v[:, :, r0 + 32:r0 + 64], scalar1=-1.0, scalar2=None, op0=ALU.mult))
                k.op("pool", [res], [res], lambda h: h.tensor_copy(out=v[:, :, r0 + 96:r0 + 128], in_=v[:, :, r0:r0 + 32]))
            return post

        def mla_specs(j):
            piece(("ma1", j), [(w_mdown[j, :, 0:384].rearrange("(k p) n -> p k n", p=128), 8, 384)])
            piece(("ma2", j), [(w_mdown[j, :, 384:704].rearrange("(k p) n -> p k n", p=128), 8, 320, 384)],
                  post=rot_post(0, 8, 384, 256))
            for h_ in range(H):
                piece(("mh", j, h_), [
                    (w_muq[j, :, h_ * 192:(h_ + 1) * 192].rearrange("(k p) n -> p k n", p=128), 3, 192, 256),
                    (w_mukv[j, :, h_ * 256:(h_ + 1) * 256].rearrange("(k p) n -> p k n", p=128), 2, 256),
                    (w_mout[j, h_ * 128:(h_ + 1) * 128, :].rearrange("(k p) n -> p k n", p=128), 1, 1024),
                ], post=rot_post(0, 3, 256, 128))

        for s_ in range(nseq):
            for l in layers:
                if do_mix:
                    (mla_specs if l % 2 == 0 else gmlp_specs)(l // 2)
                if do_ffn:
                    ffn_specs(l)
                if do_ple:
                    ple_specs(l)

        wst = {"emitted": 0, "next": 0}
        slots = {}

        wscr = nc.dram_tensor("wscr", [max(len(keyid), 1), 128, 4096], BF16, kind="Internal").ap()
        NG = 6
        guard = [Res(f"scrg{g}") for g in range(NG)]
        for g_ in guard:
            g_.dsem = k.new_sem("d_" + g_.name)
            k.dcount[g_.dsem] = 0
        scr_res = {}
        wcnt = {"n": 0}

        def _emit_piece(i):
            tag, key, tr, total, post = specs[i]
            wi = rr["w"] % NSLOT
            rr["w"] += 1
            kid = keyid.get(key)
            if key in scr_res:
                k.dma([scr_res[key]], [wres[wi]], [(wsl[wi][:, 0:total], wscr[kid, :, 0:total])])
            else:
                si = rr["st"] % 2
                rr["st"] += 1
                k.dma([], [stres[si]], [
                    (stage[si][:, off:off + nk * pw_].rearrange("p (k n) -> p k n", k=nk)[:, :, 0:w_], ap_)
                    for off, nk, w_, pw_, ap_ in tr])
                if key[0] in ("ma1", "ma2", "mh"):
                    k.op("pool", [stres[si]], [wres[wi]], lambda h: h.tensor_copy(out=wsl[wi][:, 0:total], in_=stage[si][:, 0:total]))
                else:
                    k.op("act", [stres[si]], [wres[wi]], lambda h: h.activation(out=wsl[wi][:, 0:total], in_=stage[si][:, 0:total], func=AF.Copy))
                if post is not None:
                    post(wsl[wi], wres[wi])
                if key[0] in CACHE_KINDS and sum(1 for sp_ in specs if sp_[1] == key) > 1:
                    g_ = guard[wcnt["n"] % NG]
                    wcnt["n"] += 1
                    r_ = Res("scr")
                    r_.dsem = g_.dsem
                    k.dma([wres[wi]], [r_, g_], [(wscr[kid, :, 0:total], wsl[wi][:, 0:total])])
                    scr_res[key] = r_
            slots[i] = (wsl[wi], wres[wi])

        PREFETCH = 1

        def wnext(tag):
            i = wst["next"]
            wst["next"] += 1
            assert specs[i][0] == tag, (specs[i][0], tag)
            while wst["emitted"] <= min(i + PREFETCH, len(specs) - 1):
                _emit_piece(wst["emitted"])
                wst["emitted"] += 1
            return slots.pop(i)

        def stage_load(transfers_fn, total, dst_ap, dst_res):
            si = rr["st"] % 2
            rr["st"] += 1
            k.dma([], [stres[si]], transfers_fn(stage[si]))
            k.op("act", [stres[si]], [dst_res], lambda h: h.activation(out=dst_ap, in_=stage[si][:, 0:total], func=AF.Copy))

        def mm(out_ap, lhsT, rhs, start, stop):
            return nc.tensor.matmul(out_ap, lhsT=lhsT, rhs=rhs, start=start, stop=stop)

        def rms_block(srcs, sres_l, P, gcol, Dn, outs, out_res_l, gscale_ap=None):
            n = len(srcs)
            sb_, sbr = bank()
            for c in range(n):
                q_, qr = btmp()
                k.op("act", [sres_l[c]], [qr], lambda h, c=c, q_=q_: h.activation(out=q_[0:P, :], in_=srcs[c], func=AF.Square))
                k.op("pe", [qr, cres], [sbr], lambda h, c=c, q_=q_: mm(sb_[0:P, :], ones[0:P, 0:P], q_[0:P, :], c == 0, c == n - 1))
            rs, rsr = ftmp()
            k.op("act", [sbr], [rsr], lambda h: h.activation(out=rs[0:P, :], in_=sb_[0:P, :], func=AF.Ln, scale=1.0 / Dn, bias=EPS))
            k.op("act", [rsr], [rsr], lambda h: h.activation(out=rs[0:P, :], in_=rs[0:P, :], func=AF.Exp, scale=-0.5))
            for c in range(n):
                k.op("dve", [sres_l[c], rsr, cres], [out_res_l[c]], lambda h, c=c: h.scalar_tensor_tensor(
                    out=outs[c], in0=srcs[c], scalar=gcol(c), in1=rs[0:P, :], op0=ALU.mult, op1=ALU.mult))
            return rs, rsr

        def rms_multi(items, Dn):
            sq = []
            for (src, sr, gcol, out, outr) in items:
                q_, qr = btmp()
                k.op("act", [sr], [qr], lambda h, q_=q_, src=src: h.activation(out=q_[:], in_=src, func=AF.Square))
                sq.append((q_, qr))
            ss = []
            for i in range(len(items)):
                sb_, sbr = bank()
                k.op("pe", [sq[i][1], cres], [sbr], lambda h, sb_=sb_, i=i: mm(sb_[:], ones[:, :], sq[i][0][:], True, True))
                ss.append((sb_, sbr))
            rl = []
            for i in range(len(items)):
                rs, rsr = ftmp()
                k.op("act", [ss[i][1]], [rsr], lambda h, rs=rs, i=i: h.activation(out=rs[:], in_=ss[i][0][:], func=AF.Ln, scale=1.0 / Dn, bias=EPS))
                rl.append((rs, rsr))
            for rs, rsr in rl:
                k.op("act", [rsr], [rsr], lambda h, rs=rs: h.activation(out=rs[:], in_=rs[:], func=AF.Exp, scale=-0.5))
            for i, (src, sr, gcol, out, outr) in enumerate(items):
                k.op("dve", [sr, rl[i][1], cres], [outr], lambda h, i=i, src=src, gcol=gcol, out=out: h.scalar_tensor_tensor(
                    out=out, in0=src, scalar=gcol, in1=rl[i][0][:], op0=ALU.mult, op1=ALU.mult))

        def norm_h(tb, gbase, hn_view, hn_res):
            sl = slice(tb * T, (tb + 1) * T)
            rms_block([hT[:, c, sl] for c in range(KD)], [hres[c][tb] for c in range(KD)], 128,
                      lambda c: cp[:, gbase + c:gbase + c + 1], D,
                      [hn_view[:, c, :] for c in range(KD)], hn_res)

        def resid_add(c, tb, bk, bkr):
            sl = slice(tb * T, (tb + 1) * T)
            k.op("dve", [bkr, hres[c][tb]], [hres[c][tb]], lambda h: h.tensor_tensor(
                out=hT[:, c, sl], in0=bk[:], in1=hT[:, c, sl], op=ALU.add))

        def ffn_phase(l):
            k.barrier()
            hn = av(0, KD * SEQ).rearrange("p (c t) -> p c t", c=KD)
            hnr = [[Res(f"hn{b}_{c}") for c in range(KD)] for b in range(NB)]
            hid = [av(KD * SEQ + i * 2048, 2048).rearrange("p (j t) -> p j t", j=4) for i in range(2)]
            hidr = [[Res(f"hid{i}_{j}") for j in range(4)] for i in range(2)]
            for tb in range(NB):
                norm_h(tb, C_FFN + l * 8, hn[:, :, tb * T:(tb + 1) * T], hnr[tb])
            it = 0
            for fg in range(8):
                wu, wur = wnext(("fup", l, fg))
                wd, wdr = wnext(("fdn", l, fg))
                wuv = wu[:, 0:4096].rearrange("p (k n) -> p k n", k=8)
                wdv = wd[:, 0:4096].rearrange("p (k n) -> p k n", k=4)

                def up(tb, hi):
                    for j in range(4):
                        bk, bkr = bank()
                        k.op("pe", [wur] + hnr[tb], [bkr], lambda h, j=j, bk=bk: [
                            mm(bk[:], wuv[:, kk, j * 128:(j + 1) * 128], hn[:, kk, tb * T:(tb + 1) * T], kk == 0, kk == KD - 1)
                            for kk in range(KD)][-1])
                        sq, sqr = ftmp()
                        k.op("act", [bkr], [sqr], lambda h, bk=bk, sq=sq: h.activation(out=sq[:], in_=bk[:], func=AF.Square))
                        k.op("dve", [bkr, sqr], [hidr[hi][j]], lambda h, j=j, bk=bk, sq=sq: h.scalar_tensor_tensor(
                            out=hid[hi][:, j, :], in0=bk[:], scalar=0.0, in1=sq[:], op0=ALU.is_gt, op1=ALU.mult))

                def down(tb, hi):
                    for c in range(KD):
                        bk, bkr = bank()
                        k.op("pe", [wdr] + hidr[hi], [bkr], lambda h, c=c, bk=bk: [
                            mm(bk[:], wdv[:, j, c * 128:(c + 1) * 128], hid[hi][:, j, :], j == 0, j == 3)
                            for j in range(4)][-1])
                        resid_add(c, tb, bk, bkr)

                up(0, it % 2)
                for tb in range(NB):
                    if tb + 1 < NB:
                        up(tb + 1, (it + 1) % 2)
                    down(tb, it % 2)
                    it += 1

        def ple_phase(l, s_):
            k.barrier()
            hn = av(0, KD * SEQ).rearrange("p (c t) -> p c t", c=KD)
            hnr = [[Res(f"hn{b}_{c}") for c in range(KD)] for b in range(NB)]
            pb_ = av(KD * SEQ, 2 * SEQ)
            pbv = pb_.rearrange("p (c t) -> p c t", c=2)
            pbr = Res("pb")
            stage_load(lambda st: [(st[:, 0:2 * SEQ].rearrange("p (c t) -> p c t", c=2),
                                    pT[l * PLE:(l + 1) * PLE, s_ * SEQ:(s_ + 1) * SEQ].rearrange("(c p) t -> p c t", p=128))],
                       2 * SEQ, pb_, pbr)
            for tb in range(NB):
                norm_h(tb, C_PLE + l * 8, hn[:, :, tb * T:(tb + 1) * T], hnr[tb])
            for hf in range(2):
                wg, wgr = wnext(("pg", l, hf))
                wp, wpr = wnext(("pp", l, hf))
                wgv = wg[:, 0:4096].rearrange("p (k n) -> p k n", k=8)
                wpv = wp[:, 0:1024].rearrange("p (k n) -> p k n", k=2)
                for tb in range(NB):
                    sl = slice(tb * T, (tb + 1) * T)
                    for c4 in range(4):
                        c = hf * 4 + c4
                        gb, gbr = bank()
                        k.op("pe", [wgr] + hnr[tb], [gbr], lambda h, gb=gb, c4=c4: [
                            mm(gb[:], wgv[:, kk, c4 * 128:(c4 + 1) * 128], hn[:, kk, sl], kk == 0, kk == KD - 1)
                            for kk in range(KD)][-1])
                        qb_, qbr = bank()
                        k.op("pe", [wpr, pbr], [qbr], lambda h, qb_=qb_, c4=c4: [
                            mm(qb_[:], wpv[:, kk, c4 * 128:(c4 + 1) * 128], pbv[:, kk, sl], kk == 0, kk == 1)
                            for kk in range(2)][-1])
                        gs, gsr = ftmp()
                        k.op("act", [gbr], [gsr], lambda h, gb=gb, gs=gs: h.activation(out=gs[:], in_=gb[:], func=AF.Sigmoid))
                        k.op("dve", [gsr, qbr], [gsr], lambda h, qb_=qb_, gs=gs: h.tensor_tensor(
                            out=gs[:], in0=qb_[:], in1=gs[:], op=ALU.mult))
                        k.op("pool", [gsr, hres[c][tb]], [hres[c][tb]], lambda h, gs=gs, c=c: h.tensor_tensor(
                            out=hT[:, c, sl], in0=hT[:, c, sl], in1=gs[:], op=ALU.add))

        def gmlp_phase(l, s_):
            j = l // 2
            k.barrier()
            hnb = [av(0, KD * T).rearrange("p (c t) -> p c t", c=KD), av(25600, KD * T).rearrange("p (c t) -> p c t", c=KD)]
            hnrb = [[Res(f"hnb{c}") for c in range(KD)], [Res(f"hnc{c}") for c in range(KD)]]
            vt = av(4096, 4 * GH).rearrange("p (c f) -> p c f", c=4)
            vtr = [[Res(f"vt{c}_{v}") for v in range(4)] for c in range(4)]
            cst = av(12288, 2048, F32).rearrange("p (c t) -> p c t", c=16)
            cstr = Res("cst")
            y = [av(16384 + i * 2048, 2048).rearrange("p (j t) -> p j t", j=4) for i in range(2)]
            yr = [[Res(f"y{i}_{f}") for f in range(4)] for i in range(2)]
            wsf = av(20480, 1024, F32)
            wsfr = Res("wsf")
            bsb = av(22528, 1024, F32)
            bsbr = Res("bsb")
            wsb = av(24576, 1024).rearrange("p (g t) -> p g t", g=8)
            wsbr = Res("wsb")
            k.dma([], [wsfr], [(wsf, wsT_d[j, :, :])])
            k.dma([], [bsbr], [(bsb, bs_d[j:j + 1, :].broadcast_to([128, 1024]))])
            k.op("pool", [wsfr], [wsbr], lambda h: h.affine_select(
                out=wsb, in_=wsf.rearrange("p (g t) -> p g t", g=8), pattern=[[0, 8], [1, 128]],
                compare_op=ALU.is_ge, fill=0.0, base=0, channel_multiplier=-1))
            rb = []
            for hf in range(2):
                b_, br_ = bank()
                k.op("pe", [wsbr, cres], [br_], lambda h, b_=b_, hf=hf: mm(
                    b_[:], ones[:, :], wsb[:, hf * 4:(hf + 1) * 4, :], True, True))
                rb.append((b_, br_))
            for fc in range(16):
                g = fc // 2
                b_, br_ = rb[g // 4]
                k.op("dve", [br_, bsbr, cres], [cstr], lambda h, fc=fc, g=g, b_=b_: h.scalar_tensor_tensor(
                    out=cst[:, fc, :], in0=b_[:, (g % 4) * 128:(g % 4 + 1) * 128],
                    scalar=cp[:, C_LNB + j * 16 + fc:C_LNB + j * 16 + fc + 1],
                    in1=bsb[:, g * 128:(g + 1) * 128], op0=ALU.mult, op1=ALU.add))
            yi = 0
            norm_h(0, C_MIX + l * 8, hnb[0], hnrb[0])
            for tb in range(NB):
                hn, hnr = hnb[tb % 2], hnrb[tb % 2]
                for vp in range(4):
                    wv, wvr = wnext(("gv", j, tb, vp))
                    wvv = wv[:, 0:4096].rearrange("p (k n) -> p k n", k=8)
                    for tc in range(4):
                        bk, bkr = bank()
                        k.op("pe", [wvr] + hnr, [bkr], lambda h, bk=bk, tc=tc: [
                            mm(bk[:], hn[:, kk, tc * 128:(tc + 1) * 128], wvv[:, kk, :], kk == 0, kk == KD - 1)
                            for kk in range(KD)][-1])
                        col = tc * 4 + vp
                        k.op("act", [bkr], [vtr[tc][vp], sres], lambda h, bk=bk, tc=tc, vp=vp, col=col: h.activation(
                            out=vt[:, tc, vp * 512:(vp + 1) * 512], in_=bk[:], func=AF.Gelu_apprx_tanh,
                            accum_out=st1[:, col:col + 1]))
                        jk, jkr = btmp()
                        k.op("act", [vtr[tc][vp]], [jkr, sres], lambda h, jk=jk, tc=tc, vp=vp, col=col: h.activation(
                            out=jk[:], in_=vt[:, tc, vp * 512:(vp + 1) * 512], func=AF.Square,
                            accum_out=st2[:, col:col + 1]))
                k.op("dve", [sres], [sres], lambda h: h.tensor_reduce(
                    out=stt[:, 0:4], in_=st1[:, 0:16].rearrange("p (a b) -> p a b", a=4), axis=mybir.AxisListType.X, op=ALU.add))
                k.op("dve", [sres], [sres], lambda h: h.tensor_reduce(
                    out=stt[:, 4:8], in_=st2[:, 0:16].rearrange("p (a b) -> p a b", a=4), axis=mybir.AxisListType.X, op=ALU.add))
                k.op("dve", [sres], [sres], lambda h: h.tensor_scalar(
                    out=stt[:, 8:12], in0=stt[:, 0:4], scalar1=1.0 / GH, scalar2=None, op0=ALU.mult))
                k.op("dve", [sres], [sres], lambda h: h.tensor_tensor(
                    out=stt[:, 12:16], in0=stt[:, 8:12], in1=stt[:, 8:12], op=ALU.mult))
                k.op("dve", [sres], [sres], lambda h: h.scalar_tensor_tensor(
                    out=stt[:, 16:20], in0=stt[:, 4:8], scalar=1.0 / GH, in1=stt[:, 12:16], op0=ALU.mult, op1=ALU.subtract))
                k.op("act", [sres], [sres], lambda h: h.activation(
                    out=stt[:, 20:24], in_=stt[:, 16:20], func=AF.Sqrt, scale=1.0, bias=EPS))
                k.op("dve", [sres], [sres], lambda h: h.reciprocal(out=stt[:, 24:28], in_=stt[:, 20:24]))
                for tc in range(4):
                    k.op("dve", [sres] + vtr[tc], vtr[tc], lambda h, tc=tc: h.tensor_scalar(
                        out=vt[:, tc, :], in0=vt[:, tc, :], scalar1=stt[:, 8 + tc:9 + tc], scalar2=stt[:, 24 + tc:25 + tc],
                        op0=ALU.subtract, op1=ALU.mult))
                for up in range(4):
                    wu, wur = wnext(("gu", j, tb, up))
                    wo, wor = wnext(("go", j, tb, up))
                    wuv = wu[:, 0:4096].rearrange("p (k n) -> p k n", k=8)
                    wov = wo[:, 0:4096].rearrange("p (k n) -> p k n", k=4)
                    yy, yyr = y[yi % 2], yr[yi % 2]
                    yi += 1
                    usl = []
                    for fl in range(4):
                        ub, ubr = bank()
                        k.op("pe", [wur] + hnr, [ubr], lambda h, ub=ub, fl=fl: [
                            mm(ub[:], wuv[:, kk, fl * 128:(fl + 1) * 128], hn[:, kk, :], kk == 0, kk == KD - 1)
                            for kk in range(KD)][-1])
                        us, usr = ftmp()
                        k.op("act", [ubr], [usr], lambda h, ub=ub, us=us: h.activation(out=us[:], in_=ub[:], func=AF.Gelu_apprx_tanh))
                        usl.append((us, usr))
                    for fl in range(4):
                        fc = up * 4 + fl
                        g = fc // 2
                        us, usr = usl[fl]
                        sb_, sbr = bank()
                        k.op("pe", [vtr[tc][fc // 4] for tc in range(4)] + [wsbr], [sbr], lambda h, sb_=sb_, fc=fc, g=g: [
                            mm(sb_[:, tc * 128:(tc + 1) * 128], vt[:, tc, fc * 128:(fc + 1) * 128], wsb[:, g, :], True, True)
                            for tc in range(4)][-1])
                        sv, svr = ftmp()
                        k.op("dve", [sbr, cstr, cres], [svr], lambda h, sb_=sb_, sv=sv, fc=fc: h.scalar_tensor_tensor(
                            out=sv[:].rearrange("p (a t) -> p a t", a=4), in0=sb_[:].rearrange("p (a t) -> p a t", a=4),
                            scalar=cp[:, C_LNG + j * 16 + fc:C_LNG + j * 16 + fc + 1],
                            in1=cst[:, fc:fc + 1, :].broadcast_to([128, 4, 128]), op0=ALU.mult, op1=ALU.add))
                        k.op("pool", [usr, svr], [yyr[fl]], lambda h, us=us, sv=sv, yy=yy, fl=fl: h.tensor_tensor(
                            out=yy[:, fl, :], in0=us[:], in1=sv[:], op=ALU.mult))
                    for c in range(KD):
                        bk, bkr = bank()
                        k.op("pe", [wor] + yyr, [bkr], lambda h, bk=bk, c=c, yy=yy: [
                            mm(bk[:], wov[:, fl, c * 128:(c + 1) * 128], yy[:, fl, :], fl == 0, fl == 3)
                            for fl in range(4)][-1])
                        resid_add(c, tb, bk, bkr)
                    if up == 0 and tb + 1 < NB:
                        norm_h(tb + 1, C_MIX + l * 8, hnb[(tb + 1) % 2], hnrb[(tb + 1) % 2])

        def mla_phase(l, s_):
            j = l // 2
            k.barrier()
            hn = av(0, KD * T).rearrange("p (c t) -> p c t", c=KD)
            hnr = [Res(f"hnb{c}") for c in range(KD)]
            cq = av(4096, 3 * SEQ).rearrange("p (c t) -> p c t", c=3)
            cqr = [Res(f"cq{b}") for b in range(NB)]
            ckv = av(10240, 2 * SEQ).rearrange("p (c t) -> p c t", c=2)
            ckvr = [Res(f"ckv{b}") for b in range(NB)]
            kr = av(14336, SEQ)
            krr = [Res(f"kr{b}") for b in range(NB)]
            qn = av(16384, SEQ)
            qnr = [Res(f"qn{b}") for b in range(NB)]
            qr_ = av(18432, SEQ)
            qrr = [Res(f"qr{b}") for b in range(NB)]
            kn = av(20480, SEQ)
            knr = [Res(f"kn{b}") for b in range(NB)]
            vtk = av(22528, SEQ).rearrange("p (c d) -> p c d", c=16)
            vtkr = [Res(f"vk{b}") for b in range(NB)]
            oth = av(24576, SEQ)
            othr = [Res(f"ot{b}") for b in range(NB)]
            raw = av(26624, 3 * T, F32).rearrange("p (c t) -> p c t", c=3)
            rawr = [Res(f"raw{c}") for c in range(3)]
            pi_ = av(29696, T, F32)
            pi_i = pi_.bitcast(I32)
            pir = Res("pi")
            rt = [av(30720 + i * 1024, T, F32) for i in range(2)]
            rtr = [Res(f"rt{i}") for i in range(2)]
            cst_ = av(32768, SEQ, F32)
            cstr = [Res(f"cst{b}") for b in range(NB)]
            TWO_PI = 2.0 * math.pi
            C1 = 6.28125
            C2 = TWO_PI - C1

            def rope_tables(tb):
                sl = slice(tb * T, (tb + 1) * T)
                t0 = s_ * SEQ + tb * T
                k.dma([], [pir], [(pi_i, pos[0:1, t0:t0 + T].broadcast_to([128, T]))])
                a, ar = rt[0], rtr[0]
                b_, br_ = rt[1], rtr[1]
                k.op("dve", [pir], [ar], lambda h: h.tensor_copy(out=a, in_=pi_i))
                k.op("dve", [ar, cres], [ar], lambda h: h.tensor_scalar(
                    out=a, in0=a, scalar1=cp[:, C_INVF:C_INVF + 1], scalar2=None, op0=ALU.mult))
                k.op("dve", [ar, pir], [pir], lambda h: h.tensor_scalar(
                    out=pi_i, in0=a, scalar1=1.0 / TWO_PI, scalar2=None, op0=ALU.mult))
                k.op("dve", [pir], [br_], lambda h: h.tensor_copy(out=b_, in_=pi_i))
                k.op("dve", [ar, br_], [ar], lambda h: h.scalar_tensor_tensor(
                    out=a, in0=b_, scalar=-C1, in1=a, op0=ALU.mult, op1=ALU.add))
                k.op("dve", [ar, br_], [ar], lambda h: h.scalar_tensor_tensor(
                    out=a, in0=b_, scalar=-C2, in1=a, op0=ALU.mult, op1=ALU.add))
                k.op("dve", [ar], [ar], lambda h: h.tensor_scalar(
                    out=a, in0=a, scalar1=math.pi, scalar2=-math.pi, op0=ALU.min, op1=ALU.max))
                k.op("act", [ar], [cstr[tb]], lambda h: h.activation(out=cst_[64:128, sl], in_=a[64:128, :], func=AF.Sin))
                k.op("act", [ar], [br_], lambda h: h.activation(out=b_[0:64, :], in_=a[0:64, :], func=AF.Sin, scale=0.5))
                k.op("dve", [br_], [br_], lambda h: h.tensor_tensor(out=b_[0:64, :], in0=b_[0:64, :], in1=b_[0:64, :], op=ALU.mult))
                k.op("dve", [br_], [cstr[tb]], lambda h: h.tensor_scalar(
                    out=cst_[0:64, sl], in0=b_[0:64, :], scalar1=-2.0, scalar2=1.0, op0=ALU.mult, op1=ALU.add))

            def rope_norm(xx, xxr, ggcol, tb, out_ap, out_res):
                sl = slice(tb * T, (tb + 1) * T)
                sb_, sbr = bank()
                q_, qr2 = btmp()
                k.op("act", [xxr], [qr2], lambda h: h.activation(out=q_[0:64, :], in_=xx[0:64, :], func=AF.Square))
                k.op("pe", [qr2, cres], [sbr], lambda h: mm(sb_[0:64, :], ones[0:64, 0:64], q_[0:64, :], True, True))
                rs, rsr = ftmp()
                k.op("act", [sbr], [rsr], lambda h: h.activation(out=rs[0:64, :], in_=sb_[0:64, :], func=AF.Sqrt, scale=1.0 / RD, bias=EPS))
                k.op("dve", [rsr], [rsr], lambda h: h.reciprocal(out=rs[0:64, :], in_=rs[0:64, :]))
                t1, t1r = ftmp()
                k.op("dve", [xxr, cstr[tb], cres], [t1r], lambda h: h.scalar_tensor_tensor(
                    out=t1[:], in0=xx[:], scalar=ggcol, in1=cst_[:, sl], op0=ALU.mult, op1=ALU.mult))
                fb, fbr = bank()
                k.op("pe", [t1r, cres], [fbr], lambda h: mm(fb[0:64, :], fold[:, :], t1[:], True, True))
                k.op("dve", [fbr, rsr], [out_res], lambda h: h.tensor_tensor(out=out_ap, in0=fb[0:64, :], in1=rs[0:64, :], op=ALU.mult))

            w1, w1r = wnext(("ma1", j))
            w2, w2r = wnext(("ma2", j))
            w1v = w1[:, 0:3072].rearrange("p (k n) -> p k n", k=8)
            w2v = w2[:, 0:3072].rearrange("p (k n) -> p k n", k=8)
            for tb in range(NB):
                sl = slice(tb * T, (tb + 1) * T)
                rope_tables(tb)
                norm_h(tb, C_MIX + l * 8, hn, hnr)
                for (wv_, wr_, nchunk, dst, dres, gb, Dn) in ((w1v, w1r, 3, cq, cqr, C_QL + j * 3, QL), (w2v, w2r, 2, ckv, ckvr, C_KVL + j * 2, KVL)):
                    for m in range(nchunk):
                        bk, bkr = bank()
                        k.op("pe", [wr_] + hnr, [bkr], lambda h, bk=bk, m=m, wv_=wv_: [
                            mm(bk[:], wv_[:, kk, m * 128:(m + 1) * 128], hn[:, kk, :], kk == 0, kk == KD - 1)
                            for kk in range(KD)][-1])
                        k.op("act", [bkr], [rawr[m]], lambda h, bk=bk, m=m: h.activation(out=raw[:, m, :], in_=bk[:], func=AF.Copy))
                    rms_block([raw[:, m, :] for m in range(nchunk)], rawr[:nchunk], 128,
                              lambda c, gb=gb: cp[:, gb + c:gb + c + 1], Dn,
                              [dst[:, m, sl] for m in range(nchunk)], [dres[tb]] * nchunk)
                xx, xxr = bank()
                k.op("pe", [w2r] + hnr, [xxr], lambda h, xx=xx: [
                    mm(xx[:], w2v[:, kk, 256:384], hn[:, kk, :], kk == 0, kk == KD - 1) for kk in range(KD)][-1])
                rope_norm(xx, xxr, cp[:, C_KR + j:C_KR + j + 1], tb, kr[0:64, sl], krr[tb])

            kn2 = [kn, av(36864, SEQ)]
            vtk2 = [vtk, av(38912, SEQ).rearrange("p (c d) -> p c d", c=16)]
            knr2 = [knr, [Res(f"knb{b}") for b in range(NB)]]
            vtkr2 = [vtkr, [Res(f"vkb{b}") for b in range(NB)]]
            bankset[0] = [0, 1, 2, 3]
            hw = {}

            def views(h_):
                wh, whr = hw[h_]
                return (wh[:, 0:768].rearrange("p (k n) -> p k n", k=3),
                        wh[:, 768:1280].rearrange("p (k n) -> p k n", k=2),
                        wh[:, 1280:2304], whr)

            def proj_qn(h_, tbs):
                wq, wkv, wo, whr = views(h_)
                items = []
                for tb in tbs:
                    sl = slice(tb * T, (tb + 1) * T)
                    bk, bkr = bank()
                    k.op("pe", [whr, cqr[tb]], [bkr], lambda h, bk=bk, sl=sl: [
                        mm(bk[:], wq[:, kk, 0:128], cq[:, kk, sl], kk == 0, kk == 2) for kk in range(3)][-1])
                    items.append((bk[:], bkr, cpd[:, j:j + 1], qn[:, sl], qnr[tb]))
                rms_multi(items, 128)

            def proj_qr(h_, tb):
                wq, wkv, wo, whr = views(h_)
                sl = slice(tb * T, (tb + 1) * T)
                xx, xxr = bank()
                k.op("pe", [whr, cqr[tb]], [xxr], lambda h: [
                    mm(xx[:], wq[:, kk, 128:256], cq[:, kk, sl], kk == 0, kk == 2) for kk in range(3)][-1])
                rope_norm(xx, xxr, cpd[:, 2 + j:3 + j], tb, qr_[0:64, sl], qrr[tb])

            def proj_q(h_, tbs):
                for i in range(0, len(tbs), 2):
                    proj_qn(h_, tbs[i:i + 2])
                for tb in tbs:
                    proj_qr(h_, tb)

            def proj_kv(h_, tbs):
                wq, wkv, wo, whr = views(h_)
                b_ = h_ % 2
                for i in range(0, len(tbs), 2):
                    items = []
                    for tb in tbs[i:i + 2]:
                        sl = slice(tb * T, (tb + 1) * T)
                        bk, bkr = bank()
                        k.op("pe", [whr, ckvr[tb]], [bkr], lambda h, bk=bk, sl=sl: [
                            mm(bk[:], wkv[:, kk, 0:128], ckv[:, kk, sl], kk == 0, kk == 1) for kk in range(2)][-1])
                        items.append((bk[:], bkr, cp[:, C_KN + j:C_KN + j + 1], kn2[b_][:, sl], knr2[b_][tb]))
                    rms_multi(items, 128)
                for tb in tbs:
                    bv, bvr = bank()
                    k.op("pe", [whr, ckvr[tb]], [bvr], lambda h, bv=bv, tb=tb: [
                        mm(bv[:, kt4 * 128:(kt4 + 1) * 128], ckv[:, kk, (tb * 4 + kt4) * 128:(tb * 4 + kt4 + 1) * 128],
                           wkv[:, kk, 128:256], kk == 0, kk == 1)
                        for kt4 in range(4) for kk in range(2)][-1])
                    k.op("act", [bvr], [vtkr2[b_][tb]], lambda h, bv=bv, tb=tb: h.activation(
                        out=vtk2[b_][:, tb * 4:(tb + 1) * 4, :], in_=bv[:].rearrange("p (a d) -> p a d", a=4), func=AF.Copy))

            def att(h_, qb):
                b_ = h_ % 2
                kn_, vtk_, knr_, vtkr_ = kn2[b_], vtk2[b_], knr2[b_], vtkr2[b_]
                ob, obr = pbank[4 + qb % 2], pres[4 + qb % 2]
                db, dbr = pbank[6 + qb % 2], pres[6 + qb % 2]
                nkt = 4 * qb + 4

                def s_exp(kt):
                    jd = kt - 4 * qb
                    c0 = max(jd, 0) * 128
                    qs = slice(qb * T + c0, (qb + 1) * T)
                    ks = slice(kt * 128, (kt + 1) * 128)
                    sbk, sbkr = bank()
                    k.op("pe", [knr_[kt // 4], qnr[qb], krr[kt // 4], qrr[qb]], [sbkr], lambda h: [
                        mm(sbk[:, c0:T], kn_[:, ks], qn[:, qs], True, False),
                        mm(sbk[:, c0:T], kr[0:64, ks], qr_[0:64, qs], False, True)][-1])
                    pt, ptr = btmp()
                    k.op("act", [sbkr, cres], [ptr], lambda h: h.activation(
                        out=pt[:, c0:T], in_=sbk[:, c0:T], func=AF.Exp, bias=cpd[:, 8:9]))
                    if jd >= 0:
                        k.op("pool", [ptr], [ptr], lambda h: h.affine_select(
                            out=pt[:, c0:c0 + 128], in_=pt[:, c0:c0 + 128], pattern=[[1, 128]],
                            compare_op=ALU.is_ge, fill=0.0, base=0, channel_multiplier=-1))
                    return pt, ptr, c0

                pend = s_exp(0)
                for kt in range(nkt):
                    nxt = s_exp(kt + 1) if kt + 1 < nkt else None
                    pt, ptr, c0 = pend
                    k.op("pe", [ptr, vtkr_[kt // 4]], [obr], lambda h, pt=pt, c0=c0, kt=kt: mm(
                        ob[:, c0:T], vtk_[:, kt, :], pt[:, c0:T], kt == 0, kt == nkt - 1))
                    k.op("pe", [ptr, cres], [dbr], lambda h, pt=pt, c0=c0, kt=kt: mm(
                        db[:, c0:T], ones[:, :], pt[:, c0:T], kt == 0, kt == nkt - 1))
                    pend = nxt
                rd, rdr = ftmp()
                k.op("dve", [dbr], [rdr], lambda h: h.reciprocal(out=rd[:], in_=db[:]))
                k.op("dve", [obr, rdr], [othr[qb]], lambda h: h.tensor_tensor(
                    out=oth[:, qb * T:(qb + 1) * T], in0=ob[:], in1=rd[:], op=ALU.mult))

            def outproj(h_):
                wq, wkv, wo, whr = views(h_)
                for tb in range(NB):
                    for c in range(KD):
                        bk, bkr = bank()
                        k.op("pe", [whr, othr[tb]], [bkr], lambda h, bk=bk, c=c, tb=tb: mm(
                            bk[:], wo[:, c * 128:(c + 1) * 128], oth[:, tb * T:(tb + 1) * T], True, True))
                        resid_add(c, tb, bk, bkr)

            hw[0] = wnext(("mh", j, 0))
            proj_kv(0, [0, 1, 2, 3])
            proj_q(0, [0, 1, 2, 3])
            for h_ in range(H):
                nx = h_ + 1 if h_ + 1 < H else None
                if nx is not None:
                    hw[nx] = wnext(("mh", j, nx))
                att(h_, 0)
                if nx is not None:
                    proj_kv(nx, [0, 1])
                    proj_qr(nx, 0)
                att(h_, 1)
                if nx is not None:
                    proj_qn(nx, [0, 1])
                    proj_qr(nx, 1)
                    proj_kv(nx, [2, 3])
                att(h_, 2)
                if nx is not None:
                    proj_qr(nx, 2)
                att(h_, 3)
                if nx is not None:
                    proj_qn(nx, [2, 3])
                    proj_qr(nx, 3)
                outproj(h_)
            bankset[0] = list(range(8))

        outres = Res("out")
        for s_ in range(nseq):
            k.dma([], [hres[c][b] for c in range(KD) for b in range(NB)],
                  [(hT[:, c, :], xT[c * 128:(c + 1) * 128, s_ * SEQ:(s_ + 1) * SEQ]) for c in range(KD)])
            for l in layers:
                if do_mix:
                    if l % 2 == 0:
                        mla_phase(l, s_)
                    else:
                        gmlp_phase(l, s_)
                if do_ffn:
                    ffn_phase(l)
                if do_ple:
                    ple_phase(l, s_)
            k.dma([hres[c][b] for c in range(KD) for b in range(NB)], [outres],
                  [(outT[c * 128:(c + 1) * 128, s_ * SEQ:(s_ + 1) * SEQ], hT[:, c, :]) for c in range(KD)])
        assert wst["next"] == len(specs)
        if debug:
            k.barrier()
            dres = Res("dbg")
            ad = nc.dram_tensor("arena_dump", [128, ARENA], BF16, kind="ExternalOutput").ap()
            k.dma([], [dres], [(ad[:, :], arena[:, :])])
            k.eng["sp"].h.wait_ge(k.sems[dres.dsem], dres.dcnt)
        sp = k.eng["sp"]
        sp.h.wait_ge(k.sems[outres.dsem], outres.dcnt)
        print(f"[build] layers={layers} nseq={nseq} ops: " + " ".join(f"{n}={e.cnt}" for n, e in k.eng.items())
              + f" sems={len(k.sems)} n_inst~{k.n_inst}")
    return nc


def _pack_consts(inp):
    cp = np.zeros((128, NCP), np.float32)

    def chunks(v, n):
        return np.ascontiguousarray(v.reshape(n, 128).T)

    for l in range(DEPTH):
        cp[:, C_MIX + l * 8:C_MIX + l * 8 + 8] = chunks(inp["norm_mix"][l], 8)
        cp[:, C_FFN + l * 8:C_FFN + l * 8 + 8] = chunks(inp["norm_ffn"][l], 8)
        cp[:, C_PLE + l * 8:C_PLE + l * 8 + 8] = chunks(inp["norm_ple"][l], 8)
    perm = (np.arange(64) + 32) % 64
    for j in range(2):
        cp[:, C_QL + j * 3:C_QL + j * 3 + 3] = chunks(inp["mla_q_lora_g"][j], 3)
        cp[:, C_KVL + j * 2:C_KVL + j * 2 + 2] = chunks(inp["mla_kv_lora_g"][j], 2)
        cp[:, C_QN + j] = inp["mla_q_nope_g"][j]
        cp[:, C_KN + j] = inp["mla_k_nope_g"][j]
        cp[:64, C_QR + j] = inp["mla_q_rope_g"][j]
        cp[64:, C_QR + j] = inp["mla_q_rope_g"][j][perm]
        cp[:64, C_KR + j] = inp["mla_k_rope_g"][j]
        cp[64:, C_KR + j] = inp["mla_k_rope_g"][j][perm]
        cp[:, C_LNG + j * 16:C_LNG + j * 16 + 16] = chunks(inp["gmlp_ln_g"][j], 16)
        cp[:, C_LNB + j * 16:C_LNB + j * 16 + 16] = chunks(inp["gmlp_ln_b"][j], 16)
    invf = (10000.0 ** (-(np.arange(0, 64, 2, dtype=np.float32) / np.float32(64)))).astype(np.float32)
    cp[:, C_INVF] = np.concatenate([invf, invf, invf, invf])
    return cp


_PROG_CACHE = {}


def _get_prog(layers):
    key = tuple(layers)
    if key not in _PROG_CACHE:
        _PROG_CACHE[key] = build_program(list(layers))
    return _PROG_CACHE[key]


def kernel(**inp):
    inp = {k_: np.asarray(v) for k_, v in inp.items()}
    x = inp["x"].astype(np.float32, copy=False)
    p = inp["p"].astype(np.float32, copy=False)
    positions = inp["positions"].astype(np.int32, copy=False)
    cp = _pack_consts(inp)
    shared = {
        "cp": cp,
        "b_s": np.ascontiguousarray(inp["gmlp_b_s"].reshape(2, 1024)),
        "w_sT": np.ascontiguousarray(inp["gmlp_w_s"].transpose(0, 3, 1, 2).reshape(2, 128, 1024)),
    }
    for n in ("mla_w_down", "mla_w_uq", "mla_w_ukv", "mla_w_out", "gmlp_w_in", "gmlp_w_out",
              "ffn_w_up", "ffn_w_down", "ple_w_gate", "ple_w_proj"):
        shared[n] = np.ascontiguousarray(inp[n], dtype=np.float32)
    per_core = []
    for c in range(N_CORES):
        b0 = c * NSEQ
        xs = x[b0:b0 + NSEQ].reshape(NSEQ * SEQ, D)
        ps = p[:, b0:b0 + NSEQ].reshape(DEPTH, NSEQ * SEQ, PLE)
        per_core.append({
            "xT": np.ascontiguousarray(xs.T),
            "pT": np.ascontiguousarray(ps.transpose(0, 2, 1).reshape(DEPTH * PLE, NSEQ * SEQ)),
            "pos": np.ascontiguousarray(positions[b0:b0 + NSEQ].reshape(1, NSEQ * SEQ)),
        })
    launches = [list(range(DEPTH))] if FUSED else [[l] for l in range(DEPTH)]
    for layers in launches:
        nc = _get_prog(layers)
        in_maps = [dict(shared, **pc) for pc in per_core]
        res = run_bass_kernel_spmd(nc, in_maps, core_ids=list(range(N_CORES)))
        for c in range(N_CORES):
            per_core[c]["xT"] = np.ascontiguousarray(res.results[c]["outT"])
    out = np.empty((BATCH, SEQ, D), np.float32)
    for c in range(N_CORES):
        out[c * NSEQ:(c + 1) * NSEQ] = per_core[c]["xT"].T.reshape(NSEQ, SEQ, D)
    return out
```

```python
import contextlib
import math

import numpy as np
import concourse.bass as bass
import concourse.mybir as mybir
from concourse.bass_utils import run_bass_kernel_spmd

F32 = mybir.dt.float32
BF16 = mybir.dt.bfloat16
I32 = mybir.dt.int32
AF = mybir.ActivationFunctionType
ALU = mybir.AluOpType

N_CORES = 8
D = 1024
SEQ = 2048
BATCH = 16
DEPTH = 4
T = 512
NB = SEQ // T
KD = D // 128
NSEQ = BATCH // N_CORES
H = 8
QL, KVL, RD = 384, 256, 64
DFF = 4096
GH = 2048
PLE = 256
EPS = 1e-6
SM_SCALE = (128 + 64) ** -0.5
SM_SHIFT = 16.0

SAME_ENGINE_SYNC = True
CACHE_KINDS = ("gv", "gu", "go")
FUSED = True

C_MIX, C_FFN, C_PLE = 0, 32, 64
C_QL, C_KVL = 96, 102
C_QN, C_KN = 106, 108
C_QR, C_KR = 110, 114
C_LNG, C_LNB = 118, 150
C_INVF = 182
NCP = 184


class Res:
    __slots__ = ("name", "w", "r", "dsem", "dcnt")

    def __init__(self, name):
        self.name = name
        self.w = None
        self.r = {}
        self.dsem = None
        self.dcnt = 0


class Eng:
    def __init__(self, name, h, sem):
        self.name, self.h, self.sem = name, h, sem
        self.cnt = 0
        self.waited = {}


class K:
    def __init__(self, nc, es):
        self.nc, self.es = nc, es
        self.sems = []
        self.eng = {}
        for n, h in (("pe", nc.tensor), ("act", nc.scalar), ("dve", nc.vector),
                     ("pool", nc.gpsimd), ("sp", nc.sync)):
            self.eng[n] = Eng(n, h, self.new_sem("e_" + n))
        self.dsem_by_name = {}
        self.dcount = {}
        self.n_inst = 0

    def new_sem(self, name):
        self.sems.append(self.es.enter_context(self.nc.semaphore(name)))
        return len(self.sems) - 1

    def sb(self, name, shape, dt):
        return self.es.enter_context(self.nc.sbuf_tensor("sb_" + name, shape, dt))

    @staticmethod
    def _deps(reads, writes):
        deps = {}
        for r in reads:
            if r.w is not None and deps.get(r.w[0], 0) < r.w[1]:
                deps[r.w[0]] = r.w[1]
        for w in writes:
            if w.w is not None and deps.get(w.w[0], 0) < w.w[1]:
                deps[w.w[0]] = w.w[1]
            for s, v in w.r.items():
                if deps.get(s, 0) < v:
                    deps[s] = v
        return deps

    def _wait(self, E, deps):
        for s, v in deps.items():
            if s == E.sem and (E.name == "pe" or not SAME_ENGINE_SYNC):
                continue
            if E.waited.get(s, 0) >= v:
                continue
            E.h.wait_ge(self.sems[s], v)
            E.waited[s] = v
            self.n_inst += 1

    def op(self, en, reads, writes, fn):
        E = self.eng[en]
        self._wait(E, self._deps(reads, writes))
        last = fn(E.h)
        E.cnt += 1
        assert E.cnt < 60000
        last.then_inc(self.sems[E.sem], 1)
        ev = (E.sem, E.cnt)
        for r in reads:
            if r.r.get(E.sem, 0) < E.cnt:
                r.r[E.sem] = E.cnt
        for w in writes:
            w.w = ev
            w.r = {}
        self.n_inst += 1
        return ev

    def dma(self, reads, writes, transfers, q="sp"):
        E = self.eng[q]
        self._wait(E, self._deps(reads, writes))
        R = writes[0] if writes else reads[0]
        if R.dsem is None:
            if R.name not in self.dsem_by_name:
                self.dsem_by_name[R.name] = self.new_sem("d_" + R.name)
                self.dcount[self.dsem_by_name[R.name]] = 0
            R.dsem = self.dsem_by_name[R.name]
        for out_ap, in_ap in transfers:
            E.h.dma_start(out=out_ap, in_=in_ap).then_inc(self.sems[R.dsem], 16)
            self.dcount[R.dsem] += 16
            self.n_inst += 1
        cnt = self.dcount[R.dsem]
        assert cnt < 60000
        R.dcnt = cnt
        ev = (R.dsem, cnt)
        for r in reads:
            if r.r.get(R.dsem, 0) < cnt:
                r.r[R.dsem] = cnt
        for w in writes:
            w.w = ev
            w.r = {}
        return ev

    def barrier(self):
        tgt = {E.sem: E.cnt for E in self.eng.values() if E.cnt > 0}
        for sidx, cnt in self.dcount.items():
            if cnt > 0:
                tgt[sidx] = cnt
        for E in self.eng.values():
            d = {s: v for s, v in tgt.items() if s != E.sem}
            self._wait(E, d)


def build_program(layers, nseq=NSEQ, do_mix=True, do_ffn=True, do_ple=True, debug=False):
    nc = bass.Bass("TRN2", target_bir_lowering=False, dynamic_dma_scratch_size=2048)
    NT = nseq * SEQ

    def din(name, shape, dt=F32):
        return nc.dram_tensor(name, shape, dt, kind="ExternalInput").ap()

    xT = din("xT", [D, NT])
    pT = din("pT", [DEPTH * PLE, NT])
    pos = din("pos", [1, NT], I32)
    cp_d = din("cp", [128, NCP])
    bs_d = din("b_s", [2, 1024])
    wsT_d = din("w_sT", [2, 128, 1024])
    w_mdown = din("mla_w_down", [2, D, 704])
    w_muq = din("mla_w_uq", [2, QL, 1536])
    w_mukv = din("mla_w_ukv", [2, KVL, 2048])
    w_mout = din("mla_w_out", [2, D, D])
    w_gin = din("gmlp_w_in", [2, D, 2 * GH])
    w_gout = din("gmlp_w_out", [2, GH, D])
    w_fup = din("ffn_w_up", [DEPTH, D, DFF])
    w_fdn = din("ffn_w_down", [DEPTH, DFF, D])
    w_pg = din("ple_w_gate", [DEPTH, D, D])
    w_pp = din("ple_w_proj", [DEPTH, PLE, D])
    outT = nc.dram_tensor("outT", [D, NT], F32, kind="ExternalOutput").ap()

    es = contextlib.ExitStack()
    with es:
        k = K(nc, es)
        hT = k.sb("hT", [128, KD, SEQ], F32)
        hres = [[Res(f"h{c}_{b}") for b in range(NB)] for c in range(KD)]
        stage = [k.sb(f"stage{i}", [128, 4096], F32) for i in range(2)]
        stres = [Res(f"st{i}") for i in range(2)]
        NSLOT = 3
        wsl = [k.sb(f"wsl{i}", [128, 4096], BF16) for i in range(NSLOT)]
        wres = [Res(f"w{i}") for i in range(NSLOT)]
        NF = 6
        ft = [k.sb(f"ft{i}", [128, T], F32) for i in range(NF)]
        fres = [Res(f"ft{i}") for i in range(NF)]
        NBT = 4
        bt = [k.sb(f"bt{i}", [128, T], BF16) for i in range(NBT)]
        bres = [Res(f"bt{i}") for i in range(NBT)]
        cp = k.sb("cp", [128, NCP], F32)
        cpd = k.sb("cpd", [128, 16], F32)
        ones = k.sb("ones", [128, 128], BF16)
        onesf = k.sb("onesf", [128, 64], F32)
        fold = k.sb("fold", [128, 64], F32)
        fold2 = k.sb("fold2", [128, 64], F32)
        st1 = k.sb("st1", [128, 16], F32)
        st2 = k.sb("st2", [128, 16], F32)
        stt = k.sb("stt", [128, 32], F32)
        cres = Res("const")
        sres = Res("stats")
        ARENA = 80 * 1024 // 2
        arena = k.sb("arena", [128, ARENA], BF16)
        pbank = [es.enter_context(nc.psum_tensor(f"pb{i}", [128, T], F32)) for i in range(8)]
        pres = [Res(f"pb{i}") for i in range(8)]

        rr = {"f": 0, "b": 0, "p": 0, "st": 0, "w": 0}

        def ftmp():
            i = rr["f"] % NF
            rr["f"] += 1
            return ft[i], fres[i]

        def btmp():
            i = rr["b"] % NBT
            rr["b"] += 1
            return bt[i], bres[i]

        bankset = [list(range(8))]

        def bank():
            s = bankset[0]
            i = s[rr["p"] % len(s)]
            rr["p"] += 1
            return pbank[i], pres[i]

        def av(off, n, dt=BF16):
            if dt == BF16:
                return arena[:, off:off + n]
            return arena[:, off:off + 2 * n].bitcast(F32)

        k.dma([], [cres], [(cp[:], cp_d[:, :])])
        k.op("dve", [], [cres], lambda h: h.memset(ones[:], 1.0))
        for i_ in range(2):
            k.op("pool", [], [stres[i_]], lambda h, i_=i_: h.memset(stage[i_][:], 0.0))
        k.op("dve", [], [cres], lambda h: h.memset(onesf[:], 1.0))
        k.op("dve", [], [cres], lambda h: h.memset(cpd[:], -SM_SHIFT))
        k.op("pool", [cres], [cres], lambda h: h.affine_select(
            out=fold[:], in_=onesf[:], pattern=[[-1, 64]], compare_op=ALU.is_equal, fill=0.0, base=0, channel_multiplier=1))
        k.op("pool", [cres], [cres], lambda h: h.affine_select(
            out=fold2[:], in_=onesf[:], pattern=[[-1, 64]], compare_op=ALU.is_equal, fill=0.0, base=-64, channel_multiplier=1))
        k.op("pool", [cres], [cres], lambda h: h.tensor_tensor(out=fold[:], in0=fold[:], in1=fold2[:], op=ALU.add))
        for j in range(2):
            k.op("dve", [cres], [cres], lambda h, j=j: h.tensor_scalar(
                out=cpd[:, j:j + 1], in0=cp[:, C_QN + j:C_QN + j + 1], scalar1=SM_SCALE, scalar2=None, op0=ALU.mult))
            k.op("dve", [cres], [cres], lambda h, j=j: h.tensor_scalar(
                out=cpd[:, 2 + j:3 + j], in0=cp[:, C_QR + j:C_QR + j + 1], scalar1=SM_SCALE, scalar2=None, op0=ALU.mult))

        specs = []
        keyid = {}

        def piece(tag, parts, post=None, key=None):
            key = tag if key is None else key
            if key[0] in CACHE_KINDS:
                keyid.setdefault(key, len(keyid))
            off = 0
            tr = []
            for part in parts:
                ap_, nk, width = part[:3]
                padw = part[3] if len(part) > 3 else width
                tr.append((off, nk, width, padw, ap_))
                off += nk * padw
            specs.append((tag, key, tr, off, post))

        def ffn_specs(l):
            for fg in range(8):
                piece(("fup", l, fg), [(w_fup[l, :, fg * 512:(fg + 1) * 512].rearrange("(k p) n -> p k n", p=128), 8, 512)])
                piece(("fdn", l, fg), [(w_fdn[l, fg * 512:(fg + 1) * 512, :].rearrange("(k p) n -> p k n", p=128), 4, 1024)])

        def ple_specs(l):
            for hf in range(2):
                piece(("pg", l, hf), [(w_pg[l, :, hf * 512:(hf + 1) * 512].rearrange("(k p) n -> p k n", p=128), 8, 512)])
                piece(("pp", l, hf), [(w_pp[l, :, hf * 512:(hf + 1) * 512].rearrange("(k p) n -> p k n", p=128), 2, 512)])

        def gmlp_specs(j):
            for tb in range(NB):
                for vp in range(4):
                    piece(("gv", j, tb, vp), [(w_gin[j, :, GH + vp * 512:GH + (vp + 1) * 512].rearrange("(k p) n -> p k n", p=128), 8, 512)], key=("gv", j, vp))
                for up in range(4):
                    piece(("gu", j, tb, up), [(w_gin[j, :, up * 512:(up + 1) * 512].rearrange("(k p) n -> p k n", p=128), 8, 512)], key=("gu", j, up))
                    piece(("go", j, tb, up), [(w_gout[j, up * 512:(up + 1) * 512, :].rearrange("(k p) n -> p k n", p=128), 4, 1024)], key=("go", j, up))

        def rot_post(off, nk, padw, r0):
            def post(slot, res):
                v = slot[:, off:off + nk * padw].rearrange("p (k n) -> p k n", k=nk)
                k.op("pool", [res], [res], lambda h: h.tensor_scalar(
                    out=v[:, :, r0 + 64:r0 + 96], in0=v[:, :, r0 + 32:r0 + 64], scalar1=-1.0, scalar2=None, op0=ALU.mult))
                k.op("pool", [res], [res], lambda h: h.tensor_copy(out=v[:, :, r0 + 96:r0 + 128], in_=v[:, :, r0:r0 + 32]))
            return post

        def mla_specs(j):
            piece(("ma1", j), [(w_mdown[j, :, 0:384].rearrange("(k p) n -> p k n", p=128), 8, 384)])
            piece(("ma2", j), [(w_mdown[j, :, 384:704].rearrange("(k p) n -> p k n", p=128), 8, 320, 384)],
                  post=rot_post(0, 8, 384, 256))
            for h_ in range(H):
                piece(("mh", j, h_), [
                    (w_muq[j, :, h_ * 192:(h_ + 1) * 192].rearrange("(k p) n -> p k n", p=128), 3, 192, 256),
                    (w_mukv[j, :, h_ * 256:(h_ + 1) * 256].rearrange("(k p) n -> p k n", p=128), 2, 256),
                    (w_mout[j, h_ * 128:(h_ + 1) * 128, :].rearrange("(k p) n -> p k n", p=128), 1, 1024),
                ], post=rot_post(0, 3, 256, 128))

        for s_ in range(nseq):
            for l in layers:
                if do_mix:
                    (mla_specs if l % 2 == 0 else gmlp_specs)(l // 2)
                if do_ffn:
                    ffn_specs(l)
                if do_ple:
                    ple_specs(l)

        wst = {"emitted": 0, "next": 0}
        slots = {}

        wscr = nc.dram_tensor("wscr", [max(len(keyid), 1), 128, 4096], BF16, kind="Internal").ap()
        NG = 6
        guard = [Res(f"scrg{g}") for g in range(NG)]
        for g_ in guard:
            g_.dsem = k.new_sem("d_" + g_.name)
            k.dcount[g_.dsem] = 0
        scr_res = {}
        wcnt = {"n": 0}

        def _emit_piece(i):
            tag, key, tr, total, post = specs[i]
            wi = rr["w"] % NSLOT
            rr["w"] += 1
            kid = keyid.get(key)
            if key in scr_res:
                k.dma([scr_res[key]], [wres[wi]], [(wsl[wi][:, 0:total], wscr[kid, :, 0:total])])
            else:
                si = rr["st"] % 2
                rr["st"] += 1
                k.dma([], [stres[si]], [
                    (stage[si][:, off:off + nk * pw_].rearrange("p (k n) -> p k n", k=nk)[:, :, 0:w_], ap_)
                    for off, nk, w_, pw_, ap_ in tr])
                if key[0] in ("ma1", "ma2", "mh"):
                    k.op("pool", [stres[si]], [wres[wi]], lambda h: h.tensor_copy(out=wsl[wi][:, 0:total], in_=stage[si][:, 0:total]))
                else:
                    k.op("act", [stres[si]], [wres[wi]], lambda h: h.activation(out=wsl[wi][:, 0:total], in_=stage[si][:, 0:total], func=AF.Copy))
                if post is not None:
                    post(wsl[wi], wres[wi])
                if key[0] in CACHE_KINDS and sum(1 for sp_ in specs if sp_[1] == key) > 1:
                    g_ = guard[wcnt["n"] % NG]
                    wcnt["n"] += 1
                    r_ = Res("scr")
                    r_.dsem = g_.dsem
                    k.dma([wres[wi]], [r_, g_], [(wscr[kid, :, 0:total], wsl[wi][:, 0:total])])
                    scr_res[key] = r_
            slots[i] = (wsl[wi], wres[wi])

        PREFETCH = 1

        def wnext(tag):
            i = wst["next"]
            wst["next"] += 1
            assert specs[i][0] == tag, (specs[i][0], tag)
            while wst["emitted"] <= min(i + PREFETCH, len(specs) - 1):
                _emit_piece(wst["emitted"])
                wst["emitted"] += 1
            return slots.pop(i)

        def stage_load(transfers_fn, total, dst_ap, dst_res):
            si = rr["st"] % 2
            rr["st"] += 1
            k.dma([], [stres[si]], transfers_fn(stage[si]))
            k.op("pool", [stres[si]], [dst_res], lambda h: h.tensor_copy(out=dst_ap, in_=stage[si][:, 0:total]))

        def mm(out_ap, lhsT, rhs, start, stop):
            return nc.tensor.matmul(out_ap, lhsT=lhsT, rhs=rhs, start=start, stop=stop)

        def rms_block(srcs, sres_l, P, gcol, Dn, outs, out_res_l, gscale_ap=None):
            n = len(srcs)
            sb_, sbr = bank()
            for c in range(n):
                q_, qr = btmp()
                k.op("act", [sres_l[c]], [qr], lambda h, c=c, q_=q_: h.activation(out=q_[0:P, :], in_=srcs[c], func=AF.Square))
                k.op("pe", [qr, cres], [sbr], lambda h, c=c, q_=q_: mm(sb_[0:P, :], ones[0:P, 0:P], q_[0:P, :], c == 0, c == n - 1))
            rs, rsr = ftmp()
            k.op("act", [sbr], [rsr], lambda h: h.activation(out=rs[0:P, :], in_=sb_[0:P, :], func=AF.Ln, scale=1.0 / Dn, bias=EPS))
            k.op("act", [rsr], [rsr], lambda h: h.activation(out=rs[0:P, :], in_=rs[0:P, :], func=AF.Exp, scale=-0.5))
            for c in range(n):
                k.op("dve", [sres_l[c], rsr, cres], [out_res_l[c]], lambda h, c=c: h.scalar_tensor_tensor(
                    out=outs[c], in0=srcs[c], scalar=gcol(c), in1=rs[0:P, :], op0=ALU.mult, op1=ALU.mult))
            return rs, rsr

        def rms_multi(items, Dn):
            sq = []
            for (src, sr, gcol, out, outr) in items:
                q_, qr = btmp()
                k.op("act", [sr], [qr], lambda h, q_=q_, src=src: h.activation(out=q_[:], in_=src, func=AF.Square))
                sq.append((q_, qr))
            ss = []
            for i in range(len(items)):
                sb_, sbr = bank()
                k.op("pe", [sq[i][1], cres], [sbr], lambda h, sb_=sb_, i=i: mm(sb_[:], ones[:, :], sq[i][0][:], True, True))
                ss.append((sb_, sbr))
            rl = []
            for i in range(len(items)):
                rs, rsr = ftmp()
                k.op("act", [ss[i][1]], [rsr], lambda h, rs=rs, i=i: h.activation(out=rs[:], in_=ss[i][0][:], func=AF.Ln, scale=1.0 / Dn, bias=EPS))
                rl.append((rs, rsr))
            for rs, rsr in rl:
                k.op("act", [rsr], [rsr], lambda h, rs=rs: h.activation(out=rs[:], in_=rs[:], func=AF.Exp, scale=-0.5))
            for i, (src, sr, gcol, out, outr) in enumerate(items):
                k.op("dve", [sr, rl[i][1], cres], [outr], lambda h, i=i, src=src, gcol=gcol, out=out: h.scalar_tensor_tensor(
                    out=out, in0=src, scalar=gcol, in1=rl[i][0][:], op0=ALU.mult, op1=ALU.mult))

        def norm_h(tb, gbase, hn_view, hn_res):
            sl = slice(tb * T, (tb + 1) * T)
            rms_block([hT[:, c, sl] for c in range(KD)], [hres[c][tb] for c in range(KD)], 128,
                      lambda c: cp[:, gbase + c:gbase + c + 1], D,
                      [hn_view[:, c, :] for c in range(KD)], hn_res)

        def resid_add(c, tb, bk, bkr):
            sl = slice(tb * T, (tb + 1) * T)
            k.op("dve", [bkr, hres[c][tb]], [hres[c][tb]], lambda h: h.tensor_tensor(
                out=hT[:, c, sl], in0=bk[:], in1=hT[:, c, sl], op=ALU.add))

        def ffn_phase(l):
            k.barrier()
            hn = av(0, KD * SEQ).rearrange("p (c t) -> p c t", c=KD)
            hnr = [[Res(f"hn{b}_{c}") for c in range(KD)] for b in range(NB)]
            hid = [av(KD * SEQ + i * 2048, 2048).rearrange("p (j t) -> p j t", j=4) for i in range(2)]
            hidr = [[Res(f"hid{i}_{j}") for j in range(4)] for i in range(2)]
            for tb in range(NB):
                norm_h(tb, C_FFN + l * 8, hn[:, :, tb * T:(tb + 1) * T], hnr[tb])
            it = 0
            for fg in range(8):
                wu, wur = wnext(("fup", l, fg))
                wd, wdr = wnext(("fdn", l, fg))
                wuv = wu[:, 0:4096].rearrange("p (k n) -> p k n", k=8)
                wdv = wd[:, 0:4096].rearrange("p (k n) -> p k n", k=4)

                def up(tb, hi):
                    for j in range(4):
                        bk, bkr = bank()
                        k.op("pe", [wur] + hnr[tb], [bkr], lambda h, j=j, bk=bk: [
                            mm(bk[:], wuv[:, kk, j * 128:(j + 1) * 128], hn[:, kk, tb * T:(tb + 1) * T], kk == 0, kk == KD - 1)
                            for kk in range(KD)][-1])
                        sq, sqr = ftmp()
                        k.op("act", [bkr], [sqr], lambda h, bk=bk, sq=sq: h.activation(out=sq[:], in_=bk[:], func=AF.Square))
                        k.op("dve", [bkr, sqr], [hidr[hi][j]], lambda h, j=j, bk=bk, sq=sq: h.scalar_tensor_tensor(
                            out=hid[hi][:, j, :], in0=bk[:], scalar=0.0, in1=sq[:], op0=ALU.is_gt, op1=ALU.mult))

                def down(tb, hi):
                    for c in range(KD):
                        bk, bkr = bank()
                        k.op("pe", [wdr] + hidr[hi], [bkr], lambda h, c=c, bk=bk: [
                            mm(bk[:], wdv[:, j, c * 128:(c + 1) * 128], hid[hi][:, j, :], j == 0, j == 3)
                            for j in range(4)][-1])
                        resid_add(c, tb, bk, bkr)

                up(0, it % 2)
                for tb in range(NB):
                    if tb + 1 < NB:
                        up(tb + 1, (it + 1) % 2)
                    down(tb, it % 2)
                    it += 1

        def ple_phase(l, s_):
            k.barrier()
            hn = av(0, KD * SEQ).rearrange("p (c t) -> p c t", c=KD)
            hnr = [[Res(f"hn{b}_{c}") for c in range(KD)] for b in range(NB)]
            pb_ = av(KD * SEQ, 2 * SEQ)
            pbv = pb_.rearrange("p (c t) -> p c t", c=2)
            pbr = Res("pb")
            stage_load(lambda st: [(st[:, 0:2 * SEQ].rearrange("p (c t) -> p c t", c=2),
                                    pT[l * PLE:(l + 1) * PLE, s_ * SEQ:(s_ + 1) * SEQ].rearrange("(c p) t -> p c t", p=128))],
                       2 * SEQ, pb_, pbr)
            for tb in range(NB):
                norm_h(tb, C_PLE + l * 8, hn[:, :, tb * T:(tb + 1) * T], hnr[tb])
            for hf in range(2):
                wg, wgr = wnext(("pg", l, hf))
                wp, wpr = wnext(("pp", l, hf))
                wgv = wg[:, 0:4096].rearrange("p (k n) -> p k n", k=8)
                wpv = wp[:, 0:1024].rearrange("p (k n) -> p k n", k=2)
                for tb in range(NB):
                    sl = slice(tb * T, (tb + 1) * T)
                    for c4 in range(4):
                        c = hf * 4 + c4
                        gb, gbr = bank()
                        k.op("pe", [wgr] + hnr[tb], [gbr], lambda h, gb=gb, c4=c4: [
                            mm(gb[:], wgv[:, kk, c4 * 128:(c4 + 1) * 128], hn[:, kk, sl], kk == 0, kk == KD - 1)
                            for kk in range(KD)][-1])
                        qb_, qbr = bank()
                        k.op("pe", [wpr, pbr], [qbr], lambda h, qb_=qb_, c4=c4: [
                            mm(qb_[:], wpv[:, kk, c4 * 128:(c4 + 1) * 128], pbv[:, kk, sl], kk == 0, kk == 1)
                            for kk in range(2)][-1])
                        gs, gsr = ftmp()
                        k.op("act", [gbr], [gsr], lambda h, gb=gb, gs=gs: h.activation(out=gs[:], in_=gb[:], func=AF.Sigmoid))
                        k.op("dve", [gsr, qbr], [gsr], lambda h, qb_=qb_, gs=gs: h.tensor_tensor(
                            out=gs[:], in0=qb_[:], in1=gs[:], op=ALU.mult))
                        k.op("pool", [gsr, hres[c][tb]], [hres[c][tb]], lambda h, gs=gs, c=c: h.tensor_tensor(
                            out=hT[:, c, sl], in0=hT[:, c, sl], in1=gs[:], op=ALU.add))

        def gmlp_phase(l, s_):
            j = l // 2
            k.barrier()
            hn = av(0, KD * T).rearrange("p (c t) -> p c t", c=KD)
            hnr = [Res(f"hnb{c}") for c in range(KD)]
            vt = av(4096, 4 * GH).rearrange("p (c f) -> p c f", c=4)
            vtr = [[Res(f"vt{c}_{v}") for v in range(4)] for c in range(4)]
            cst = av(12288, 2048, F32).rearrange("p (c t) -> p c t", c=16)
            cstr = Res("cst")
            y = [av(16384 + i * 2048, 2048).rearrange("p (j t) -> p j t", j=4) for i in range(2)]
            yr = [[Res(f"y{i}_{f}") for f in range(4)] for i in range(2)]
            wsf = av(20480, 1024, F32)
            wsfr = Res("wsf")
            bsb = av(22528, 1024, F32)
            bsbr = Res("bsb")
            wsb = av(24576, 1024).rearrange("p (g t) -> p g t", g=8)
            wsbr = Res("wsb")
            k.dma([], [wsfr], [(wsf, wsT_d[j, :, :])])
            k.dma([], [bsbr], [(bsb, bs_d[j:j + 1, :].broadcast_to([128, 1024]))])
            k.op("pool", [wsfr], [wsbr], lambda h: h.affine_select(
                out=wsb, in_=wsf.rearrange("p (g t) -> p g t", g=8), pattern=[[0, 8], [1, 128]],
                compare_op=ALU.is_ge, fill=0.0, base=0, channel_multiplier=-1))
            rb = []
            for hf in range(2):
                b_, br_ = bank()
                k.op("pe", [wsbr, cres], [br_], lambda h, b_=b_, hf=hf: mm(
                    b_[:], ones[:, :], wsb[:, hf * 4:(hf + 1) * 4, :], True, True))
                rb.append((b_, br_))
            for fc in range(16):
                g = fc // 2
                b_, br_ = rb[g // 4]
                k.op("dve", [br_, bsbr, cres], [cstr], lambda h, fc=fc, g=g, b_=b_: h.scalar_tensor_tensor(
                    out=cst[:, fc, :], in0=b_[:, (g % 4) * 128:(g % 4 + 1) * 128],
                    scalar=cp[:, C_LNB + j * 16 + fc:C_LNB + j * 16 + fc + 1],
                    in1=bsb[:, g * 128:(g + 1) * 128], op0=ALU.mult, op1=ALU.add))
            yi = 0
            for tb in range(NB):
                norm_h(tb, C_MIX + l * 8, hn, hnr)
                for vp in range(4):
                    wv, wvr = wnext(("gv", j, tb, vp))
                    wvv = wv[:, 0:4096].rearrange("p (k n) -> p k n", k=8)
                    for tc in range(4):
                        bk, bkr = bank()
                        k.op("pe", [wvr] + hnr, [bkr], lambda h, bk=bk, tc=tc: [
                            mm(bk[:], hn[:, kk, tc * 128:(tc + 1) * 128], wvv[:, kk, :], kk == 0, kk == KD - 1)
                            for kk in range(KD)][-1])
                        col = tc * 4 + vp
                        k.op("act", [bkr], [vtr[tc][vp], sres], lambda h, bk=bk, tc=tc, vp=vp, col=col: h.activation(
                            out=vt[:, tc, vp * 512:(vp + 1) * 512], in_=bk[:], func=AF.Gelu_apprx_tanh,
                            accum_out=st1[:, col:col + 1]))
                        jk, jkr = btmp()
                        k.op("act", [vtr[tc][vp]], [jkr, sres], lambda h, jk=jk, tc=tc, vp=vp, col=col: h.activation(
                            out=jk[:], in_=vt[:, tc, vp * 512:(vp + 1) * 512], func=AF.Square,
                            accum_out=st2[:, col:col + 1]))
                k.op("dve", [sres], [sres], lambda h: h.tensor_reduce(
                    out=stt[:, 0:4], in_=st1[:, 0:16].rearrange("p (a b) -> p a b", a=4), axis=mybir.AxisListType.X, op=ALU.add))
                k.op("dve", [sres], [sres], lambda h: h.tensor_reduce(
                    out=stt[:, 4:8], in_=st2[:, 0:16].rearrange("p (a b) -> p a b", a=4), axis=mybir.AxisListType.X, op=ALU.add))
                k.op("dve", [sres], [sres], lambda h: h.tensor_scalar(
                    out=stt[:, 8:12], in0=stt[:, 0:4], scalar1=1.0 / GH, scalar2=None, op0=ALU.mult))
                k.op("dve", [sres], [sres], lambda h: h.tensor_tensor(
                    out=stt[:, 12:16], in0=stt[:, 8:12], in1=stt[:, 8:12], op=ALU.mult))
                k.op("dve", [sres], [sres], lambda h: h.scalar_tensor_tensor(
                    out=stt[:, 16:20], in0=stt[:, 4:8], scalar=1.0 / GH, in1=stt[:, 12:16], op0=ALU.mult, op1=ALU.subtract))
                k.op("act", [sres], [sres], lambda h: h.activation(
                    out=stt[:, 20:24], in_=stt[:, 16:20], func=AF.Sqrt, scale=1.0, bias=EPS))
                k.op("dve", [sres], [sres], lambda h: h.reciprocal(out=stt[:, 24:28], in_=stt[:, 20:24]))
                for tc in range(4):
                    k.op("dve", [sres] + vtr[tc], vtr[tc], lambda h, tc=tc: h.tensor_scalar(
                        out=vt[:, tc, :], in0=vt[:, tc, :], scalar1=stt[:, 8 + tc:9 + tc], scalar2=stt[:, 24 + tc:25 + tc],
                        op0=ALU.subtract, op1=ALU.mult))
                for up in range(4):
                    wu, wur = wnext(("gu", j, tb, up))
                    wo, wor = wnext(("go", j, tb, up))
                    wuv = wu[:, 0:4096].rearrange("p (k n) -> p k n", k=8)
                    wov = wo[:, 0:4096].rearrange("p (k n) -> p k n", k=4)
                    yy, yyr = y[yi % 2], yr[yi % 2]
                    yi += 1
                    for fl in range(4):
                        fc = up * 4 + fl
                        g = fc // 2
                        ub, ubr = bank()
                        k.op("pe", [wur] + hnr, [ubr], lambda h, ub=ub, fl=fl: [
                            mm(ub[:], wuv[:, kk, fl * 128:(fl + 1) * 128], hn[:, kk, :], kk == 0, kk == KD - 1)
                            for kk in range(KD)][-1])
                        us, usr = ftmp()
                        k.op("act", [ubr], [usr], lambda h, ub=ub, us=us: h.activation(out=us[:], in_=ub[:], func=AF.Gelu_apprx_tanh))
                        sb_, sbr = bank()
                        k.op("pe", [vtr[tc][fc // 4] for tc in range(4)] + [wsbr], [sbr], lambda h, sb_=sb_, fc=fc, g=g: [
                            mm(sb_[:, tc * 128:(tc + 1) * 128], vt[:, tc, fc * 128:(fc + 1) * 128], wsb[:, g, :], True, True)
                            for tc in range(4)][-1])
                        sv, svr = ftmp()
                        k.op("dve", [sbr, cstr, cres], [svr], lambda h, sb_=sb_, sv=sv, fc=fc: h.scalar_tensor_tensor(
                            out=sv[:].rearrange("p (a t) -> p a t", a=4), in0=sb_[:].rearrange("p (a t) -> p a t", a=4),
                            scalar=cp[:, C_LNG + j * 16 + fc:C_LNG + j * 16 + fc + 1],
                            in1=cst[:, fc:fc + 1, :].broadcast_to([128, 4, 128]), op0=ALU.mult, op1=ALU.add))
                        k.op("pool", [usr, svr], [yyr[fl]], lambda h, us=us, sv=sv, yy=yy, fl=fl: h.tensor_tensor(
                            out=yy[:, fl, :], in0=us[:], in1=sv[:], op=ALU.mult))
                    for c in range(KD):
                        bk, bkr = bank()
                        k.op("pe", [wor] + yyr, [bkr], lambda h, bk=bk, c=c, yy=yy: [
                            mm(bk[:], wov[:, fl, c * 128:(c + 1) * 128], yy[:, fl, :], fl == 0, fl == 3)
                            for fl in range(4)][-1])
                        resid_add(c, tb, bk, bkr)

        def mla_phase(l, s_):
            j = l // 2
            k.barrier()
            hn = av(0, KD * T).rearrange("p (c t) -> p c t", c=KD)
            hnr = [Res(f"hnb{c}") for c in range(KD)]
            cq = av(4096, 3 * SEQ).rearrange("p (c t) -> p c t", c=3)
            cqr = [Res(f"cq{b}") for b in range(NB)]
            ckv = av(10240, 2 * SEQ).rearrange("p (c t) -> p c t", c=2)
            ckvr = [Res(f"ckv{b}") for b in range(NB)]
            kr = av(14336, SEQ)
            krr = [Res(f"kr{b}") for b in range(NB)]
            qn = av(16384, SEQ)
            qnr = [Res(f"qn{b}") for b in range(NB)]
            qr_ = av(18432, SEQ)
            qrr = [Res(f"qr{b}") for b in range(NB)]
            kn = av(20480, SEQ)
            knr = [Res(f"kn{b}") for b in range(NB)]
            vtk = av(22528, SEQ).rearrange("p (c d) -> p c d", c=16)
            vtkr = [Res(f"vk{b}") for b in range(NB)]
            oth = av(24576, SEQ)
            othr = [Res(f"ot{b}") for b in range(NB)]
            raw = av(26624, 3 * T, F32).rearrange("p (c t) -> p c t", c=3)
            rawr = [Res(f"raw{c}") for c in range(3)]
            pi_ = av(29696, T, F32)
            pi_i = pi_.bitcast(I32)
            pir = Res("pi")
            rt = [av(30720 + i * 1024, T, F32) for i in range(2)]
            rtr = [Res(f"rt{i}") for i in range(2)]
            cst_ = av(32768, SEQ, F32)
            cstr = [Res(f"cst{b}") for b in range(NB)]
            TWO_PI = 2.0 * math.pi
            C1 = 6.28125
            C2 = TWO_PI - C1

            def rope_tables(tb):
                sl = slice(tb * T, (tb + 1) * T)
                t0 = s_ * SEQ + tb * T
                k.dma([], [pir], [(pi_i, pos[0:1, t0:t0 + T].broadcast_to([128, T]))])
                a, ar = rt[0], rtr[0]
                b_, br_ = rt[1], rtr[1]
                k.op("dve", [pir], [ar], lambda h: h.tensor_copy(out=a, in_=pi_i))
                k.op("dve", [ar, cres], [ar], lambda h: h.tensor_scalar(
                    out=a, in0=a, scalar1=cp[:, C_INVF:C_INVF + 1], scalar2=None, op0=ALU.mult))
                k.op("dve", [ar, pir], [pir], lambda h: h.tensor_scalar(
                    out=pi_i, in0=a, scalar1=1.0 / TWO_PI, scalar2=None, op0=ALU.mult))
                k.op("dve", [pir], [br_], lambda h: h.tensor_copy(out=b_, in_=pi_i))
                k.op("dve", [ar, br_], [ar], lambda h: h.scalar_tensor_tensor(
                    out=a, in0=b_, scalar=-C1, in1=a, op0=ALU.mult, op1=ALU.add))
                k.op("dve", [ar, br_], [ar], lambda h: h.scalar_tensor_tensor(
                    out=a, in0=b_, scalar=-C2, in1=a, op0=ALU.mult, op1=ALU.add))
                k.op("dve", [ar], [ar], lambda h: h.tensor_scalar(
                    out=a, in0=a, scalar1=math.pi, scalar2=-math.pi, op0=ALU.min, op1=ALU.max))
                k.op("act", [ar], [cstr[tb]], lambda h: h.activation(out=cst_[64:128, sl], in_=a[64:128, :], func=AF.Sin))
                k.op("act", [ar], [br_], lambda h: h.activation(out=b_[0:64, :], in_=a[0:64, :], func=AF.Sin, scale=0.5))
                k.op("dve", [br_], [br_], lambda h: h.tensor_tensor(out=b_[0:64, :], in0=b_[0:64, :], in1=b_[0:64, :], op=ALU.mult))
                k.op("dve", [br_], [cstr[tb]], lambda h: h.tensor_scalar(
                    out=cst_[0:64, sl], in0=b_[0:64, :], scalar1=-2.0, scalar2=1.0, op0=ALU.mult, op1=ALU.add))

            def rope_norm(xx, xxr, ggcol, tb, out_ap, out_res):
                sl = slice(tb * T, (tb + 1) * T)
                sb_, sbr = bank()
                q_, qr2 = btmp()
                k.op("act", [xxr], [qr2], lambda h: h.activation(out=q_[0:64, :], in_=xx[0:64, :], func=AF.Square))
                k.op("pe", [qr2, cres], [sbr], lambda h: mm(sb_[0:64, :], ones[0:64, 0:64], q_[0:64, :], True, True))
                rs, rsr = ftmp()
                k.op("act", [sbr], [rsr], lambda h: h.activation(out=rs[0:64, :], in_=sb_[0:64, :], func=AF.Ln, scale=1.0 / RD, bias=EPS))
                k.op("act", [rsr], [rsr], lambda h: h.activation(out=rs[0:64, :], in_=rs[0:64, :], func=AF.Exp, scale=-0.5))
                t1, t1r = ftmp()
                k.op("dve", [xxr, qr2, cstr[tb], cres], [t1r], lambda h: h.scalar_tensor_tensor(
                    out=t1[:], in0=xx[:], scalar=ggcol, in1=cst_[:, sl], op0=ALU.mult, op1=ALU.mult))
                fb, fbr = bank()
                k.op("pe", [t1r, cres], [fbr], lambda h: mm(fb[0:64, :], fold[:, :], t1[:], True, True))
                k.op("dve", [fbr, rsr], [out_res], lambda h: h.tensor_tensor(out=out_ap, in0=fb[0:64, :], in1=rs[0:64, :], op=ALU.mult))

            w1, w1r = wnext(("ma1", j))
            w2, w2r = wnext(("ma2", j))
            w1v = w1[:, 0:3072].rearrange("p (k n) -> p k n", k=8)
            w2v = w2[:, 0:3072].rearrange("p (k n) -> p k n", k=8)
            for tb in range(NB):
                sl = slice(tb * T, (tb + 1) * T)
                rope_tables(tb)
                norm_h(tb, C_MIX + l * 8, hn, hnr)
                for (wv_, wr_, nchunk, dst, dres, gb, Dn) in ((w1v, w1r, 3, cq, cqr, C_QL + j * 3, QL), (w2v, w2r, 2, ckv, ckvr, C_KVL + j * 2, KVL)):
                    for m in range(nchunk):
                        bk, bkr = bank()
                        k.op("pe", [wr_] + hnr, [bkr], lambda h, bk=bk, m=m, wv_=wv_: [
                            mm(bk[:], wv_[:, kk, m * 128:(m + 1) * 128], hn[:, kk, :], kk == 0, kk == KD - 1)
                            for kk in range(KD)][-1])
                        k.op("act", [bkr], [rawr[m]], lambda h, bk=bk, m=m: h.activation(out=raw[:, m, :], in_=bk[:], func=AF.Copy))
                    rms_block([raw[:, m, :] for m in range(nchunk)], rawr[:nchunk], 128,
                              lambda c, gb=gb: cp[:, gb + c:gb + c + 1], Dn,
                              [dst[:, m, sl] for m in range(nchunk)], [dres[tb]] * nchunk)
                xx, xxr = bank()
                k.op("pe", [w2r] + hnr, [xxr], lambda h, xx=xx: [
                    mm(xx[:], w2v[:, kk, 256:384], hn[:, kk, :], kk == 0, kk == KD - 1) for kk in range(KD)][-1])
                rope_norm(xx, xxr, cp[:, C_KR + j:C_KR + j + 1], tb, kr[0:64, sl], krr[tb])

            kn2 = [kn, av(36864, SEQ)]
            vtk2 = [vtk, av(38912, SEQ).rearrange("p (c d) -> p c d", c=16)]
            knr2 = [knr, [Res(f"knb{b}") for b in range(NB)]]
            vtkr2 = [vtkr, [Res(f"vkb{b}") for b in range(NB)]]
            bankset[0] = [0, 1, 2, 3]
            hw = {}

            def views(h_):
                wh, whr = hw[h_]
                return (wh[:, 0:768].rearrange("p (k n) -> p k n", k=3),
                        wh[:, 768:1280].rearrange("p (k n) -> p k n", k=2),
                        wh[:, 1280:2304], whr)

            def proj_qn(h_, tbs):
                wq, wkv, wo, whr = views(h_)
                items = []
                for tb in tbs:
                    sl = slice(tb * T, (tb + 1) * T)
                    bk, bkr = bank()
                    k.op("pe", [whr, cqr[tb]], [bkr], lambda h, bk=bk, sl=sl: [
                        mm(bk[:], wq[:, kk, 0:128], cq[:, kk, sl], kk == 0, kk == 2) for kk in range(3)][-1])
                    items.append((bk[:], bkr, cpd[:, j:j + 1], qn[:, sl], qnr[tb]))
                rms_multi(items, 128)

            def proj_qr(h_, tb):
                wq, wkv, wo, whr = views(h_)
                sl = slice(tb * T, (tb + 1) * T)
                xx, xxr = bank()
                k.op("pe", [whr, cqr[tb]], [xxr], lambda h: [
                    mm(xx[:], wq[:, kk, 128:256], cq[:, kk, sl], kk == 0, kk == 2) for kk in range(3)][-1])
                rope_norm(xx, xxr, cpd[:, 2 + j:3 + j], tb, qr_[0:64, sl], qrr[tb])

            def proj_q(h_, tbs):
                for i in range(0, len(tbs), 2):
                    proj_qn(h_, tbs[i:i + 2])
                for tb in tbs:
                    proj_qr(h_, tb)

            def proj_kv(h_, tbs):
                wq, wkv, wo, whr = views(h_)
                b_ = h_ % 2
                for i in range(0, len(tbs), 2):
                    items = []
                    for tb in tbs[i:i + 2]:
                        sl = slice(tb * T, (tb + 1) * T)
                        bk, bkr = bank()
                        k.op("pe", [whr, ckvr[tb]], [bkr], lambda h, bk=bk, sl=sl: [
                            mm(bk[:], wkv[:, kk, 0:128], ckv[:, kk, sl], kk == 0, kk == 1) for kk in range(2)][-1])
                        items.append((bk[:], bkr, cp[:, C_KN + j:C_KN + j + 1], kn2[b_][:, sl], knr2[b_][tb]))
                    rms_multi(items, 128)
                for tb in tbs:
                    bv, bvr = bank()
                    k.op("pe", [whr, ckvr[tb]], [bvr], lambda h, bv=bv, tb=tb: [
                        mm(bv[:, kt4 * 128:(kt4 + 1) * 128], ckv[:, kk, (tb * 4 + kt4) * 128:(tb * 4 + kt4 + 1) * 128],
                           wkv[:, kk, 128:256], kk == 0, kk == 1)
                        for kt4 in range(4) for kk in range(2)][-1])
                    k.op("act", [bvr], [vtkr2[b_][tb]], lambda h, bv=bv, tb=tb: h.activation(
                        out=vtk2[b_][:, tb * 4:(tb + 1) * 4, :], in_=bv[:].rearrange("p (a d) -> p a d", a=4), func=AF.Copy))

            def att(h_, qb):
                b_ = h_ % 2
                kn_, vtk_, knr_, vtkr_ = kn2[b_], vtk2[b_], knr2[b_], vtkr2[b_]
                ob, obr = pbank[4 + qb % 2], pres[4 + qb % 2]
                db, dbr = pbank[6 + qb % 2], pres[6 + qb % 2]
                nkt = 4 * qb + 4

                def s_exp(kt):
                    jd = kt - 4 * qb
                    c0 = max(jd, 0) * 128
                    qs = slice(qb * T + c0, (qb + 1) * T)
                    ks = slice(kt * 128, (kt + 1) * 128)
                    sbk, sbkr = bank()
                    k.op("pe", [knr_[kt // 4], qnr[qb], krr[kt // 4], qrr[qb]], [sbkr], lambda h: [
                        mm(sbk[:, c0:T], kn_[:, ks], qn[:, qs], True, False),
                        mm(sbk[:, c0:T], kr[0:64, ks], qr_[0:64, qs], False, True)][-1])
                    pt, ptr = btmp()
                    k.op("act", [sbkr, cres], [ptr], lambda h: h.activation(
                        out=pt[:, c0:T], in_=sbk[:, c0:T], func=AF.Exp, bias=cpd[:, 8:9]))
                    if jd >= 0:
                        k.op("pool", [ptr], [ptr], lambda h: h.affine_select(
                            out=pt[:, c0:c0 + 128], in_=pt[:, c0:c0 + 128], pattern=[[1, 128]],
                            compare_op=ALU.is_ge, fill=0.0, base=0, channel_multiplier=-1))
                    return pt, ptr, c0

                pend = s_exp(0)
                for kt in range(nkt):
                    nxt = s_exp(kt + 1) if kt + 1 < nkt else None
                    pt, ptr, c0 = pend
                    k.op("pe", [ptr, vtkr_[kt // 4]], [obr], lambda h, pt=pt, c0=c0, kt=kt: mm(
                        ob[:, c0:T], vtk_[:, kt, :], pt[:, c0:T], kt == 0, kt == nkt - 1))
                    k.op("pe", [ptr, cres], [dbr], lambda h, pt=pt, c0=c0, kt=kt: mm(
                        db[:, c0:T], ones[:, :], pt[:, c0:T], kt == 0, kt == nkt - 1))
                    pend = nxt
                rd, rdr = ftmp()
                k.op("dve", [dbr], [rdr], lambda h: h.reciprocal(out=rd[:], in_=db[:]))
                k.op("dve", [obr, rdr], [othr[qb]], lambda h: h.tensor_tensor(
                    out=oth[:, qb * T:(qb + 1) * T], in0=ob[:], in1=rd[:], op=ALU.mult))

            def outproj(h_):
                wq, wkv, wo, whr = views(h_)
                for tb in range(NB):
                    for c in range(KD):
                        bk, bkr = bank()
                        k.op("pe", [whr, othr[tb]], [bkr], lambda h, bk=bk, c=c, tb=tb: mm(
                            bk[:], wo[:, c * 128:(c + 1) * 128], oth[:, tb * T:(tb + 1) * T], True, True))
                        resid_add(c, tb, bk, bkr)

            hw[0] = wnext(("mh", j, 0))
            proj_kv(0, [0, 1, 2, 3])
            proj_q(0, [0, 1, 2, 3])
            for h_ in range(H):
                for qb in range(3):
                    att(h_, qb)
                if h_ + 1 < H:
                    hw[h_ + 1] = wnext(("mh", j, h_ + 1))
                    proj_kv(h_ + 1, [0, 1, 2, 3])
                    proj_q(h_ + 1, [0, 1, 2])
                att(h_, 3)
                if h_ + 1 < H:
                    proj_q(h_ + 1, [3])
                outproj(h_)
            bankset[0] = list(range(8))

        outres = Res("out")
        for s_ in range(nseq):
            k.dma([], [hres[c][b] for c in range(KD) for b in range(NB)],
                  [(hT[:, c, :], xT[c * 128:(c + 1) * 128, s_ * SEQ:(s_ + 1) * SEQ]) for c in range(KD)])
            for l in layers:
                if do_mix:
                    if l % 2 == 0:
                        mla_phase(l, s_)
                    else:
                        gmlp_phase(l, s_)
                if do_ffn:
                    ffn_phase(l)
                if do_ple:
                    ple_phase(l, s_)
            k.dma([hres[c][b] for c in range(KD) for b in range(NB)], [outres],
                  [(outT[c * 128:(c + 1) * 128, s_ * SEQ:(s_ + 1) * SEQ], hT[:, c, :]) for c in range(KD)])
        assert wst["next"] == len(specs)
        if debug:
            k.barrier()
            dres = Res("dbg")
            ad = nc.dram_tensor("arena_dump", [128, ARENA], BF16, kind="ExternalOutput").ap()
            k.dma([], [dres], [(ad[:, :], arena[:, :])])
            k.eng["sp"].h.wait_ge(k.sems[dres.dsem], dres.dcnt)
        sp = k.eng["sp"]
        sp.h.wait_ge(k.sems[outres.dsem], outres.dcnt)
        print(f"[build] layers={layers} nseq={nseq} ops: " + " ".join(f"{n}={e.cnt}" for n, e in k.eng.items())
              + f" sems={len(k.sems)} n_inst~{k.n_inst}")
    return nc


def _pack_consts(inp):
    cp = np.zeros((128, NCP), np.float32)

    def chunks(v, n):
        return np.ascontiguousarray(v.reshape(n, 128).T)

    for l in range(DEPTH):
        cp[:, C_MIX + l * 8:C_MIX + l * 8 + 8] = chunks(inp["norm_mix"][l], 8)
        cp[:, C_FFN + l * 8:C_FFN + l * 8 + 8] = chunks(inp["norm_ffn"][l], 8)
        cp[:, C_PLE + l * 8:C_PLE + l * 8 + 8] = chunks(inp["norm_ple"][l], 8)
    perm = (np.arange(64) + 32) % 64
    for j in range(2):
        cp[:, C_QL + j * 3:C_QL + j * 3 + 3] = chunks(inp["mla_q_lora_g"][j], 3)
        cp[:, C_KVL + j * 2:C_KVL + j * 2 + 2] = chunks(inp["mla_kv_lora_g"][j], 2)
        cp[:, C_QN + j] = inp["mla_q_nope_g"][j]
        cp[:, C_KN + j] = inp["mla_k_nope_g"][j]
        cp[:64, C_QR + j] = inp["mla_q_rope_g"][j]
        cp[64:, C_QR + j] = inp["mla_q_rope_g"][j][perm]
        cp[:64, C_KR + j] = inp["mla_k_rope_g"][j]
        cp[64:, C_KR + j] = inp["mla_k_rope_g"][j][perm]
        cp[:, C_LNG + j * 16:C_LNG + j * 16 + 16] = chunks(inp["gmlp_ln_g"][j], 16)
        cp[:, C_LNB + j * 16:C_LNB + j * 16 + 16] = chunks(inp["gmlp_ln_b"][j], 16)
    invf = (10000.0 ** (-(np.arange(0, 64, 2, dtype=np.float32) / np.float32(64)))).astype(np.float32)
    cp[:, C_INVF] = np.concatenate([invf, invf, invf, invf])
    return cp


_PROG_CACHE = {}


def _get_prog(layers):
    key = tuple(layers)
    if key not in _PROG_CACHE:
        _PROG_CACHE[key] = build_program(list(layers))
    return _PROG_CACHE[key]


def kernel(**inp):
    inp = {k_: np.asarray(v) for k_, v in inp.items()}
    x = inp["x"].astype(np.float32, copy=False)
    p = inp["p"].astype(np.float32, copy=False)
    positions = inp["positions"].astype(np.int32, copy=False)
    cp = _pack_consts(inp)
    shared = {
        "cp": cp,
        "b_s": np.ascontiguousarray(inp["gmlp_b_s"].reshape(2, 1024)),
        "w_sT": np.ascontiguousarray(inp["gmlp_w_s"].transpose(0, 3, 1, 2).reshape(2, 128, 1024)),
    }
    for n in ("mla_w_down", "mla_w_uq", "mla_w_ukv", "mla_w_out", "gmlp_w_in", "gmlp_w_out",
              "ffn_w_up", "ffn_w_down", "ple_w_gate", "ple_w_proj"):
        shared[n] = np.ascontiguousarray(inp[n], dtype=np.float32)
    per_core = []
    for c in range(N_CORES):
        b0 = c * NSEQ
        xs = x[b0:b0 + NSEQ].reshape(NSEQ * SEQ, D)
        ps = p[:, b0:b0 + NSEQ].reshape(DEPTH, NSEQ * SEQ, PLE)
        per_core.append({
            "xT": np.ascontiguousarray(xs.T),
            "pT": np.ascontiguousarray(ps.transpose(0, 2, 1).reshape(DEPTH * PLE, NSEQ * SEQ)),
            "pos": np.ascontiguousarray(positions[b0:b0 + NSEQ].reshape(1, NSEQ * SEQ)),
        })
    launches = [list(range(DEPTH))] if FUSED else [[l] for l in range(DEPTH)]
    for layers in launches:
        nc = _get_prog(layers)
        in_maps = [dict(shared, **pc) for pc in per_core]
        res = run_bass_kernel_spmd(nc, in_maps, core_ids=list(range(N_CORES)))
        for c in range(N_CORES):
            per_core[c]["xT"] = np.ascontiguousarray(res.results[c]["outT"])
    out = np.empty((BATCH, SEQ, D), np.float32)
    for c in range(N_CORES):
        out[c * NSEQ:(c + 1) * NSEQ] = per_core[c]["xT"].T.reshape(NSEQ, SEQ, D)
    return out
```

```python
import contextlib
import math

import numpy as np
import concourse.bass as bass
import concourse.mybir as mybir
from concourse.bass_utils import run_bass_kernel_spmd

F32 = mybir.dt.float32
BF16 = mybir.dt.bfloat16
I32 = mybir.dt.int32
AF = mybir.ActivationFunctionType
ALU = mybir.AluOpType

N_CORES = 8
D = 1024
SEQ = 2048
BATCH = 16
DEPTH = 4
T = 512
NB = SEQ // T
KD = D // 128
NSEQ = BATCH // N_CORES
H = 8
QL, KVL, RD = 384, 256, 64
DFF = 4096
GH = 2048
PLE = 256
EPS = 1e-6
SM_SCALE = (128 + 64) ** -0.5
SM_SHIFT = 16.0

SAME_ENGINE_SYNC = True
CACHE_KINDS = ("gv", "gu", "go")
FUSED = True

C_MIX, C_FFN, C_PLE = 0, 32, 64
C_QL, C_KVL = 96, 102
C_QN, C_KN = 106, 108
C_QR, C_KR = 110, 114
C_LNG, C_LNB = 118, 150
C_INVF = 182
NCP = 184


class Res:
    __slots__ = ("name", "w", "r", "dsem", "dcnt")

    def __init__(self, name):
        self.name = name
        self.w = None
        self.r = {}
        self.dsem = None
        self.dcnt = 0


class Eng:
    def __init__(self, name, h, sem):
        self.name, self.h, self.sem = name, h, sem
        self.cnt = 0
        self.waited = {}


class K:
    def __init__(self, nc, es):
        self.nc, self.es = nc, es
        self.sems = []
        self.eng = {}
        for n, h in (("pe", nc.tensor), ("act", nc.scalar), ("dve", nc.vector),
                     ("pool", nc.gpsimd), ("sp", nc.sync)):
            self.eng[n] = Eng(n, h, self.new_sem("e_" + n))
        self.dsem_by_name = {}
        self.dcount = {}
        self.n_inst = 0

    def new_sem(self, name):
        self.sems.append(self.es.enter_context(self.nc.semaphore(name)))
        return len(self.sems) - 1

    def sb(self, name, shape, dt):
        return self.es.enter_context(self.nc.sbuf_tensor("sb_" + name, shape, dt))

    @staticmethod
    def _deps(reads, writes):
        deps = {}
        for r in reads:
            if r.w is not None and deps.get(r.w[0], 0) < r.w[1]:
                deps[r.w[0]] = r.w[1]
        for w in writes:
            if w.w is not None and deps.get(w.w[0], 0) < w.w[1]:
                deps[w.w[0]] = w.w[1]
            for s, v in w.r.items():
                if deps.get(s, 0) < v:
                    deps[s] = v
        return deps

    def _wait(self, E, deps):
        for s, v in deps.items():
            if s == E.sem and (E.name == "pe" or not SAME_ENGINE_SYNC):
                continue
            if E.waited.get(s, 0) >= v:
                continue
            E.h.wait_ge(self.sems[s], v)
            E.waited[s] = v
            self.n_inst += 1

    def op(self, en, reads, writes, fn):
        E = self.eng[en]
        self._wait(E, self._deps(reads, writes))
        last = fn(E.h)
        E.cnt += 1
        assert E.cnt < 60000
        last.then_inc(self.sems[E.sem], 1)
        ev = (E.sem, E.cnt)
        for r in reads:
            if r.r.get(E.sem, 0) < E.cnt:
                r.r[E.sem] = E.cnt
        for w in writes:
            w.w = ev
            w.r = {}
        self.n_inst += 1
        return ev

    def dma(self, reads, writes, transfers, q="sp"):
        E = self.eng[q]
        self._wait(E, self._deps(reads, writes))
        R = writes[0] if writes else reads[0]
        if R.dsem is None:
            if R.name not in self.dsem_by_name:
                self.dsem_by_name[R.name] = self.new_sem("d_" + R.name)
                self.dcount[self.dsem_by_name[R.name]] = 0
            R.dsem = self.dsem_by_name[R.name]
        for out_ap, in_ap in transfers:
            E.h.dma_start(out=out_ap, in_=in_ap).then_inc(self.sems[R.dsem], 16)
            self.dcount[R.dsem] += 16
            self.n_inst += 1
        cnt = self.dcount[R.dsem]
        assert cnt < 60000
        R.dcnt = cnt
        ev = (R.dsem, cnt)
        for r in reads:
            if r.r.get(R.dsem, 0) < cnt:
                r.r[R.dsem] = cnt
        for w in writes:
            w.w = ev
            w.r = {}
        return ev

    def barrier(self):
        tgt = {E.sem: E.cnt for E in self.eng.values() if E.cnt > 0}
        for sidx, cnt in self.dcount.items():
            if cnt > 0:
                tgt[sidx] = cnt
        for E in self.eng.values():
            d = {s: v for s, v in tgt.items() if s != E.sem}
            self._wait(E, d)


def build_program(layers, nseq=NSEQ, do_mix=True, do_ffn=True, do_ple=True, debug=False):
    nc = bass.Bass("TRN2", target_bir_lowering=False, dynamic_dma_scratch_size=2048)
    NT = nseq * SEQ

    def din(name, shape, dt=F32):
        return nc.dram_tensor(name, shape, dt, kind="ExternalInput").ap()

    xT = din("xT", [D, NT])
    pT = din("pT", [DEPTH * PLE, NT])
    pos = din("pos", [1, NT], I32)
    cp_d = din("cp", [128, NCP])
    bs_d = din("b_s", [2, 1024])
    wsT_d = din("w_sT", [2, 128, 1024])
    w_mdown = din("mla_w_down", [2, D, 704])
    w_muq = din("mla_w_uq", [2, QL, 1536])
    w_mukv = din("mla_w_ukv", [2, KVL, 2048])
    w_mout = din("mla_w_out", [2, D, D])
    w_gin = din("gmlp_w_in", [2, D, 2 * GH])
    w_gout = din("gmlp_w_out", [2, GH, D])
    w_fup = din("ffn_w_up", [DEPTH, D, DFF])
    w_fdn = din("ffn_w_down", [DEPTH, DFF, D])
    w_pg = din("ple_w_gate", [DEPTH, D, D])
    w_pp = din("ple_w_proj", [DEPTH, PLE, D])
    outT = nc.dram_tensor("outT", [D, NT], F32, kind="ExternalOutput").ap()

    es = contextlib.ExitStack()
    with es:
        k = K(nc, es)
        hT = k.sb("hT", [128, KD, SEQ], F32)
        hres = [[Res(f"h{c}_{b}") for b in range(NB)] for c in range(KD)]
        stage = [k.sb(f"stage{i}", [128, 4096], F32) for i in range(2)]
        stres = [Res(f"st{i}") for i in range(2)]
        NSLOT = 3
        wsl = [k.sb(f"wsl{i}", [128, 4096], BF16) for i in range(NSLOT)]
        wres = [Res(f"w{i}") for i in range(NSLOT)]
        NF = 6
        ft = [k.sb(f"ft{i}", [128, T], F32) for i in range(NF)]
        fres = [Res(f"ft{i}") for i in range(NF)]
        NBT = 4
        bt = [k.sb(f"bt{i}", [128, T], BF16) for i in range(NBT)]
        bres = [Res(f"bt{i}") for i in range(NBT)]
        cp = k.sb("cp", [128, NCP], F32)
        cpd = k.sb("cpd", [128, 16], F32)
        ones = k.sb("ones", [128, 128], BF16)
        onesf = k.sb("onesf", [128, 64], F32)
        fold = k.sb("fold", [128, 64], F32)
        fold2 = k.sb("fold2", [128, 64], F32)
        st1 = k.sb("st1", [128, 16], F32)
        st2 = k.sb("st2", [128, 16], F32)
        stt = k.sb("stt", [128, 32], F32)
        cres = Res("const")
        sres = Res("stats")
        ARENA = 80 * 1024 // 2
        arena = k.sb("arena", [128, ARENA], BF16)
        pbank = [es.enter_context(nc.psum_tensor(f"pb{i}", [128, T], F32)) for i in range(8)]
        pres = [Res(f"pb{i}") for i in range(8)]

        rr = {"f": 0, "b": 0, "p": 0, "st": 0, "w": 0}

        def ftmp():
            i = rr["f"] % NF
            rr["f"] += 1
            return ft[i], fres[i]

        def btmp():
            i = rr["b"] % NBT
            rr["b"] += 1
            return bt[i], bres[i]

        bankset = [list(range(8))]

        def bank():
            s = bankset[0]
            i = s[rr["p"] % len(s)]
            rr["p"] += 1
            return pbank[i], pres[i]

        def av(off, n, dt=BF16):
            if dt == BF16:
                return arena[:, off:off + n]
            return arena[:, off:off + 2 * n].bitcast(F32)

        k.dma([], [cres], [(cp[:], cp_d[:, :])])
        k.op("dve", [], [cres], lambda h: h.memset(ones[:], 1.0))
        for i_ in range(2):
            k.op("pool", [], [stres[i_]], lambda h, i_=i_: h.memset(stage[i_][:], 0.0))
        k.op("dve", [], [cres], lambda h: h.memset(onesf[:], 1.0))
        k.op("dve", [], [cres], lambda h: h.memset(cpd[:], -SM_SHIFT))
        k.op("pool", [cres], [cres], lambda h: h.affine_select(
            out=fold[:], in_=onesf[:], pattern=[[-1, 64]], compare_op=ALU.is_equal, fill=0.0, base=0, channel_multiplier=1))
        k.op("pool", [cres], [cres], lambda h: h.affine_select(
            out=fold2[:], in_=onesf[:], pattern=[[-1, 64]], compare_op=ALU.is_equal, fill=0.0, base=-64, channel_multiplier=1))
        k.op("pool", [cres], [cres], lambda h: h.tensor_tensor(out=fold[:], in0=fold[:], in1=fold2[:], op=ALU.add))
        for j in range(2):
            k.op("dve", [cres], [cres], lambda h, j=j: h.tensor_scalar(
                out=cpd[:, j:j + 1], in0=cp[:, C_QN + j:C_QN + j + 1], scalar1=SM_SCALE, scalar2=None, op0=ALU.mult))
            k.op("dve", [cres], [cres], lambda h, j=j: h.tensor_scalar(
                out=cpd[:, 2 + j:3 + j], in0=cp[:, C_QR + j:C_QR + j + 1], scalar1=SM_SCALE, scalar2=None, op0=ALU.mult))

        specs = []
        keyid = {}

        def piece(tag, parts, post=None, key=None):
            key = tag if key is None else key
            if key[0] in CACHE_KINDS:
                keyid.setdefault(key, len(keyid))
            off = 0
            tr = []
            for part in parts:
                ap_, nk, width = part[:3]
                padw = part[3] if len(part) > 3 else width
                tr.append((off, nk, width, padw, ap_))
                off += nk * padw
            specs.append((tag, key, tr, off, post))

        def ffn_specs(l):
            for fg in range(8):
                piece(("fup", l, fg), [(w_fup[l, :, fg * 512:(fg + 1) * 512].rearrange("(k p) n -> p k n", p=128), 8, 512)])
                piece(("fdn", l, fg), [(w_fdn[l, fg * 512:(fg + 1) * 512, :].rearrange("(k p) n -> p k n", p=128), 4, 1024)])

        def ple_specs(l):
            for hf in range(2):
                piece(("pg", l, hf), [(w_pg[l, :, hf * 512:(hf + 1) * 512].rearrange("(k p) n -> p k n", p=128), 8, 512)])
                piece(("pp", l, hf), [(w_pp[l, :, hf * 512:(hf + 1) * 512].rearrange("(k p) n -> p k n", p=128), 2, 512)])

        def gmlp_specs(j):
            for tb in range(NB):
                for vp in range(4):
                    piece(("gv", j, tb, vp), [(w_gin[j, :, GH + vp * 512:GH + (vp + 1) * 512].rearrange("(k p) n -> p k n", p=128), 8, 512)], key=("gv", j, vp))
                for up in range(4):
                    piece(("gu", j, tb, up), [(w_gin[j, :, up * 512:(up + 1) * 512].rearrange("(k p) n -> p k n", p=128), 8, 512)], key=("gu", j, up))
                    piece(("go", j, tb, up), [(w_gout[j, up * 512:(up + 1) * 512, :].rearrange("(k p) n -> p k n", p=128), 4, 1024)], key=("go", j, up))

        def rot_post(off, nk, padw, r0):
            def post(slot, res):
                v = slot[:, off:off + nk * padw].rearrange("p (k n) -> p k n", k=nk)
                k.op("pool", [res], [res], lambda h: h.tensor_scalar(
                    out=v[:, :, r0 + 64:r0 + 96], in0=v[:, :, r0 + 32:r0 + 64], scalar1=-1.0, scalar2=None, op0=ALU.mult))
                k.op("pool", [res], [res], lambda h: h.tensor_copy(out=v[:, :, r0 + 96:r0 + 128], in_=v[:, :, r0:r0 + 32]))
            return post

        def mla_specs(j):
            piece(("ma1", j), [(w_mdown[j, :, 0:384].rearrange("(k p) n -> p k n", p=128), 8, 384)])
            piece(("ma2", j), [(w_mdown[j, :, 384:704].rearrange("(k p) n -> p k n", p=128), 8, 320, 384)],
                  post=rot_post(0, 8, 384, 256))
            for h_ in range(H):
                piece(("mh", j, h_), [
                    (w_muq[j, :, h_ * 192:(h_ + 1) * 192].rearrange("(k p) n -> p k n", p=128), 3, 192, 256),
                    (w_mukv[j, :, h_ * 256:(h_ + 1) * 256].rearrange("(k p) n -> p k n", p=128), 2, 256),
                    (w_mout[j, h_ * 128:(h_ + 1) * 128, :].rearrange("(k p) n -> p k n", p=128), 1, 1024),
                ], post=rot_post(0, 3, 256, 128))

        for s_ in range(nseq):
            for l in layers:
                if do_mix:
                    (mla_specs if l % 2 == 0 else gmlp_specs)(l // 2)
                if do_ffn:
                    ffn_specs(l)
                if do_ple:
                    ple_specs(l)

        wst = {"emitted": 0, "next": 0}
        slots = {}

        wscr = nc.dram_tensor("wscr", [max(len(keyid), 1), 128, 4096], BF16, kind="Internal").ap()
        NG = 6
        guard = [Res(f"scrg{g}") for g in range(NG)]
        for g_ in guard:
            g_.dsem = k.new_sem("d_" + g_.name)
            k.dcount[g_.dsem] = 0
        scr_res = {}
        wcnt = {"n": 0}

        def _emit_piece(i):
            tag, key, tr, total, post = specs[i]
            wi = rr["w"] % NSLOT
            rr["w"] += 1
            kid = keyid.get(key)
            if key in scr_res:
                k.dma([scr_res[key]], [wres[wi]], [(wsl[wi][:, 0:total], wscr[kid, :, 0:total])])
            else:
                si = rr["st"] % 2
                rr["st"] += 1
                k.dma([], [stres[si]], [
                    (stage[si][:, off:off + nk * pw_].rearrange("p (k n) -> p k n", k=nk)[:, :, 0:w_], ap_)
                    for off, nk, w_, pw_, ap_ in tr])
                if key[0] in ("ma1", "ma2", "mh"):
                    k.op("pool", [stres[si]], [wres[wi]], lambda h: h.tensor_copy(out=wsl[wi][:, 0:total], in_=stage[si][:, 0:total]))
                else:
                    k.op("act", [stres[si]], [wres[wi]], lambda h: h.activation(out=wsl[wi][:, 0:total], in_=stage[si][:, 0:total], func=AF.Copy))
                if post is not None:
                    post(wsl[wi], wres[wi])
                if key[0] in CACHE_KINDS and sum(1 for sp_ in specs if sp_[1] == key) > 1:
                    g_ = guard[wcnt["n"] % NG]
                    wcnt["n"] += 1
                    r_ = Res("scr")
                    r_.dsem = g_.dsem
                    k.dma([wres[wi]], [r_, g_], [(wscr[kid, :, 0:total], wsl[wi][:, 0:total])])
                    scr_res[key] = r_
            slots[i] = (wsl[wi], wres[wi])

        PREFETCH = 1

        def wnext(tag):
            i = wst["next"]
            wst["next"] += 1
            assert specs[i][0] == tag, (specs[i][0], tag)
            while wst["emitted"] <= min(i + PREFETCH, len(specs) - 1):
                _emit_piece(wst["emitted"])
                wst["emitted"] += 1
            return slots.pop(i)

        def stage_load(transfers_fn, total, dst_ap, dst_res):
            si = rr["st"] % 2
            rr["st"] += 1
            k.dma([], [stres[si]], transfers_fn(stage[si]))
            k.op("pool", [stres[si]], [dst_res], lambda h: h.tensor_copy(out=dst_ap, in_=stage[si][:, 0:total]))

        def mm(out_ap, lhsT, rhs, start, stop):
            return nc.tensor.matmul(out_ap, lhsT=lhsT, rhs=rhs, start=start, stop=stop)

        def rms_block(srcs, sres_l, P, gcol, Dn, outs, out_res_l, gscale_ap=None):
            n = len(srcs)
            sb_, sbr = bank()
            for c in range(n):
                q_, qr = btmp()
                k.op("act", [sres_l[c]], [qr], lambda h, c=c, q_=q_: h.activation(out=q_[0:P, :], in_=srcs[c], func=AF.Square))
                k.op("pe", [qr, cres], [sbr], lambda h, c=c, q_=q_: mm(sb_[0:P, :], ones[0:P, 0:P], q_[0:P, :], c == 0, c == n - 1))
            rs, rsr = ftmp()
            k.op("act", [sbr], [rsr], lambda h: h.activation(out=rs[0:P, :], in_=sb_[0:P, :], func=AF.Ln, scale=1.0 / Dn, bias=EPS))
            k.op("act", [rsr], [rsr], lambda h: h.activation(out=rs[0:P, :], in_=rs[0:P, :], func=AF.Exp, scale=-0.5))
            for c in range(n):
                k.op("dve", [sres_l[c], rsr, cres], [out_res_l[c]], lambda h, c=c: h.scalar_tensor_tensor(
                    out=outs[c], in0=srcs[c], scalar=gcol(c), in1=rs[0:P, :], op0=ALU.mult, op1=ALU.mult))
            return rs, rsr

        def rms_multi(items, Dn):
            sq = []
            for (src, sr, gcol, out, outr) in items:
                q_, qr = btmp()
                k.op("act", [sr], [qr], lambda h, q_=q_, src=src: h.activation(out=q_[:], in_=src, func=AF.Square))
                sq.append((q_, qr))
            ss = []
            for i in range(len(items)):
                sb_, sbr = bank()
                k.op("pe", [sq[i][1], cres], [sbr], lambda h, sb_=sb_, i=i: mm(sb_[:], ones[:, :], sq[i][0][:], True, True))
                ss.append((sb_, sbr))
            rl = []
            for i in range(len(items)):
                rs, rsr = ftmp()
                k.op("act", [ss[i][1]], [rsr], lambda h, rs=rs, i=i: h.activation(out=rs[:], in_=ss[i][0][:], func=AF.Ln, scale=1.0 / Dn, bias=EPS))
                rl.append((rs, rsr))
            for rs, rsr in rl:
                k.op("act", [rsr], [rsr], lambda h, rs=rs: h.activation(out=rs[:], in_=rs[:], func=AF.Exp, scale=-0.5))
            for i, (src, sr, gcol, out, outr) in enumerate(items):
                k.op("dve", [sr, rl[i][1], cres], [outr], lambda h, i=i, src=src, gcol=gcol, out=out: h.scalar_tensor_tensor(
                    out=out, in0=src, scalar=gcol, in1=rl[i][0][:], op0=ALU.mult, op1=ALU.mult))

        def norm_h(tb, gbase, hn_view, hn_res):
            sl = slice(tb * T, (tb + 1) * T)
            rms_block([hT[:, c, sl] for c in range(KD)], [hres[c][tb] for c in range(KD)], 128,
                      lambda c: cp[:, gbase + c:gbase + c + 1], D,
                      [hn_view[:, c, :] for c in range(KD)], hn_res)

        def resid_add(c, tb, bk, bkr):
            sl = slice(tb * T, (tb + 1) * T)
            k.op("dve", [bkr, hres[c][tb]], [hres[c][tb]], lambda h: h.tensor_tensor(
                out=hT[:, c, sl], in0=bk[:], in1=hT[:, c, sl], op=ALU.add))

        def ffn_phase(l):
            k.barrier()
            hn = av(0, KD * SEQ).rearrange("p (c t) -> p c t", c=KD)
            hnr = [[Res(f"hn{b}_{c}") for c in range(KD)] for b in range(NB)]
            hid = [av(KD * SEQ + i * 2048, 2048).rearrange("p (j t) -> p j t", j=4) for i in range(2)]
            hidr = [[Res(f"hid{i}_{j}") for j in range(4)] for i in range(2)]
            for tb in range(NB):
                norm_h(tb, C_FFN + l * 8, hn[:, :, tb * T:(tb + 1) * T], hnr[tb])
            it = 0
            for fg in range(8):
                wu, wur = wnext(("fup", l, fg))
                wd, wdr = wnext(("fdn", l, fg))
                wuv = wu[:, 0:4096].rearrange("p (k n) -> p k n", k=8)
                wdv = wd[:, 0:4096].rearrange("p (k n) -> p k n", k=4)

                def up(tb, hi):
                    for j in range(4):
                        bk, bkr = bank()
                        k.op("pe", [wur] + hnr[tb], [bkr], lambda h, j=j, bk=bk: [
                            mm(bk[:], wuv[:, kk, j * 128:(j + 1) * 128], hn[:, kk, tb * T:(tb + 1) * T], kk == 0, kk == KD - 1)
                            for kk in range(KD)][-1])
                        sq, sqr = ftmp()
                        k.op("act", [bkr], [sqr], lambda h, bk=bk, sq=sq: h.activation(out=sq[:], in_=bk[:], func=AF.Square))
                        k.op("dve", [bkr, sqr], [hidr[hi][j]], lambda h, j=j, bk=bk, sq=sq: h.scalar_tensor_tensor(
                            out=hid[hi][:, j, :], in0=bk[:], scalar=0.0, in1=sq[:], op0=ALU.is_gt, op1=ALU.mult))

                def down(tb, hi):
                    for c in range(KD):
                        bk, bkr = bank()
                        k.op("pe", [wdr] + hidr[hi], [bkr], lambda h, c=c, bk=bk: [
                            mm(bk[:], wdv[:, j, c * 128:(c + 1) * 128], hid[hi][:, j, :], j == 0, j == 3)
                            for j in range(4)][-1])
                        resid_add(c, tb, bk, bkr)

                up(0, it % 2)
                for tb in range(NB):
                    if tb + 1 < NB:
                        up(tb + 1, (it + 1) % 2)
                    down(tb, it % 2)
                    it += 1

        def ple_phase(l, s_):
            k.barrier()
            hn = av(0, KD * SEQ).rearrange("p (c t) -> p c t", c=KD)
            hnr = [[Res(f"hn{b}_{c}") for c in range(KD)] for b in range(NB)]
            pb_ = av(KD * SEQ, 2 * SEQ)
            pbv = pb_.rearrange("p (c t) -> p c t", c=2)
            pbr = Res("pb")
            stage_load(lambda st: [(st[:, 0:2 * SEQ].rearrange("p (c t) -> p c t", c=2),
                                    pT[l * PLE:(l + 1) * PLE, s_ * SEQ:(s_ + 1) * SEQ].rearrange("(c p) t -> p c t", p=128))],
                       2 * SEQ, pb_, pbr)
            for tb in range(NB):
                norm_h(tb, C_PLE + l * 8, hn[:, :, tb * T:(tb + 1) * T], hnr[tb])
            for hf in range(2):
                wg, wgr = wnext(("pg", l, hf))
                wp, wpr = wnext(("pp", l, hf))
                wgv = wg[:, 0:4096].rearrange("p (k n) -> p k n", k=8)
                wpv = wp[:, 0:1024].rearrange("p (k n) -> p k n", k=2)
                for tb in range(NB):
                    sl = slice(tb * T, (tb + 1) * T)
                    for c4 in range(4):
                        c = hf * 4 + c4
                        gb, gbr = bank()
                        k.op("pe", [wgr] + hnr[tb], [gbr], lambda h, gb=gb, c4=c4: [
                            mm(gb[:], wgv[:, kk, c4 * 128:(c4 + 1) * 128], hn[:, kk, sl], kk == 0, kk == KD - 1)
                            for kk in range(KD)][-1])
                        qb_, qbr = bank()
                        k.op("pe", [wpr, pbr], [qbr], lambda h, qb_=qb_, c4=c4: [
                            mm(qb_[:], wpv[:, kk, c4 * 128:(c4 + 1) * 128], pbv[:, kk, sl], kk == 0, kk == 1)
                            for kk in range(2)][-1])
                        gs, gsr = ftmp()
                        k.op("act", [gbr], [gsr], lambda h, gb=gb, gs=gs: h.activation(out=gs[:], in_=gb[:], func=AF.Sigmoid))
                        k.op("dve", [gsr, qbr], [gsr], lambda h, qb_=qb_, gs=gs: h.tensor_tensor(
                            out=gs[:], in0=qb_[:], in1=gs[:], op=ALU.mult))
                        k.op("pool", [gsr, hres[c][tb]], [hres[c][tb]], lambda h, gs=gs, c=c: h.tensor_tensor(
                            out=hT[:, c, sl], in0=hT[:, c, sl], in1=gs[:], op=ALU.add))

        def gmlp_phase(l, s_):
            j = l // 2
            k.barrier()
            hn = av(0, KD * T).rearrange("p (c t) -> p c t", c=KD)
            hnr = [Res(f"hnb{c}") for c in range(KD)]
            vt = av(4096, 4 * GH).rearrange("p (c f) -> p c f", c=4)
            vtr = [[Res(f"vt{c}_{v}") for v in range(4)] for c in range(4)]
            cst = av(12288, 2048, F32).rearrange("p (c t) -> p c t", c=16)
            cstr = Res("cst")
            y = [av(16384 + i * 2048, 2048).rearrange("p (j t) -> p j t", j=4) for i in range(2)]
            yr = [[Res(f"y{i}_{f}") for f in range(4)] for i in range(2)]
            wsf = av(20480, 1024, F32)
            wsfr = Res("wsf")
            bsb = av(22528, 1024, F32)
            bsbr = Res("bsb")
            wsb = av(24576, 1024).rearrange("p (g t) -> p g t", g=8)
            wsbr = Res("wsb")
            k.dma([], [wsfr], [(wsf, wsT_d[j, :, :])])
            k.dma([], [bsbr], [(bsb, bs_d[j:j + 1, :].broadcast_to([128, 1024]))])
            k.op("pool", [wsfr], [wsbr], lambda h: h.affine_select(
                out=wsb, in_=wsf.rearrange("p (g t) -> p g t", g=8), pattern=[[0, 8], [1, 128]],
                compare_op=ALU.is_ge, fill=0.0, base=0, channel_multiplier=-1))
            rb = []
            for hf in range(2):
                b_, br_ = bank()
                k.op("pe", [wsbr, cres], [br_], lambda h, b_=b_, hf=hf: mm(
                    b_[:], ones[:, :], wsb[:, hf * 4:(hf + 1) * 4, :], True, True))
                rb.append((b_, br_))
            for fc in range(16):
                g = fc // 2
                b_, br_ = rb[g // 4]
                k.op("dve", [br_, bsbr, cres], [cstr], lambda h, fc=fc, g=g, b_=b_: h.scalar_tensor_tensor(
                    out=cst[:, fc, :], in0=b_[:, (g % 4) * 128:(g % 4 + 1) * 128],
                    scalar=cp[:, C_LNB + j * 16 + fc:C_LNB + j * 16 + fc + 1],
                    in1=bsb[:, g * 128:(g + 1) * 128], op0=ALU.mult, op1=ALU.add))
            yi = 0
            for tb in range(NB):
                norm_h(tb, C_MIX + l * 8, hn, hnr)
                for vp in range(4):
                    wv, wvr = wnext(("gv", j, tb, vp))
                    wvv = wv[:, 0:4096].rearrange("p (k n) -> p k n", k=8)
                    for tc in range(4):
                        bk, bkr = bank()
                        k.op("pe", [wvr] + hnr, [bkr], lambda h, bk=bk, tc=tc: [
                            mm(bk[:], hn[:, kk, tc * 128:(tc + 1) * 128], wvv[:, kk, :], kk == 0, kk == KD - 1)
                            for kk in range(KD)][-1])
                        col = tc * 4 + vp
                        k.op("act", [bkr], [vtr[tc][vp], sres], lambda h, bk=bk, tc=tc, vp=vp, col=col: h.activation(
                            out=vt[:, tc, vp * 512:(vp + 1) * 512], in_=bk[:], func=AF.Gelu_apprx_tanh,
                            accum_out=st1[:, col:col + 1]))
                        jk, jkr = btmp()
                        k.op("act", [vtr[tc][vp]], [jkr, sres], lambda h, jk=jk, tc=tc, vp=vp, col=col: h.activation(
                            out=jk[:], in_=vt[:, tc, vp * 512:(vp + 1) * 512], func=AF.Square,
                            accum_out=st2[:, col:col + 1]))
                k.op("dve", [sres], [sres], lambda h: h.tensor_reduce(
                    out=stt[:, 0:4], in_=st1[:, 0:16].rearrange("p (a b) -> p a b", a=4), axis=mybir.AxisListType.X, op=ALU.add))
                k.op("dve", [sres], [sres], lambda h: h.tensor_reduce(
                    out=stt[:, 4:8], in_=st2[:, 0:16].rearrange("p (a b) -> p a b", a=4), axis=mybir.AxisListType.X, op=ALU.add))
                k.op("dve", [sres], [sres], lambda h: h.tensor_scalar(
                    out=stt[:, 8:12], in0=stt[:, 0:4], scalar1=1.0 / GH, scalar2=None, op0=ALU.mult))
                k.op("dve", [sres], [sres], lambda h: h.tensor_tensor(
                    out=stt[:, 12:16], in0=stt[:, 8:12], in1=stt[:, 8:12], op=ALU.mult))
                k.op("dve", [sres], [sres], lambda h: h.scalar_tensor_tensor(
                    out=stt[:, 16:20], in0=stt[:, 4:8], scalar=1.0 / GH, in1=stt[:, 12:16], op0=ALU.mult, op1=ALU.subtract))
                k.op("act", [sres], [sres], lambda h: h.activation(
                    out=stt[:, 20:24], in_=stt[:, 16:20], func=AF.Sqrt, scale=1.0, bias=EPS))
                k.op("dve", [sres], [sres], lambda h: h.reciprocal(out=stt[:, 24:28], in_=stt[:, 20:24]))
                for tc in range(4):
                    k.op("dve", [sres] + vtr[tc], vtr[tc], lambda h, tc=tc: h.tensor_scalar(
                        out=vt[:, tc, :], in0=vt[:, tc, :], scalar1=stt[:, 8 + tc:9 + tc], scalar2=stt[:, 24 + tc:25 + tc],
                        op0=ALU.subtract, op1=ALU.mult))
                for up in range(4):
                    wu, wur = wnext(("gu", j, tb, up))
                    wo, wor = wnext(("go", j, tb, up))
                    wuv = wu[:, 0:4096].rearrange("p (k n) -> p k n", k=8)
                    wov = wo[:, 0:4096].rearrange("p (k n) -> p k n", k=4)
                    yy, yyr = y[yi % 2], yr[yi % 2]
                    yi += 1
                    for fl in range(4):
                        fc = up * 4 + fl
                        g = fc // 2
                        ub, ubr = bank()
                        k.op("pe", [wur] + hnr, [ubr], lambda h, ub=ub, fl=fl: [
                            mm(ub[:], wuv[:, kk, fl * 128:(fl + 1) * 128], hn[:, kk, :], kk == 0, kk == KD - 1)
                            for kk in range(KD)][-1])
                        us, usr = ftmp()
                        k.op("act", [ubr], [usr], lambda h, ub=ub, us=us: h.activation(out=us[:], in_=ub[:], func=AF.Gelu_apprx_tanh))
                        sb_, sbr = bank()
                        k.op("pe", [vtr[tc][fc // 4] for tc in range(4)] + [wsbr], [sbr], lambda h, sb_=sb_, fc=fc, g=g: [
                            mm(sb_[:, tc * 128:(tc + 1) * 128], vt[:, tc, fc * 128:(fc + 1) * 128], wsb[:, g, :], True, True)
                            for tc in range(4)][-1])
                        sv, svr = ftmp()
                        k.op("dve", [sbr, cstr, cres], [svr], lambda h, sb_=sb_, sv=sv, fc=fc: h.scalar_tensor_tensor(
                            out=sv[:].rearrange("p (a t) -> p a t", a=4), in0=sb_[:].rearrange("p (a t) -> p a t", a=4),
                            scalar=cp[:, C_LNG + j * 16 + fc:C_LNG + j * 16 + fc + 1],
                            in1=cst[:, fc:fc + 1, :].broadcast_to([128, 4, 128]), op0=ALU.mult, op1=ALU.add))
                        k.op("pool", [usr, svr], [yyr[fl]], lambda h, us=us, sv=sv, yy=yy, fl=fl: h.tensor_tensor(
                            out=yy[:, fl, :], in0=us[:], in1=sv[:], op=ALU.mult))
                    for c in range(KD):
                        bk, bkr = bank()
                        k.op("pe", [wor] + yyr, [bkr], lambda h, bk=bk, c=c, yy=yy: [
                            mm(bk[:], wov[:, fl, c * 128:(c + 1) * 128], yy[:, fl, :], fl == 0, fl == 3)
                            for fl in range(4)][-1])
                        resid_add(c, tb, bk, bkr)

        def mla_phase(l, s_):
            j = l // 2
            k.barrier()
            hn = av(0, KD * T).rearrange("p (c t) -> p c t", c=KD)
            hnr = [Res(f"hnb{c}") for c in range(KD)]
            cq = av(4096, 3 * SEQ).rearrange("p (c t) -> p c t", c=3)
            cqr = [Res(f"cq{b}") for b in range(NB)]
            ckv = av(10240, 2 * SEQ).rearrange("p (c t) -> p c t", c=2)
            ckvr = [Res(f"ckv{b}") for b in range(NB)]
            kr = av(14336, SEQ)
            krr = [Res(f"kr{b}") for b in range(NB)]
            qn = av(16384, SEQ)
            qnr = [Res(f"qn{b}") for b in range(NB)]
            qr_ = av(18432, SEQ)
            qrr = [Res(f"qr{b}") for b in range(NB)]
            kn = av(20480, SEQ)
            knr = [Res(f"kn{b}") for b in range(NB)]
            vtk = av(22528, SEQ).rearrange("p (c d) -> p c d", c=16)
            vtkr = [Res(f"vk{b}") for b in range(NB)]
            oth = av(24576, SEQ)
            othr = [Res(f"ot{b}") for b in range(NB)]
            raw = av(26624, 3 * T, F32).rearrange("p (c t) -> p c t", c=3)
            rawr = [Res(f"raw{c}") for c in range(3)]
            pi_ = av(29696, T, F32)
            pi_i = pi_.bitcast(I32)
            pir = Res("pi")
            rt = [av(30720 + i * 1024, T, F32) for i in range(2)]
            rtr = [Res(f"rt{i}") for i in range(2)]
            cst_ = av(32768, SEQ, F32)
            cstr = [Res(f"cst{b}") for b in range(NB)]
            TWO_PI = 2.0 * math.pi
            C1 = 6.28125
            C2 = TWO_PI - C1

            def rope_tables(tb):
                sl = slice(tb * T, (tb + 1) * T)
                t0 = s_ * SEQ + tb * T
                k.dma([], [pir], [(pi_i, pos[0:1, t0:t0 + T].broadcast_to([128, T]))])
                a, ar = rt[0], rtr[0]
                b_, br_ = rt[1], rtr[1]
                k.op("dve", [pir], [ar], lambda h: h.tensor_copy(out=a, in_=pi_i))
                k.op("dve", [ar, cres], [ar], lambda h: h.tensor_scalar(
                    out=a, in0=a, scalar1=cp[:, C_INVF:C_INVF + 1], scalar2=None, op0=ALU.mult))
                k.op("dve", [ar, pir], [pir], lambda h: h.tensor_scalar(
                    out=pi_i, in0=a, scalar1=1.0 / TWO_PI, scalar2=None, op0=ALU.mult))
                k.op("dve", [pir], [br_], lambda h: h.tensor_copy(out=b_, in_=pi_i))
                k.op("dve", [ar, br_], [ar], lambda h: h.scalar_tensor_tensor(
                    out=a, in0=b_, scalar=-C1, in1=a, op0=ALU.mult, op1=ALU.add))
                k.op("dve", [ar, br_], [ar], lambda h: h.scalar_tensor_tensor(
                    out=a, in0=b_, scalar=-C2, in1=a, op0=ALU.mult, op1=ALU.add))
                k.op("dve", [ar], [ar], lambda h: h.tensor_scalar(
                    out=a, in0=a, scalar1=math.pi, scalar2=-math.pi, op0=ALU.min, op1=ALU.max))
                k.op("act", [ar], [cstr[tb]], lambda h: h.activation(out=cst_[64:128, sl], in_=a[64:128, :], func=AF.Sin))
                k.op("act", [ar], [br_], lambda h: h.activation(out=b_[0:64, :], in_=a[0:64, :], func=AF.Sin, scale=0.5))
                k.op("dve", [br_], [br_], lambda h: h.tensor_tensor(out=b_[0:64, :], in0=b_[0:64, :], in1=b_[0:64, :], op=ALU.mult))
                k.op("dve", [br_], [cstr[tb]], lambda h: h.tensor_scalar(
                    out=cst_[0:64, sl], in0=b_[0:64, :], scalar1=-2.0, scalar2=1.0, op0=ALU.mult, op1=ALU.add))

            def rope_norm(xx, xxr, ggcol, tb, out_ap, out_res):
                sl = slice(tb * T, (tb + 1) * T)
                sb_, sbr = bank()
                q_, qr2 = btmp()
                k.op("act", [xxr], [qr2], lambda h: h.activation(out=q_[0:64, :], in_=xx[0:64, :], func=AF.Square))
                k.op("pe", [qr2, cres], [sbr], lambda h: mm(sb_[0:64, :], ones[0:64, 0:64], q_[0:64, :], True, True))
                rs, rsr = ftmp()
                k.op("act", [sbr], [rsr], lambda h: h.activation(out=rs[0:64, :], in_=sb_[0:64, :], func=AF.Ln, scale=1.0 / RD, bias=EPS))
                k.op("act", [rsr], [rsr], lambda h: h.activation(out=rs[0:64, :], in_=rs[0:64, :], func=AF.Exp, scale=-0.5))
                t1, t1r = ftmp()
                k.op("dve", [xxr, qr2, cstr[tb], cres], [t1r], lambda h: h.scalar_tensor_tensor(
                    out=t1[:], in0=xx[:], scalar=ggcol, in1=cst_[:, sl], op0=ALU.mult, op1=ALU.mult))
                fb, fbr = bank()
                k.op("pe", [t1r, cres], [fbr], lambda h: mm(fb[0:64, :], fold[:, :], t1[:], True, True))
                k.op("dve", [fbr, rsr], [out_res], lambda h: h.tensor_tensor(out=out_ap, in0=fb[0:64, :], in1=rs[0:64, :], op=ALU.mult))

            w1, w1r = wnext(("ma1", j))
            w2, w2r = wnext(("ma2", j))
            w1v = w1[:, 0:3072].rearrange("p (k n) -> p k n", k=8)
            w2v = w2[:, 0:3072].rearrange("p (k n) -> p k n", k=8)
            for tb in range(NB):
                sl = slice(tb * T, (tb + 1) * T)
                rope_tables(tb)
                norm_h(tb, C_MIX + l * 8, hn, hnr)
                for (wv_, wr_, nchunk, dst, dres, gb, Dn) in ((w1v, w1r, 3, cq, cqr, C_QL + j * 3, QL), (w2v, w2r, 2, ckv, ckvr, C_KVL + j * 2, KVL)):
                    for m in range(nchunk):
                        bk, bkr = bank()
                        k.op("pe", [wr_] + hnr, [bkr], lambda h, bk=bk, m=m, wv_=wv_: [
                            mm(bk[:], wv_[:, kk, m * 128:(m + 1) * 128], hn[:, kk, :], kk == 0, kk == KD - 1)
                            for kk in range(KD)][-1])
                        k.op("act", [bkr], [rawr[m]], lambda h, bk=bk, m=m: h.activation(out=raw[:, m, :], in_=bk[:], func=AF.Copy))
                    rms_block([raw[:, m, :] for m in range(nchunk)], rawr[:nchunk], 128,
                              lambda c, gb=gb: cp[:, gb + c:gb + c + 1], Dn,
                              [dst[:, m, sl] for m in range(nchunk)], [dres[tb]] * nchunk)
                xx, xxr = bank()
                k.op("pe", [w2r] + hnr, [xxr], lambda h, xx=xx: [
                    mm(xx[:], w2v[:, kk, 256:384], hn[:, kk, :], kk == 0, kk == KD - 1) for kk in range(KD)][-1])
                rope_norm(xx, xxr, cp[:, C_KR + j:C_KR + j + 1], tb, kr[0:64, sl], krr[tb])

            kn2 = [kn, av(36864, SEQ)]
            vtk2 = [vtk, av(38912, SEQ).rearrange("p (c d) -> p c d", c=16)]
            knr2 = [knr, [Res(f"knb{b}") for b in range(NB)]]
            vtkr2 = [vtkr, [Res(f"vkb{b}") for b in range(NB)]]
            bankset[0] = [0, 1, 2, 3]
            hw = {}

            def views(h_):
                wh, whr = hw[h_]
                return (wh[:, 0:768].rearrange("p (k n) -> p k n", k=3),
                        wh[:, 768:1280].rearrange("p (k n) -> p k n", k=2),
                        wh[:, 1280:2304], whr)

            def proj_qn(h_, tbs):
                wq, wkv, wo, whr = views(h_)
                items = []
                for tb in tbs:
                    sl = slice(tb * T, (tb + 1) * T)
                    bk, bkr = bank()
                    k.op("pe", [whr, cqr[tb]], [bkr], lambda h, bk=bk, sl=sl: [
                        mm(bk[:], wq[:, kk, 0:128], cq[:, kk, sl], kk == 0, kk == 2) for kk in range(3)][-1])
                    items.append((bk[:], bkr, cpd[:, j:j + 1], qn[:, sl], qnr[tb]))
                rms_multi(items, 128)

            def rope_multi(items):
                sq, ss, rl, tl, fl = [], [], [], [], []
                for (xx, xxr, gg, tb, out, outr) in items:
                    q_, qr2 = btmp()
                    k.op("act", [xxr], [qr2], lambda h, q_=q_, xx=xx: h.activation(out=q_[0:64, :], in_=xx[0:64, :], func=AF.Square))
                    sq.append((q_, qr2))
                for i in range(len(items)):
                    sb_, sbr = bank()
                    k.op("pe", [sq[i][1], cres], [sbr], lambda h, sb_=sb_, i=i: mm(sb_[0:64, :], ones[0:64, 0:64], sq[i][0][0:64, :], True, True))
                    ss.append((sb_, sbr))
                for i in range(len(items)):
                    rs, rsr = ftmp()
                    k.op("act", [ss[i][1]], [rsr], lambda h, rs=rs, i=i: h.activation(out=rs[0:64, :], in_=ss[i][0][0:64, :], func=AF.Ln, scale=1.0 / RD, bias=EPS))
                    rl.append((rs, rsr))
                for rs, rsr in rl:
                    k.op("act", [rsr], [rsr], lambda h, rs=rs: h.activation(out=rs[0:64, :], in_=rs[0:64, :], func=AF.Exp, scale=-0.5))
                for i, (xx, xxr, gg, tb, out, outr) in enumerate(items):
                    t1, t1r = ftmp()
                    k.op("dve", [xxr, sq[i][1], cstr[tb], cres], [t1r], lambda h, t1=t1, xx=xx, gg=gg, tb=tb: h.scalar_tensor_tensor(
                        out=t1[:], in0=xx[:], scalar=gg, in1=cst_[:, tb * T:(tb + 1) * T], op0=ALU.mult, op1=ALU.mult))
                    tl.append((t1, t1r))
                for i in range(len(items)):
                    fb, fbr = bank()
                    k.op("pe", [tl[i][1], cres], [fbr], lambda h, fb=fb, i=i: mm(fb[0:64, :], fold[:, :], tl[i][0][:], True, True))
                    fl.append((fb, fbr))
                for i, (xx, xxr, gg, tb, out, outr) in enumerate(items):
                    k.op("dve", [fl[i][1], rl[i][1]], [outr], lambda h, i=i, out=out: h.tensor_tensor(
                        out=out, in0=fl[i][0][0:64, :], in1=rl[i][0][0:64, :], op=ALU.mult))

            def proj_qr(h_, tbs):
                wq, wkv, wo, whr = views(h_)
                items = []
                for tb in tbs:
                    sl = slice(tb * T, (tb + 1) * T)
                    xx, xxr = bank()
                    k.op("pe", [whr, cqr[tb]], [xxr], lambda h, xx=xx, sl=sl: [
                        mm(xx[:], wq[:, kk, 128:256], cq[:, kk, sl], kk == 0, kk == 2) for kk in range(3)][-1])
                    items.append((xx, xxr, cpd[:, 2 + j:3 + j], tb, qr_[0:64, sl], qrr[tb]))
                rope_multi(items)

            def proj_q(h_, tbs):
                for i in range(0, len(tbs), 2):
                    proj_qn(h_, tbs[i:i + 2])
                for i in range(0, len(tbs), 2):
                    proj_qr(h_, tbs[i:i + 2])

            def proj_kv(h_, tbs):
                wq, wkv, wo, whr = views(h_)
                b_ = h_ % 2
                for i in range(0, len(tbs), 2):
                    items = []
                    for tb in tbs[i:i + 2]:
                        sl = slice(tb * T, (tb + 1) * T)
                        bk, bkr = bank()
                        k.op("pe", [whr, ckvr[tb]], [bkr], lambda h, bk=bk, sl=sl: [
                            mm(bk[:], wkv[:, kk, 0:128], ckv[:, kk, sl], kk == 0, kk == 1) for kk in range(2)][-1])
                        items.append((bk[:], bkr, cp[:, C_KN + j:C_KN + j + 1], kn2[b_][:, sl], knr2[b_][tb]))
                    rms_multi(items, 128)
                for tb in tbs:
                    bv, bvr = bank()
                    k.op("pe", [whr, ckvr[tb]], [bvr], lambda h, bv=bv, tb=tb: [
                        mm(bv[:, kt4 * 128:(kt4 + 1) * 128], ckv[:, kk, (tb * 4 + kt4) * 128:(tb * 4 + kt4 + 1) * 128],
                           wkv[:, kk, 128:256], kk == 0, kk == 1)
                        for kt4 in range(4) for kk in range(2)][-1])
                    k.op("act", [bvr], [vtkr2[b_][tb]], lambda h, bv=bv, tb=tb: h.activation(
                        out=vtk2[b_][:, tb * 4:(tb + 1) * 4, :], in_=bv[:].rearrange("p (a d) -> p a d", a=4), func=AF.Copy))

            def att(h_, qb):
                b_ = h_ % 2
                kn_, vtk_, knr_, vtkr_ = kn2[b_], vtk2[b_], knr2[b_], vtkr2[b_]
                ob, obr = pbank[4 + qb % 2], pres[4 + qb % 2]
                db, dbr = pbank[6 + qb % 2], pres[6 + qb % 2]
                nkt = 4 * qb + 4

                def s_exp(kt):
                    jd = kt - 4 * qb
                    c0 = max(jd, 0) * 128
                    qs = slice(qb * T + c0, (qb + 1) * T)
                    ks = slice(kt * 128, (kt + 1) * 128)
                    sbk, sbkr = bank()
                    k.op("pe", [knr_[kt // 4], qnr[qb], krr[kt // 4], qrr[qb]], [sbkr], lambda h: [
                        mm(sbk[:, c0:T], kn_[:, ks], qn[:, qs], True, False),
                        mm(sbk[:, c0:T], kr[0:64, ks], qr_[0:64, qs], False, True)][-1])
                    pt, ptr = btmp()
                    k.op("act", [sbkr, cres], [ptr], lambda h: h.activation(
                        out=pt[:, c0:T], in_=sbk[:, c0:T], func=AF.Exp, bias=cpd[:, 8:9]))
                    if jd >= 0:
                        k.op("pool", [ptr], [ptr], lambda h: h.affine_select(
                            out=pt[:, c0:c0 + 128], in_=pt[:, c0:c0 + 128], pattern=[[1, 128]],
                            compare_op=ALU.is_ge, fill=0.0, base=0, channel_multiplier=-1))
                    return pt, ptr, c0

                pend = s_exp(0)
                for kt in range(nkt):
                    nxt = s_exp(kt + 1) if kt + 1 < nkt else None
                    pt, ptr, c0 = pend
                    k.op("pe", [ptr, vtkr_[kt // 4]], [obr], lambda h, pt=pt, c0=c0, kt=kt: mm(
                        ob[:, c0:T], vtk_[:, kt, :], pt[:, c0:T], kt == 0, kt == nkt - 1))
                    k.op("pe", [ptr, cres], [dbr], lambda h, pt=pt, c0=c0, kt=kt: mm(
                        db[:, c0:T], ones[:, :], pt[:, c0:T], kt == 0, kt == nkt - 1))
                    pend = nxt
                rd, rdr = ftmp()
                k.op("dve", [dbr], [rdr], lambda h: h.reciprocal(out=rd[:], in_=db[:]))
                k.op("dve", [obr, rdr], [othr[qb]], lambda h: h.tensor_tensor(
                    out=oth[:, qb * T:(qb + 1) * T], in0=ob[:], in1=rd[:], op=ALU.mult))

            def outproj(h_):
                wq, wkv, wo, whr = views(h_)
                for tb in range(NB):
                    for c in range(KD):
                        bk, bkr = bank()
                        k.op("pe", [whr, othr[tb]], [bkr], lambda h, bk=bk, c=c, tb=tb: mm(
                            bk[:], wo[:, c * 128:(c + 1) * 128], oth[:, tb * T:(tb + 1) * T], True, True))
                        resid_add(c, tb, bk, bkr)

            hw[0] = wnext(("mh", j, 0))
            proj_kv(0, [0, 1, 2, 3])
            proj_q(0, [0, 1, 2, 3])
            for h_ in range(H):
                for qb in range(3):
                    att(h_, qb)
                if h_ + 1 < H:
                    hw[h_ + 1] = wnext(("mh", j, h_ + 1))
                    proj_kv(h_ + 1, [0, 1, 2, 3])
                    proj_q(h_ + 1, [0, 1, 2])
                att(h_, 3)
                if h_ + 1 < H:
                    proj_q(h_ + 1, [3])
                outproj(h_)
            bankset[0] = list(range(8))

        outres = Res("out")
        for s_ in range(nseq):
            k.dma([], [hres[c][b] for c in range(KD) for b in range(NB)],
                  [(hT[:, c, :], xT[c * 128:(c + 1) * 128, s_ * SEQ:(s_ + 1) * SEQ]) for c in range(KD)])
            for l in layers:
                if do_mix:
                    if l % 2 == 0:
                        mla_phase(l, s_)
                    else:
                        gmlp_phase(l, s_)
                if do_ffn:
                    ffn_phase(l)
                if do_ple:
                    ple_phase(l, s_)
            k.dma([hres[c][b] for c in range(KD) for b in range(NB)], [outres],
                  [(outT[c * 128:(c + 1) * 128, s_ * SEQ:(s_ + 1) * SEQ], hT[:, c, :]) for c in range(KD)])
        assert wst["next"] == len(specs)
        if debug:
            k.barrier()
            dres = Res("dbg")
            ad = nc.dram_tensor("arena_dump", [128, ARENA], BF16, kind="ExternalOutput").ap()
            k.dma([], [dres], [(ad[:, :], arena[:, :])])
            k.eng["sp"].h.wait_ge(k.sems[dres.dsem], dres.dcnt)
        sp = k.eng["sp"]
        sp.h.wait_ge(k.sems[outres.dsem], outres.dcnt)
        print(f"[build] layers={layers} nseq={nseq} ops: " + " ".join(f"{n}={e.cnt}" for n, e in k.eng.items())
              + f" sems={len(k.sems)} n_inst~{k.n_inst}")
    return nc


def _pack_consts(inp):
    cp = np.zeros((128, NCP), np.float32)

    def chunks(v, n):
        return np.ascontiguousarray(v.reshape(n, 128).T)

    for l in range(DEPTH):
        cp[:, C_MIX + l * 8:C_MIX + l * 8 + 8] = chunks(inp["norm_mix"][l], 8)
        cp[:, C_FFN + l * 8:C_FFN + l * 8 + 8] = chunks(inp["norm_ffn"][l], 8)
        cp[:, C_PLE + l * 8:C_PLE + l * 8 + 8] = chunks(inp["norm_ple"][l], 8)
    perm = (np.arange(64) + 32) % 64
    for j in range(2):
        cp[:, C_QL + j * 3:C_QL + j * 3 + 3] = chunks(inp["mla_q_lora_g"][j], 3)
        cp[:, C_KVL + j * 2:C_KVL + j * 2 + 2] = chunks(inp["mla_kv_lora_g"][j], 2)
        cp[:, C_QN + j] = inp["mla_q_nope_g"][j]
        cp[:, C_KN + j] = inp["mla_k_nope_g"][j]
        cp[:64, C_QR + j] = inp["mla_q_rope_g"][j]
        cp[64:, C_QR + j] = inp["mla_q_rope_g"][j][perm]
        cp[:64, C_KR + j] = inp["mla_k_rope_g"][j]
        cp[64:, C_KR + j] = inp["mla_k_rope_g"][j][perm]
        cp[:, C_LNG + j * 16:C_LNG + j * 16 + 16] = chunks(inp["gmlp_ln_g"][j], 16)
        cp[:, C_LNB + j * 16:C_LNB + j * 16 + 16] = chunks(inp["gmlp_ln_b"][j], 16)
    invf = (10000.0 ** (-(np.arange(0, 64, 2, dtype=np.float32) / np.float32(64)))).astype(np.float32)
    cp[:, C_INVF] = np.concatenate([invf, invf, invf, invf])
    return cp


_PROG_CACHE = {}


def _get_prog(layers):
    key = tuple(layers)
    if key not in _PROG_CACHE:
        _PROG_CACHE[key] = build_program(list(layers))
    return _PROG_CACHE[key]


def kernel(**inp):
    inp = {k_: np.asarray(v) for k_, v in inp.items()}
    x = inp["x"].astype(np.float32, copy=False)
    p = inp["p"].astype(np.float32, copy=False)
    positions = inp["positions"].astype(np.int32, copy=False)
    cp = _pack_consts(inp)
    shared = {
        "cp": cp,
        "b_s": np.ascontiguousarray(inp["gmlp_b_s"].reshape(2, 1024)),
        "w_sT": np.ascontiguousarray(inp["gmlp_w_s"].transpose(0, 3, 1, 2).reshape(2, 128, 1024)),
    }
    for n in ("mla_w_down", "mla_w_uq", "mla_w_ukv", "mla_w_out", "gmlp_w_in", "gmlp_w_out",
              "ffn_w_up", "ffn_w_down", "ple_w_gate", "ple_w_proj"):
        shared[n] = np.ascontiguousarray(inp[n], dtype=np.float32)
    per_core = []
    for c in range(N_CORES):
        b0 = c * NSEQ
        xs = x[b0:b0 + NSEQ].reshape(NSEQ * SEQ, D)
        ps = p[:, b0:b0 + NSEQ].reshape(DEPTH, NSEQ * SEQ, PLE)
        per_core.append({
            "xT": np.ascontiguousarray(xs.T),
            "pT": np.ascontiguousarray(ps.transpose(0, 2, 1).reshape(DEPTH * PLE, NSEQ * SEQ)),
            "pos": np.ascontiguousarray(positions[b0:b0 + NSEQ].reshape(1, NSEQ * SEQ)),
        })
    launches = [list(range(DEPTH))] if FUSED else [[l] for l in range(DEPTH)]
    for layers in launches:
        nc = _get_prog(layers)
        in_maps = [dict(shared, **pc) for pc in per_core]
        res = run_bass_kernel_spmd(nc, in_maps, core_ids=list(range(N_CORES)))
        for c in range(N_CORES):
            per_core[c]["xT"] = np.ascontiguousarray(res.results[c]["outT"])
    out = np.empty((BATCH, SEQ, D), np.float32)
    for c in range(N_CORES):
        out[c * NSEQ:(c + 1) * NSEQ] = per_core[c]["xT"].T.reshape(NSEQ, SEQ, D)
    return out
```

```python
import contextlib
import math

import numpy as np
import concourse.bass as bass
import concourse.mybir as mybir
from concourse.bass_utils import run_bass_kernel_spmd

F32 = mybir.dt.float32
BF16 = mybir.dt.bfloat16
I32 = mybir.dt.int32
AF = mybir.ActivationFunctionType
ALU = mybir.AluOpType

N_CORES = 8
D = 1024
SEQ = 2048
BATCH = 16
DEPTH = 4
T = 512
NB = SEQ // T
KD = D // 128
NSEQ = BATCH // N_CORES
H = 8
QL, KVL, RD = 384, 256, 64
DFF = 4096
GH = 2048
PLE = 256
EPS = 1e-6
SM_SCALE = (128 + 64) ** -0.5
SM_SHIFT = 16.0

SAME_ENGINE_SYNC = True
CACHE_KINDS = ("gv", "gu", "go")
FUSED = True

C_MIX, C_FFN, C_PLE = 0, 32, 64
C_QL, C_KVL = 96, 102
C_QN, C_KN = 106, 108
C_QR, C_KR = 110, 114
C_LNG, C_LNB = 118, 150
C_INVF = 182
NCP = 184


class Res:
    __slots__ = ("name", "w", "r", "dsem", "dcnt")

    def __init__(self, name):
        self.name = name
        self.w = None
        self.r = {}
        self.dsem = None
        self.dcnt = 0


class Eng:
    def __init__(self, name, h, sem):
        self.name, self.h, self.sem = name, h, sem
        self.cnt = 0
        self.waited = {}


class K:
    def __init__(self, nc, es):
        self.nc, self.es = nc, es
        self.sems = []
        self.eng = {}
        for n, h in (("pe", nc.tensor), ("act", nc.scalar), ("dve", nc.vector),
                     ("pool", nc.gpsimd), ("sp", nc.sync)):
            self.eng[n] = Eng(n, h, self.new_sem("e_" + n))
        self.dsem_by_name = {}
        self.dcount = {}
        self.n_inst = 0

    def new_sem(self, name):
        self.sems.append(self.es.enter_context(self.nc.semaphore(name)))
        return len(self.sems) - 1

    def sb(self, name, shape, dt):
        return self.es.enter_context(self.nc.sbuf_tensor("sb_" + name, shape, dt))

    @staticmethod
    def _deps(reads, writes):
        deps = {}
        for r in reads:
            if r.w is not None and deps.get(r.w[0], 0) < r.w[1]:
                deps[r.w[0]] = r.w[1]
        for w in writes:
            if w.w is not None and deps.get(w.w[0], 0) < w.w[1]:
                deps[w.w[0]] = w.w[1]
            for s, v in w.r.items():
                if deps.get(s, 0) < v:
                    deps[s] = v
        return deps

    def _wait(self, E, deps):
        for s, v in deps.items():
            if s == E.sem and (E.name == "pe" or not SAME_ENGINE_SYNC):
                continue
            if E.waited.get(s, 0) >= v:
                continue
            E.h.wait_ge(self.sems[s], v)
            E.waited[s] = v
            self.n_inst += 1

    def op(self, en, reads, writes, fn):
        E = self.eng[en]
        self._wait(E, self._deps(reads, writes))
        last = fn(E.h)
        E.cnt += 1
        assert E.cnt < 60000
        last.then_inc(self.sems[E.sem], 1)
        ev = (E.sem, E.cnt)
        for r in reads:
            if r.r.get(E.sem, 0) < E.cnt:
                r.r[E.sem] = E.cnt
        for w in writes:
            w.w = ev
            w.r = {}
        self.n_inst += 1
        return ev

    def dma(self, reads, writes, transfers, q="sp"):
        E = self.eng[q]
        self._wait(E, self._deps(reads, writes))
        R = writes[0] if writes else reads[0]
        if R.dsem is None:
            if R.name not in self.dsem_by_name:
                self.dsem_by_name[R.name] = self.new_sem("d_" + R.name)
                self.dcount[self.dsem_by_name[R.name]] = 0
            R.dsem = self.dsem_by_name[R.name]
        for out_ap, in_ap in transfers:
            E.h.dma_start(out=out_ap, in_=in_ap).then_inc(self.sems[R.dsem], 16)
            self.dcount[R.dsem] += 16
            self.n_inst += 1
        cnt = self.dcount[R.dsem]
        assert cnt < 60000
        R.dcnt = cnt
        ev = (R.dsem, cnt)
        for r in reads:
            if r.r.get(R.dsem, 0) < cnt:
                r.r[R.dsem] = cnt
        for w in writes:
            w.w = ev
            w.r = {}
        return ev

    def barrier(self):
        tgt = {E.sem: E.cnt for E in self.eng.values() if E.cnt > 0}
        for sidx, cnt in self.dcount.items():
            if cnt > 0:
                tgt[sidx] = cnt
        for E in self.eng.values():
            d = {s: v for s, v in tgt.items() if s != E.sem}
            self._wait(E, d)


def build_program(layers, nseq=NSEQ, do_mix=True, do_ffn=True, do_ple=True, debug=False):
    nc = bass.Bass("TRN2", target_bir_lowering=False, dynamic_dma_scratch_size=2048)
    NT = nseq * SEQ

    def din(name, shape, dt=F32):
        return nc.dram_tensor(name, shape, dt, kind="ExternalInput").ap()

    xT = din("xT", [D, NT])
    pT = din("pT", [DEPTH * PLE, NT])
    pos = din("pos", [1, NT], I32)
    cp_d = din("cp", [128, NCP])
    bs_d = din("b_s", [2, 1024])
    wsT_d = din("w_sT", [2, 128, 1024])
    w_mdown = din("mla_w_down", [2, D, 704])
    w_muq = din("mla_w_uq", [2, QL, 1536])
    w_mukv = din("mla_w_ukv", [2, KVL, 2048])
    w_mout = din("mla_w_out", [2, D, D])
    w_gin = din("gmlp_w_in", [2, D, 2 * GH])
    w_gout = din("gmlp_w_out", [2, GH, D])
    w_fup = din("ffn_w_up", [DEPTH, D, DFF])
    w_fdn = din("ffn_w_down", [DEPTH, DFF, D])
    w_pg = din("ple_w_gate", [DEPTH, D, D])
    w_pp = din("ple_w_proj", [DEPTH, PLE, D])
    outT = nc.dram_tensor("outT", [D, NT], F32, kind="ExternalOutput").ap()

    es = contextlib.ExitStack()
    with es:
        k = K(nc, es)
        hT = k.sb("hT", [128, KD, SEQ], F32)
        hres = [[Res(f"h{c}_{b}") for b in range(NB)] for c in range(KD)]
        stage = [k.sb(f"stage{i}", [128, 4096], F32) for i in range(2)]
        stres = [Res(f"st{i}") for i in range(2)]
        NSLOT = 3
        wsl = [k.sb(f"wsl{i}", [128, 4096], BF16) for i in range(NSLOT)]
        wres = [Res(f"w{i}") for i in range(NSLOT)]
        NF = 6
        ft = [k.sb(f"ft{i}", [128, T], F32) for i in range(NF)]
        fres = [Res(f"ft{i}") for i in range(NF)]
        NBT = 4
        bt = [k.sb(f"bt{i}", [128, T], BF16) for i in range(NBT)]
        bres = [Res(f"bt{i}") for i in range(NBT)]
        cp = k.sb("cp", [128, NCP], F32)
        cpd = k.sb("cpd", [128, 16], F32)
        ones = k.sb("ones", [128, 128], BF16)
        onesf = k.sb("onesf", [128, 64], F32)
        fold = k.sb("fold", [128, 64], F32)
        fold2 = k.sb("fold2", [128, 64], F32)
        st1 = k.sb("st1", [128, 16], F32)
        st2 = k.sb("st2", [128, 16], F32)
        stt = k.sb("stt", [128, 32], F32)
        cres = Res("const")
        sres = Res("stats")
        ARENA = 80 * 1024 // 2
        arena = k.sb("arena", [128, ARENA], BF16)
        pbank = [es.enter_context(nc.psum_tensor(f"pb{i}", [128, T], F32)) for i in range(8)]
        pres = [Res(f"pb{i}") for i in range(8)]

        rr = {"f": 0, "b": 0, "p": 0, "st": 0, "w": 0}

        def ftmp():
            i = rr["f"] % NF
            rr["f"] += 1
            return ft[i], fres[i]

        def btmp():
            i = rr["b"] % NBT
            rr["b"] += 1
            return bt[i], bres[i]

        bankset = [list(range(8))]

        def bank():
            s = bankset[0]
            i = s[rr["p"] % len(s)]
            rr["p"] += 1
            return pbank[i], pres[i]

        def av(off, n, dt=BF16):
            if dt == BF16:
                return arena[:, off:off + n]
            return arena[:, off:off + 2 * n].bitcast(F32)

        k.dma([], [cres], [(cp[:], cp_d[:, :])])
        k.op("dve", [], [cres], lambda h: h.memset(ones[:], 1.0))
        for i_ in range(2):
            k.op("pool", [], [stres[i_]], lambda h, i_=i_: h.memset(stage[i_][:], 0.0))
        k.op("dve", [], [cres], lambda h: h.memset(onesf[:], 1.0))
        k.op("dve", [], [cres], lambda h: h.memset(cpd[:], -SM_SHIFT))
        k.op("pool", [cres], [cres], lambda h: h.affine_select(
            out=fold[:], in_=onesf[:], pattern=[[-1, 64]], compare_op=ALU.is_equal, fill=0.0, base=0, channel_multiplier=1))
        k.op("pool", [cres], [cres], lambda h: h.affine_select(
            out=fold2[:], in_=onesf[:], pattern=[[-1, 64]], compare_op=ALU.is_equal, fill=0.0, base=-64, channel_multiplier=1))
        k.op("pool", [cres], [cres], lambda h: h.tensor_tensor(out=fold[:], in0=fold[:], in1=fold2[:], op=ALU.add))
        for j in range(2):
            k.op("dve", [cres], [cres], lambda h, j=j: h.tensor_scalar(
                out=cpd[:, j:j + 1], in0=cp[:, C_QN + j:C_QN + j + 1], scalar1=SM_SCALE, scalar2=None, op0=ALU.mult))
            k.op("dve", [cres], [cres], lambda h, j=j: h.tensor_scalar(
                out=cpd[:, 2 + j:3 + j], in0=cp[:, C_QR + j:C_QR + j + 1], scalar1=SM_SCALE, scalar2=None, op0=ALU.mult))

        specs = []
        keyid = {}

        def piece(tag, parts, post=None, key=None):
            key = tag if key is None else key
            if key[0] in CACHE_KINDS:
                keyid.setdefault(key, len(keyid))
            off = 0
            tr = []
            for part in parts:
                ap_, nk, width = part[:3]
                padw = part[3] if len(part) > 3 else width
                tr.append((off, nk, width, padw, ap_))
                off += nk * padw
            specs.append((tag, key, tr, off, post))

        def ffn_specs(l):
            for fg in range(8):
                piece(("fup", l, fg), [(w_fup[l, :, fg * 512:(fg + 1) * 512].rearrange("(k p) n -> p k n", p=128), 8, 512)])
                piece(("fdn", l, fg), [(w_fdn[l, fg * 512:(fg + 1) * 512, :].rearrange("(k p) n -> p k n", p=128), 4, 1024)])

        def ple_specs(l):
            for hf in range(2):
                piece(("pg", l, hf), [(w_pg[l, :, hf * 512:(hf + 1) * 512].rearrange("(k p) n -> p k n", p=128), 8, 512)])
                piece(("pp", l, hf), [(w_pp[l, :, hf * 512:(hf + 1) * 512].rearrange("(k p) n -> p k n", p=128), 2, 512)])

        def gmlp_specs(j):
            for tb in range(NB):
                for vp in range(4):
                    piece(("gv", j, tb, vp), [(w_gin[j, :, GH + vp * 512:GH + (vp + 1) * 512].rearrange("(k p) n -> p k n", p=128), 8, 512)], key=("gv", j, vp))
                for up in range(4):
                    piece(("gu", j, tb, up), [(w_gin[j, :, up * 512:(up + 1) * 512].rearrange("(k p) n -> p k n", p=128), 8, 512)], key=("gu", j, up))
                    piece(("go", j, tb, up), [(w_gout[j, up * 512:(up + 1) * 512, :].rearrange("(k p) n -> p k n", p=128), 4, 1024)], key=("go", j, up))

        def rot_post(off, nk, padw, r0):
            def post(slot, res):
                v = slot[:, off:off + nk * padw].rearrange("p (k n) -> p k n", k=nk)
                k.op("pool", [res], [res], lambda h: h.tensor_scalar(
                    out=v[:, :, r0 + 64:r0 + 96], in0=v[:, :, r0 + 32:r0 + 64], scalar1=-1.0, scalar2=None, op0=ALU.mult))
                k.op("pool", [res], [res], lambda h: h.tensor_copy(out=v[:, :, r0 + 96:r0 + 128], in_=v[:, :, r0:r0 + 32]))
            return post

        def mla_specs(j):
            piece(("ma1", j), [(w_mdown[j, :, 0:384].rearrange("(k p) n -> p k n", p=128), 8, 384)])
            piece(("ma2", j), [(w_mdown[j, :, 384:704].rearrange("(k p) n -> p k n", p=128), 8, 320, 384)],
                  post=rot_post(0, 8, 384, 256))
            for h_ in range(H):
                piece(("mh", j, h_), [
                    (w_muq[j, :, h_ * 192:(h_ + 1) * 192].rearrange("(k p) n -> p k n", p=128), 3, 192, 256),
                    (w_mukv[j, :, h_ * 256:(h_ + 1) * 256].rearrange("(k p) n -> p k n", p=128), 2, 256),
                    (w_mout[j, h_ * 128:(h_ + 1) * 128, :].rearrange("(k p) n -> p k n", p=128), 1, 1024),
                ], post=rot_post(0, 3, 256, 128))

        for s_ in range(nseq):
            for l in layers:
                if do_mix:
                    (mla_specs if l % 2 == 0 else gmlp_specs)(l // 2)
                if do_ffn:
                    ffn_specs(l)
                if do_ple:
                    ple_specs(l)

        wst = {"emitted": 0, "next": 0}
        slots = {}

        wscr = nc.dram_tensor("wscr", [max(len(keyid), 1), 128, 4096], BF16, kind="Internal").ap()
        NG = 6
        guard = [Res(f"scrg{g}") for g in range(NG)]
        for g_ in guard:
            g_.dsem = k.new_sem("d_" + g_.name)
            k.dcount[g_.dsem] = 0
        scr_res = {}
        wcnt = {"n": 0}

        def _emit_piece(i):
            tag, key, tr, total, post = specs[i]
            wi = rr["w"] % NSLOT
            rr["w"] += 1
            kid = keyid.get(key)
            if key in scr_res:
                k.dma([scr_res[key]], [wres[wi]], [(wsl[wi][:, 0:total], wscr[kid, :, 0:total])])
            else:
                si = rr["st"] % 2
                rr["st"] += 1
                k.dma([], [stres[si]], [
                    (stage[si][:, off:off + nk * pw_].rearrange("p (k n) -> p k n", k=nk)[:, :, 0:w_], ap_)
                    for off, nk, w_, pw_, ap_ in tr])
                if key[0] in ("ma1", "ma2", "mh"):
                    k.op("pool", [stres[si]], [wres[wi]], lambda h: h.tensor_copy(out=wsl[wi][:, 0:total], in_=stage[si][:, 0:total]))
                else:
                    k.op("act", [stres[si]], [wres[wi]], lambda h: h.activation(out=wsl[wi][:, 0:total], in_=stage[si][:, 0:total], func=AF.Copy))
                if post is not None:
                    post(wsl[wi], wres[wi])
                if key[0] in CACHE_KINDS and sum(1 for sp_ in specs if sp_[1] == key) > 1:
                    g_ = guard[wcnt["n"] % NG]
                    wcnt["n"] += 1
                    r_ = Res("scr")
                    r_.dsem = g_.dsem
                    k.dma([wres[wi]], [r_, g_], [(wscr[kid, :, 0:total], wsl[wi][:, 0:total])])
                    scr_res[key] = r_
            slots[i] = (wsl[wi], wres[wi])

        PREFETCH = 1

        def wnext(tag):
            i = wst["next"]
            wst["next"] += 1
            assert specs[i][0] == tag, (specs[i][0], tag)
            while wst["emitted"] <= min(i + PREFETCH, len(specs) - 1):
                _emit_piece(wst["emitted"])
                wst["emitted"] += 1
            return slots.pop(i)

        def stage_load(transfers_fn, total, dst_ap, dst_res):
            si = rr["st"] % 2
            rr["st"] += 1
            k.dma([], [stres[si]], transfers_fn(stage[si]))
            k.op("pool", [stres[si]], [dst_res], lambda h: h.tensor_copy(out=dst_ap, in_=stage[si][:, 0:total]))

        def mm(out_ap, lhsT, rhs, start, stop):
            return nc.tensor.matmul(out_ap, lhsT=lhsT, rhs=rhs, start=start, stop=stop)

        def rms_block(srcs, sres_l, P, gcol, Dn, outs, out_res_l, gscale_ap=None):
            n = len(srcs)
            sb_, sbr = bank()
            for c in range(n):
                q_, qr = btmp()
                k.op("act", [sres_l[c]], [qr], lambda h, c=c, q_=q_: h.activation(out=q_[0:P, :], in_=srcs[c], func=AF.Square))
                k.op("pe", [qr, cres], [sbr], lambda h, c=c, q_=q_: mm(sb_[0:P, :], ones[0:P, 0:P], q_[0:P, :], c == 0, c == n - 1))
            rs, rsr = ftmp()
            k.op("act", [sbr], [rsr], lambda h: h.activation(out=rs[0:P, :], in_=sb_[0:P, :], func=AF.Ln, scale=1.0 / Dn, bias=EPS))
            k.op("act", [rsr], [rsr], lambda h: h.activation(out=rs[0:P, :], in_=rs[0:P, :], func=AF.Exp, scale=-0.5))
            for c in range(n):
                k.op("dve", [sres_l[c], rsr, cres], [out_res_l[c]], lambda h, c=c: h.scalar_tensor_tensor(
                    out=outs[c], in0=srcs[c], scalar=gcol(c), in1=rs[0:P, :], op0=ALU.mult, op1=ALU.mult))
            return rs, rsr

        def rms_multi(items, Dn):
            sq = []
            for (src, sr, gcol, out, outr) in items:
                q_, qr = btmp()
                k.op("act", [sr], [qr], lambda h, q_=q_, src=src: h.activation(out=q_[:], in_=src, func=AF.Square))
                sq.append((q_, qr))
            ss = []
            for i in range(len(items)):
                sb_, sbr = bank()
                k.op("pe", [sq[i][1], cres], [sbr], lambda h, sb_=sb_, i=i: mm(sb_[:], ones[:, :], sq[i][0][:], True, True))
                ss.append((sb_, sbr))
            rl = []
            for i in range(len(items)):
                rs, rsr = ftmp()
                k.op("act", [ss[i][1]], [rsr], lambda h, rs=rs, i=i: h.activation(out=rs[:], in_=ss[i][0][:], func=AF.Ln, scale=1.0 / Dn, bias=EPS))
                rl.append((rs, rsr))
            for rs, rsr in rl:
                k.op("act", [rsr], [rsr], lambda h, rs=rs: h.activation(out=rs[:], in_=rs[:], func=AF.Exp, scale=-0.5))
            for i, (src, sr, gcol, out, outr) in enumerate(items):
                k.op("dve", [sr, rl[i][1], cres], [outr], lambda h, i=i, src=src, gcol=gcol, out=out: h.scalar_tensor_tensor(
                    out=out, in0=src, scalar=gcol, in1=rl[i][0][:], op0=ALU.mult, op1=ALU.mult))

        def norm_h(tb, gbase, hn_view, hn_res):
            sl = slice(tb * T, (tb + 1) * T)
            rms_block([hT[:, c, sl] for c in range(KD)], [hres[c][tb] for c in range(KD)], 128,
                      lambda c: cp[:, gbase + c:gbase + c + 1], D,
                      [hn_view[:, c, :] for c in range(KD)], hn_res)

        def resid_add(c, tb, bk, bkr):
            sl = slice(tb * T, (tb + 1) * T)
            k.op("dve", [bkr, hres[c][tb]], [hres[c][tb]], lambda h: h.tensor_tensor(
                out=hT[:, c, sl], in0=bk[:], in1=hT[:, c, sl], op=ALU.add))

        def ffn_phase(l):
            k.barrier()
            hn = av(0, KD * SEQ).rearrange("p (c t) -> p c t", c=KD)
            hnr = [[Res(f"hn{b}_{c}") for c in range(KD)] for b in range(NB)]
            hid = [av(KD * SEQ + i * 2048, 2048).rearrange("p (j t) -> p j t", j=4) for i in range(2)]
            hidr = [[Res(f"hid{i}_{j}") for j in range(4)] for i in range(2)]
            for tb in range(NB):
                norm_h(tb, C_FFN + l * 8, hn[:, :, tb * T:(tb + 1) * T], hnr[tb])
            it = 0
            for fg in range(8):
                wu, wur = wnext(("fup", l, fg))
                wd, wdr = wnext(("fdn", l, fg))
                wuv = wu[:, 0:4096].rearrange("p (k n) -> p k n", k=8)
                wdv = wd[:, 0:4096].rearrange("p (k n) -> p k n", k=4)

                def up(tb, hi):
                    for j in range(4):
                        bk, bkr = bank()
                        k.op("pe", [wur] + hnr[tb], [bkr], lambda h, j=j, bk=bk: [
                            mm(bk[:], wuv[:, kk, j * 128:(j + 1) * 128], hn[:, kk, tb * T:(tb + 1) * T], kk == 0, kk == KD - 1)
                            for kk in range(KD)][-1])
                        sq, sqr = ftmp()
                        k.op("act", [bkr], [sqr], lambda h, bk=bk, sq=sq: h.activation(out=sq[:], in_=bk[:], func=AF.Square))
                        k.op("dve", [bkr, sqr], [hidr[hi][j]], lambda h, j=j, bk=bk, sq=sq: h.scalar_tensor_tensor(
                            out=hid[hi][:, j, :], in0=bk[:], scalar=0.0, in1=sq[:], op0=ALU.is_gt, op1=ALU.mult))

                def down(tb, hi):
                    for c in range(KD):
                        bk, bkr = bank()
                        k.op("pe", [wdr] + hidr[hi], [bkr], lambda h, c=c, bk=bk: [
                            mm(bk[:], wdv[:, j, c * 128:(c + 1) * 128], hid[hi][:, j, :], j == 0, j == 3)
                            for j in range(4)][-1])
                        resid_add(c, tb, bk, bkr)

                up(0, it % 2)
                for tb in range(NB):
                    if tb + 1 < NB:
                        up(tb + 1, (it + 1) % 2)
                    down(tb, it % 2)
                    it += 1

        def ple_phase(l, s_):
            k.barrier()
            hn = av(0, KD * SEQ).rearrange("p (c t) -> p c t", c=KD)
            hnr = [[Res(f"hn{b}_{c}") for c in range(KD)] for b in range(NB)]
            pb_ = av(KD * SEQ, 2 * SEQ)
            pbv = pb_.rearrange("p (c t) -> p c t", c=2)
            pbr = Res("pb")
            stage_load(lambda st: [(st[:, 0:2 * SEQ].rearrange("p (c t) -> p c t", c=2),
                                    pT[l * PLE:(l + 1) * PLE, s_ * SEQ:(s_ + 1) * SEQ].rearrange("(c p) t -> p c t", p=128))],
                       2 * SEQ, pb_, pbr)
            for tb in range(NB):
                norm_h(tb, C_PLE + l * 8, hn[:, :, tb * T:(tb + 1) * T], hnr[tb])
            for hf in range(2):
                wg, wgr = wnext(("pg", l, hf))
                wp, wpr = wnext(("pp", l, hf))
                wgv = wg[:, 0:4096].rearrange("p (k n) -> p k n", k=8)
                wpv = wp[:, 0:1024].rearrange("p (k n) -> p k n", k=2)
                for tb in range(NB):
                    sl = slice(tb * T, (tb + 1) * T)
                    for c4 in range(4):
                        c = hf * 4 + c4
                        gb, gbr = bank()
                        k.op("pe", [wgr] + hnr[tb], [gbr], lambda h, gb=gb, c4=c4: [
                            mm(gb[:], wgv[:, kk, c4 * 128:(c4 + 1) * 128], hn[:, kk, sl], kk == 0, kk == KD - 1)
                            for kk in range(KD)][-1])
                        qb_, qbr = bank()
                        k.op("pe", [wpr, pbr], [qbr], lambda h, qb_=qb_, c4=c4: [
                            mm(qb_[:], wpv[:, kk, c4 * 128:(c4 + 1) * 128], pbv[:, kk, sl], kk == 0, kk == 1)
                            for kk in range(2)][-1])
                        gs, gsr = ftmp()
                        k.op("act", [gbr], [gsr], lambda h, gb=gb, gs=gs: h.activation(out=gs[:], in_=gb[:], func=AF.Sigmoid))
                        k.op("dve", [gsr, qbr], [gsr], lambda h, qb_=qb_, gs=gs: h.tensor_tensor(
                            out=gs[:], in0=qb_[:], in1=gs[:], op=ALU.mult))
                        k.op("pool", [gsr, hres[c][tb]], [hres[c][tb]], lambda h, gs=gs, c=c: h.tensor_tensor(
                            out=hT[:, c, sl], in0=hT[:, c, sl], in1=gs[:], op=ALU.add))

        def gmlp_phase(l, s_):
            j = l // 2
            k.barrier()
            hn = av(0, KD * T).rearrange("p (c t) -> p c t", c=KD)
            hnr = [Res(f"hnb{c}") for c in range(KD)]
            vt = av(4096, 4 * GH).rearrange("p (c f) -> p c f", c=4)
            vtr = [[Res(f"vt{c}_{v}") for v in range(4)] for c in range(4)]
            cst = av(12288, 2048, F32).rearrange("p (c t) -> p c t", c=16)
            cstr = Res("cst")
            y = [av(16384 + i * 2048, 2048).rearrange("p (j t) -> p j t", j=4) for i in range(2)]
            yr = [[Res(f"y{i}_{f}") for f in range(4)] for i in range(2)]
            wsf = av(20480, 1024, F32)
            wsfr = Res("wsf")
            bsb = av(22528, 1024, F32)
            bsbr = Res("bsb")
            wsb = av(24576, 1024).rearrange("p (g t) -> p g t", g=8)
            wsbr = Res("wsb")
            k.dma([], [wsfr], [(wsf, wsT_d[j, :, :])])
            k.dma([], [bsbr], [(bsb, bs_d[j:j + 1, :].broadcast_to([128, 1024]))])
            k.op("pool", [wsfr], [wsbr], lambda h: h.affine_select(
                out=wsb, in_=wsf.rearrange("p (g t) -> p g t", g=8), pattern=[[0, 8], [1, 128]],
                compare_op=ALU.is_ge, fill=0.0, base=0, channel_multiplier=-1))
            rb = []
            for hf in range(2):
                b_, br_ = bank()
                k.op("pe", [wsbr, cres], [br_], lambda h, b_=b_, hf=hf: mm(
                    b_[:], ones[:, :], wsb[:, hf * 4:(hf + 1) * 4, :], True, True))
                rb.append((b_, br_))
            for fc in range(16):
                g = fc // 2
                b_, br_ = rb[g // 4]
                k.op("dve", [br_, bsbr, cres], [cstr], lambda h, fc=fc, g=g, b_=b_: h.scalar_tensor_tensor(
                    out=cst[:, fc, :], in0=b_[:, (g % 4) * 128:(g % 4 + 1) * 128],
                    scalar=cp[:, C_LNB + j * 16 + fc:C_LNB + j * 16 + fc + 1],
                    in1=bsb[:, g * 128:(g + 1) * 128], op0=ALU.mult, op1=ALU.add))
            yi = 0
            for tb in range(NB):
                norm_h(tb, C_MIX + l * 8, hn, hnr)
                for vp in range(4):
                    wv, wvr = wnext(("gv", j, tb, vp))
                    wvv = wv[:, 0:4096].rearrange("p (k n) -> p k n", k=8)
                    for tc in range(4):
                        bk, bkr = bank()
                        k.op("pe", [wvr] + hnr, [bkr], lambda h, bk=bk, tc=tc: [
                            mm(bk[:], hn[:, kk, tc * 128:(tc + 1) * 128], wvv[:, kk, :], kk == 0, kk == KD - 1)
                            for kk in range(KD)][-1])
                        col = tc * 4 + vp
                        k.op("act", [bkr], [vtr[tc][vp], sres], lambda h, bk=bk, tc=tc, vp=vp, col=col: h.activation(
                            out=vt[:, tc, vp * 512:(vp + 1) * 512], in_=bk[:], func=AF.Gelu_apprx_tanh,
                            accum_out=st1[:, col:col + 1]))
                        jk, jkr = btmp()
                        k.op("act", [vtr[tc][vp]], [jkr, sres], lambda h, jk=jk, tc=tc, vp=vp, col=col: h.activation(
                            out=jk[:], in_=vt[:, tc, vp * 512:(vp + 1) * 512], func=AF.Square,
                            accum_out=st2[:, col:col + 1]))
                k.op("dve", [sres], [sres], lambda h: h.tensor_reduce(
                    out=stt[:, 0:4], in_=st1[:, 0:16].rearrange("p (a b) -> p a b", a=4), axis=mybir.AxisListType.X, op=ALU.add))
                k.op("dve", [sres], [sres], lambda h: h.tensor_reduce(
                    out=stt[:, 4:8], in_=st2[:, 0:16].rearrange("p (a b) -> p a b", a=4), axis=mybir.AxisListType.X, op=ALU.add))
                k.op("dve", [sres], [sres], lambda h: h.tensor_scalar(
                    out=stt[:, 8:12], in0=stt[:, 0:4], scalar1=1.0 / GH, scalar2=None, op0=ALU.mult))
                k.op("dve", [sres], [sres], lambda h: h.tensor_tensor(
                    out=stt[:, 12:16], in0=stt[:, 8:12], in1=stt[:, 8:12], op=ALU.mult))
                k.op("dve", [sres], [sres], lambda h: h.scalar_tensor_tensor(
                    out=stt[:, 16:20], in0=stt[:, 4:8], scalar=1.0 / GH, in1=stt[:, 12:16], op0=ALU.mult, op1=ALU.subtract))
                k.op("act", [sres], [sres], lambda h: h.activation(
                    out=stt[:, 20:24], in_=stt[:, 16:20], func=AF.Sqrt, scale=1.0, bias=EPS))
                k.op("dve", [sres], [sres], lambda h: h.reciprocal(out=stt[:, 24:28], in_=stt[:, 20:24]))
                for tc in range(4):
                    k.op("dve", [sres] + vtr[tc], vtr[tc], lambda h, tc=tc: h.tensor_scalar(
                        out=vt[:, tc, :], in0=vt[:, tc, :], scalar1=stt[:, 8 + tc:9 + tc], scalar2=stt[:, 24 + tc:25 + tc],
                        op0=ALU.subtract, op1=ALU.mult))
                for up in range(4):
                    wu, wur = wnext(("gu", j, tb, up))
                    wo, wor = wnext(("go", j, tb, up))
                    wuv = wu[:, 0:4096].rearrange("p (k n) -> p k n", k=8)
                    wov = wo[:, 0:4096].rearrange("p (k n) -> p k n", k=4)
                    yy, yyr = y[yi % 2], yr[yi % 2]
                    yi += 1
                    for fl in range(4):
                        fc = up * 4 + fl
                        g = fc // 2
                        ub, ubr = bank()
                        k.op("pe", [wur] + hnr, [ubr], lambda h, ub=ub, fl=fl: [
                            mm(ub[:], wuv[:, kk, fl * 128:(fl + 1) * 128], hn[:, kk, :], kk == 0, kk == KD - 1)
                            for kk in range(KD)][-1])
                        us, usr = ftmp()
                        k.op("act", [ubr], [usr], lambda h, ub=ub, us=us: h.activation(out=us[:], in_=ub[:], func=AF.Gelu_apprx_tanh))
                        sb_, sbr = bank()
                        k.op("pe", [vtr[tc][fc // 4] for tc in range(4)] + [wsbr], [sbr], lambda h, sb_=sb_, fc=fc, g=g: [
                            mm(sb_[:, tc * 128:(tc + 1) * 128], vt[:, tc, fc * 128:(fc + 1) * 128], wsb[:, g, :], True, True)
                            for tc in range(4)][-1])
                        sv, svr = ftmp()
                        k.op("dve", [sbr, cstr, cres], [svr], lambda h, sb_=sb_, sv=sv, fc=fc: h.scalar_tensor_tensor(
                            out=sv[:].rearrange("p (a t) -> p a t", a=4), in0=sb_[:].rearrange("p (a t) -> p a t", a=4),
                            scalar=cp[:, C_LNG + j * 16 + fc:C_LNG + j * 16 + fc + 1],
                            in1=cst[:, fc:fc + 1, :].broadcast_to([128, 4, 128]), op0=ALU.mult, op1=ALU.add))
                        k.op("pool", [usr, svr], [yyr[fl]], lambda h, us=us, sv=sv, yy=yy, fl=fl: h.tensor_tensor(
                            out=yy[:, fl, :], in0=us[:], in1=sv[:], op=ALU.mult))
                    for c in range(KD):
                        bk, bkr = bank()
                        k.op("pe", [wor] + yyr, [bkr], lambda h, bk=bk, c=c, yy=yy: [
                            mm(bk[:], wov[:, fl, c * 128:(c + 1) * 128], yy[:, fl, :], fl == 0, fl == 3)
                            for fl in range(4)][-1])
                        resid_add(c, tb, bk, bkr)

        def mla_phase(l, s_):
            j = l // 2
            k.barrier()
            hn = av(0, KD * T).rearrange("p (c t) -> p c t", c=KD)
            hnr = [Res(f"hnb{c}") for c in range(KD)]
            cq = av(4096, 3 * SEQ).rearrange("p (c t) -> p c t", c=3)
            cqr = [Res(f"cq{b}") for b in range(NB)]
            ckv = av(10240, 2 * SEQ).rearrange("p (c t) -> p c t", c=2)
            ckvr = [Res(f"ckv{b}") for b in range(NB)]
            kr = av(14336, SEQ)
            krr = [Res(f"kr{b}") for b in range(NB)]
            qn = av(16384, SEQ)
            qnr = [Res(f"qn{b}") for b in range(NB)]
            qr_ = av(18432, SEQ)
            qrr = [Res(f"qr{b}") for b in range(NB)]
            kn = av(20480, SEQ)
            knr = [Res(f"kn{b}") for b in range(NB)]
            vtk = av(22528, SEQ).rearrange("p (c d) -> p c d", c=16)
            vtkr = [Res(f"vk{b}") for b in range(NB)]
            oth = av(24576, SEQ)
            othr = [Res(f"ot{b}") for b in range(NB)]
            raw = av(26624, 3 * T, F32).rearrange("p (c t) -> p c t", c=3)
            rawr = [Res(f"raw{c}") for c in range(3)]
            pi_ = av(29696, T, F32)
            pi_i = pi_.bitcast(I32)
            pir = Res("pi")
            rt = [av(30720 + i * 1024, T, F32) for i in range(2)]
            rtr = [Res(f"rt{i}") for i in range(2)]
            cst_ = av(32768, SEQ, F32)
            cstr = [Res(f"cst{b}") for b in range(NB)]
            TWO_PI = 2.0 * math.pi
            C1 = 6.28125
            C2 = TWO_PI - C1

            def rope_tables(tb):
                sl = slice(tb * T, (tb + 1) * T)
                t0 = s_ * SEQ + tb * T
                k.dma([], [pir], [(pi_i, pos[0:1, t0:t0 + T].broadcast_to([128, T]))])
                a, ar = rt[0], rtr[0]
                b_, br_ = rt[1], rtr[1]
                k.op("dve", [pir], [ar], lambda h: h.tensor_copy(out=a, in_=pi_i))
                k.op("dve", [ar, cres], [ar], lambda h: h.tensor_scalar(
                    out=a, in0=a, scalar1=cp[:, C_INVF:C_INVF + 1], scalar2=None, op0=ALU.mult))
                k.op("dve", [ar, pir], [pir], lambda h: h.tensor_scalar(
                    out=pi_i, in0=a, scalar1=1.0 / TWO_PI, scalar2=None, op0=ALU.mult))
                k.op("dve", [pir], [br_], lambda h: h.tensor_copy(out=b_, in_=pi_i))
                k.op("dve", [ar, br_], [ar], lambda h: h.scalar_tensor_tensor(
                    out=a, in0=b_, scalar=-C1, in1=a, op0=ALU.mult, op1=ALU.add))
                k.op("dve", [ar, br_], [ar], lambda h: h.scalar_tensor_tensor(
                    out=a, in0=b_, scalar=-C2, in1=a, op0=ALU.mult, op1=ALU.add))
                k.op("dve", [ar], [ar], lambda h: h.tensor_scalar(
                    out=a, in0=a, scalar1=math.pi, scalar2=-math.pi, op0=ALU.min, op1=ALU.max))
                k.op("act", [ar], [cstr[tb]], lambda h: h.activation(out=cst_[64:128, sl], in_=a[64:128, :], func=AF.Sin))
                k.op("act", [ar], [br_], lambda h: h.activation(out=b_[0:64, :], in_=a[0:64, :], func=AF.Sin, scale=0.5))
                k.op("dve", [br_], [br_], lambda h: h.tensor_tensor(out=b_[0:64, :], in0=b_[0:64, :], in1=b_[0:64, :], op=ALU.mult))
                k.op("dve", [br_], [cstr[tb]], lambda h: h.tensor_scalar(
                    out=cst_[0:64, sl], in0=b_[0:64, :], scalar1=-2.0, scalar2=1.0, op0=ALU.mult, op1=ALU.add))

            def rope_norm(xx, xxr, ggcol, tb, out_ap, out_res):
                sl = slice(tb * T, (tb + 1) * T)
                sb_, sbr = bank()
                q_, qr2 = btmp()
                k.op("act", [xxr], [qr2], lambda h: h.activation(out=q_[0:64, :], in_=xx[0:64, :], func=AF.Square))
                k.op("pe", [qr2, cres], [sbr], lambda h: mm(sb_[0:64, :], ones[0:64, 0:64], q_[0:64, :], True, True))
                rs, rsr = ftmp()
                k.op("act", [sbr], [rsr], lambda h: h.activation(out=rs[0:64, :], in_=sb_[0:64, :], func=AF.Ln, scale=1.0 / RD, bias=EPS))
                k.op("act", [rsr], [rsr], lambda h: h.activation(out=rs[0:64, :], in_=rs[0:64, :], func=AF.Exp, scale=-0.5))
                t1, t1r = ftmp()
                k.op("dve", [xxr, qr2, cstr[tb], cres], [t1r], lambda h: h.scalar_tensor_tensor(
                    out=t1[:], in0=xx[:], scalar=ggcol, in1=cst_[:, sl], op0=ALU.mult, op1=ALU.mult))
                fb, fbr = bank()
                k.op("pe", [t1r, cres], [fbr], lambda h: mm(fb[0:64, :], fold[:, :], t1[:], True, True))
                k.op("dve", [fbr, rsr], [out_res], lambda h: h.tensor_tensor(out=out_ap, in0=fb[0:64, :], in1=rs[0:64, :], op=ALU.mult))

            w1, w1r = wnext(("ma1", j))
            w2, w2r = wnext(("ma2", j))
            w1v = w1[:, 0:3072].rearrange("p (k n) -> p k n", k=8)
            w2v = w2[:, 0:3072].rearrange("p (k n) -> p k n", k=8)
            for tb in range(NB):
                sl = slice(tb * T, (tb + 1) * T)
                rope_tables(tb)
                norm_h(tb, C_MIX + l * 8, hn, hnr)
                for (wv_, wr_, nchunk, dst, dres, gb, Dn) in ((w1v, w1r, 3, cq, cqr, C_QL + j * 3, QL), (w2v, w2r, 2, ckv, ckvr, C_KVL + j * 2, KVL)):
                    for m in range(nchunk):
                        bk, bkr = bank()
                        k.op("pe", [wr_] + hnr, [bkr], lambda h, bk=bk, m=m, wv_=wv_: [
                            mm(bk[:], wv_[:, kk, m * 128:(m + 1) * 128], hn[:, kk, :], kk == 0, kk == KD - 1)
                            for kk in range(KD)][-1])
                        k.op("act", [bkr], [rawr[m]], lambda h, bk=bk, m=m: h.activation(out=raw[:, m, :], in_=bk[:], func=AF.Copy))
                    rms_block([raw[:, m, :] for m in range(nchunk)], rawr[:nchunk], 128,
                              lambda c, gb=gb: cp[:, gb + c:gb + c + 1], Dn,
                              [dst[:, m, sl] for m in range(nchunk)], [dres[tb]] * nchunk)
                xx, xxr = bank()
                k.op("pe", [w2r] + hnr, [xxr], lambda h, xx=xx: [
                    mm(xx[:], w2v[:, kk, 256:384], hn[:, kk, :], kk == 0, kk == KD - 1) for kk in range(KD)][-1])
                rope_norm(xx, xxr, cp[:, C_KR + j:C_KR + j + 1], tb, kr[0:64, sl], krr[tb])

            kn2 = [kn, av(36864, SEQ)]
            vtk2 = [vtk, av(38912, SEQ).rearrange("p (c d) -> p c d", c=16)]
            knr2 = [knr, [Res(f"knb{b}") for b in range(NB)]]
            vtkr2 = [vtkr, [Res(f"vkb{b}") for b in range(NB)]]
            bankset[0] = [0, 1, 2, 3]
            hw = {}

            def views(h_):
                wh, whr = hw[h_]
                return (wh[:, 0:768].rearrange("p (k n) -> p k n", k=3),
                        wh[:, 768:1280].rearrange("p (k n) -> p k n", k=2),
                        wh[:, 1280:2304], whr)

            def proj_qn(h_, tbs):
                wq, wkv, wo, whr = views(h_)
                items = []
                for tb in tbs:
                    sl = slice(tb * T, (tb + 1) * T)
                    bk, bkr = bank()
                    k.op("pe", [whr, cqr[tb]], [bkr], lambda h, bk=bk, sl=sl: [
                        mm(bk[:], wq[:, kk, 0:128], cq[:, kk, sl], kk == 0, kk == 2) for kk in range(3)][-1])
                    items.append((bk[:], bkr, cpd[:, j:j + 1], qn[:, sl], qnr[tb]))
                rms_multi(items, 128)

            def rope_multi(items):
                sq, ss, rl, tl, fl = [], [], [], [], []
                for (xx, xxr, gg, tb, out, outr) in items:
                    q_, qr2 = btmp()
                    k.op("act", [xxr], [qr2], lambda h, q_=q_, xx=xx: h.activation(out=q_[0:64, :], in_=xx[0:64, :], func=AF.Square))
                    sq.append((q_, qr2))
                for i in range(len(items)):
                    sb_, sbr = bank()
                    k.op("pe", [sq[i][1], cres], [sbr], lambda h, sb_=sb_, i=i: mm(sb_[0:64, :], ones[0:64, 0:64], sq[i][0][0:64, :], True, True))
                    ss.append((sb_, sbr))
                for i in range(len(items)):
                    rs, rsr = ftmp()
                    k.op("act", [ss[i][1]], [rsr], lambda h, rs=rs, i=i: h.activation(out=rs[0:64, :], in_=ss[i][0][0:64, :], func=AF.Ln, scale=1.0 / RD, bias=EPS))
                    rl.append((rs, rsr))
                for rs, rsr in rl:
                    k.op("act", [rsr], [rsr], lambda h, rs=rs: h.activation(out=rs[0:64, :], in_=rs[0:64, :], func=AF.Exp, scale=-0.5))
                for i, (xx, xxr, gg, tb, out, outr) in enumerate(items):
                    t1, t1r = ftmp()
                    k.op("dve", [xxr, sq[i][1], cstr[tb], cres], [t1r], lambda h, t1=t1, xx=xx, gg=gg, tb=tb: h.scalar_tensor_tensor(
                        out=t1[:], in0=xx[:], scalar=gg, in1=cst_[:, tb * T:(tb + 1) * T], op0=ALU.mult, op1=ALU.mult))
                    tl.append((t1, t1r))
                for i in range(len(items)):
                    fb, fbr = bank()
                    k.op("pe", [tl[i][1], cres], [fbr], lambda h, fb=fb, i=i: mm(fb[0:64, :], fold[:, :], tl[i][0][:], True, True))
                    fl.append((fb, fbr))
                for i, (xx, xxr, gg, tb, out, outr) in enumerate(items):
                    k.op("dve", [fl[i][1], rl[i][1]], [outr], lambda h, i=i, out=out: h.tensor_tensor(
                        out=out, in0=fl[i][0][0:64, :], in1=rl[i][0][0:64, :], op=ALU.mult))

            def proj_qr(h_, tbs):
                wq, wkv, wo, whr = views(h_)
                items = []
                for tb in tbs:
                    sl = slice(tb * T, (tb + 1) * T)
                    xx, xxr = bank()
                    k.op("pe", [whr, cqr[tb]], [xxr], lambda h, xx=xx, sl=sl: [
                        mm(xx[:], wq[:, kk, 128:256], cq[:, kk, sl], kk == 0, kk == 2) for kk in range(3)][-1])
                    items.append((xx, xxr, cpd[:, 2 + j:3 + j], tb, qr_[0:64, sl], qrr[tb]))
                rope_multi(items)

            def proj_q(h_, tbs):
                for i in range(0, len(tbs), 2):
                    proj_qn(h_, tbs[i:i + 2])
                for i in range(0, len(tbs), 2):
                    proj_qr(h_, tbs[i:i + 2])

            def proj_kv(h_, tbs):
                wq, wkv, wo, whr = views(h_)
                b_ = h_ % 2
                for i in range(0, len(tbs), 2):
                    items = []
                    for tb in tbs[i:i + 2]:
                        sl = slice(tb * T, (tb + 1) * T)
                        bk, bkr = bank()
                        k.op("pe", [whr, ckvr[tb]], [bkr], lambda h, bk=bk, sl=sl: [
                            mm(bk[:], wkv[:, kk, 0:128], ckv[:, kk, sl], kk == 0, kk == 1) for kk in range(2)][-1])
                        items.append((bk[:], bkr, cp[:, C_KN + j:C_KN + j + 1], kn2[b_][:, sl], knr2[b_][tb]))
                    rms_multi(items, 128)
                for tb in tbs:
                    bv, bvr = bank()
                    k.op("pe", [whr, ckvr[tb]], [bvr], lambda h, bv=bv, tb=tb: [
                        mm(bv[:, kt4 * 128:(kt4 + 1) * 128], ckv[:, kk, (tb * 4 + kt4) * 128:(tb * 4 + kt4 + 1) * 128],
                           wkv[:, kk, 128:256], kk == 0, kk == 1)
                        for kt4 in range(4) for kk in range(2)][-1])
                    k.op("act", [bvr], [vtkr2[b_][tb]], lambda h, bv=bv, tb=tb: h.activation(
                        out=vtk2[b_][:, tb * 4:(tb + 1) * 4, :], in_=bv[:].rearrange("p (a d) -> p a d", a=4), func=AF.Copy))

            def att(h_, qb):
                b_ = h_ % 2
                kn_, vtk_, knr_, vtkr_ = kn2[b_], vtk2[b_], knr2[b_], vtkr2[b_]
                ob, obr = pbank[4 + qb % 2], pres[4 + qb % 2]
                db, dbr = pbank[6 + qb % 2], pres[6 + qb % 2]
                nkt = 4 * qb + 4

                def s_exp(kt):
                    jd = kt - 4 * qb
                    c0 = max(jd, 0) * 128
                    qs = slice(qb * T + c0, (qb + 1) * T)
                    ks = slice(kt * 128, (kt + 1) * 128)
                    sbk, sbkr = bank()
                    k.op("pe", [knr_[kt // 4], qnr[qb], krr[kt // 4], qrr[qb]], [sbkr], lambda h: [
                        mm(sbk[:, c0:T], kn_[:, ks], qn[:, qs], True, False),
                        mm(sbk[:, c0:T], kr[0:64, ks], qr_[0:64, qs], False, True)][-1])
                    pt, ptr = btmp()
                    k.op("act", [sbkr, cres], [ptr], lambda h: h.activation(
                        out=pt[:, c0:T], in_=sbk[:, c0:T], func=AF.Exp, bias=cpd[:, 8:9]))
                    if jd >= 0:
                        k.op("pool", [ptr], [ptr], lambda h: h.affine_select(
                            out=pt[:, c0:c0 + 128], in_=pt[:, c0:c0 + 128], pattern=[[1, 128]],
                            compare_op=ALU.is_ge, fill=0.0, base=0, channel_multiplier=-1))
                    return pt, ptr, c0

                pend = s_exp(0)
                for kt in range(nkt):
                    nxt = s_exp(kt + 1) if kt + 1 < nkt else None
                    pt, ptr, c0 = pend
                    k.op("pe", [ptr, vtkr_[kt // 4]], [obr], lambda h, pt=pt, c0=c0, kt=kt: mm(
                        ob[:, c0:T], vtk_[:, kt, :], pt[:, c0:T], kt == 0, kt == nkt - 1))
                    k.op("pe", [ptr, cres], [dbr], lambda h, pt=pt, c0=c0, kt=kt: mm(
                        db[:, c0:T], ones[:, :], pt[:, c0:T], kt == 0, kt == nkt - 1))
                    pend = nxt
                rd, rdr = ftmp()
                k.op("dve", [dbr], [rdr], lambda h: h.reciprocal(out=rd[:], in_=db[:]))
                k.op("dve", [obr, rdr], [othr[qb]], lambda h: h.tensor_tensor(
                    out=oth[:, qb * T:(qb + 1) * T], in0=ob[:], in1=rd[:], op=ALU.mult))

            def outproj(h_):
                wq, wkv, wo, whr = views(h_)
                for tb in range(NB):
                    for c in range(KD):
                        bk, bkr = bank()
                        k.op("pe", [whr, othr[tb]], [bkr], lambda h, bk=bk, c=c, tb=tb: mm(
                            bk[:], wo[:, c * 128:(c + 1) * 128], oth[:, tb * T:(tb + 1) * T], True, True))
                        if (tb * KD + c) % 3 == 2:
                            tp_, tpr = ftmp()
                            sl = slice(tb * T, (tb + 1) * T)
                            k.op("act", [bkr], [tpr], lambda h, bk=bk, tp_=tp_: h.activation(out=tp_[:], in_=bk[:], func=AF.Copy))
                            k.op("pool", [tpr, hres[c][tb]], [hres[c][tb]], lambda h, tp_=tp_, c=c, sl=sl: h.tensor_tensor(
                                out=hT[:, c, sl], in0=hT[:, c, sl], in1=tp_[:], op=ALU.add))
                        else:
                            resid_add(c, tb, bk, bkr)

            hw[0] = wnext(("mh", j, 0))
            proj_kv(0, [0, 1, 2, 3])
            proj_q(0, [0, 1, 2, 3])
            for h_ in range(H):
                for qb in range(3):
                    att(h_, qb)
                if h_ + 1 < H:
                    hw[h_ + 1] = wnext(("mh", j, h_ + 1))
                    proj_kv(h_ + 1, [0, 1, 2, 3])
                    proj_q(h_ + 1, [0, 1, 2])
                att(h_, 3)
                if h_ + 1 < H:
                    proj_q(h_ + 1, [3])
                outproj(h_)
            bankset[0] = list(range(8))

        outres = Res("out")
        for s_ in range(nseq):
            k.dma([], [hres[c][b] for c in range(KD) for b in range(NB)],
                  [(hT[:, c, :], xT[c * 128:(c + 1) * 128, s_ * SEQ:(s_ + 1) * SEQ]) for c in range(KD)])
            for l in layers:
                if do_mix:
                    if l % 2 == 0:
                        mla_phase(l, s_)
                    else:
                        gmlp_phase(l, s_)
                if do_ffn:
                    ffn_phase(l)
                if do_ple:
                    ple_phase(l, s_)
            k.dma([hres[c][b] for c in range(KD) for b in range(NB)], [outres],
                  [(outT[c * 128:(c + 1) * 128, s_ * SEQ:(s_ + 1) * SEQ], hT[:, c, :]) for c in range(KD)])
        assert wst["next"] == len(specs)
        if debug:
            k.barrier()
            dres = Res("dbg")
            ad = nc.dram_tensor("arena_dump", [128, ARENA], BF16, kind="ExternalOutput").ap()
            k.dma([], [dres], [(ad[:, :], arena[:, :])])
            k.eng["sp"].h.wait_ge(k.sems[dres.dsem], dres.dcnt)
        sp = k.eng["sp"]
        sp.h.wait_ge(k.sems[outres.dsem], outres.dcnt)
        print(f"[build] layers={layers} nseq={nseq} ops: " + " ".join(f"{n}={e.cnt}" for n, e in k.eng.items())
              + f" sems={len(k.sems)} n_inst~{k.n_inst}")
    return nc


def _pack_consts(inp):
    cp = np.zeros((128, NCP), np.float32)

    def chunks(v, n):
        return np.ascontiguousarray(v.reshape(n, 128).T)

    for l in range(DEPTH):
        cp[:, C_MIX + l * 8:C_MIX + l * 8 + 8] = chunks(inp["norm_mix"][l], 8)
        cp[:, C_FFN + l * 8:C_FFN + l * 8 + 8] = chunks(inp["norm_ffn"][l], 8)
        cp[:, C_PLE + l * 8:C_PLE + l * 8 + 8] = chunks(inp["norm_ple"][l], 8)
    perm = (np.arange(64) + 32) % 64
    for j in range(2):
        cp[:, C_QL + j * 3:C_QL + j * 3 + 3] = chunks(inp["mla_q_lora_g"][j], 3)
        cp[:, C_KVL + j * 2:C_KVL + j * 2 + 2] = chunks(inp["mla_kv_lora_g"][j], 2)
        cp[:, C_QN + j] = inp["mla_q_nope_g"][j]
        cp[:, C_KN + j] = inp["mla_k_nope_g"][j]
        cp[:64, C_QR + j] = inp["mla_q_rope_g"][j]
        cp[64:, C_QR + j] = inp["mla_q_rope_g"][j][perm]
        cp[:64, C_KR + j] = inp["mla_k_rope_g"][j]
        cp[64:, C_KR + j] = inp["mla_k_rope_g"][j][perm]
        cp[:, C_LNG + j * 16:C_LNG + j * 16 + 16] = chunks(inp["gmlp_ln_g"][j], 16)
        cp[:, C_LNB + j * 16:C_LNB + j * 16 + 16] = chunks(inp["gmlp_ln_b"][j], 16)
    invf = (10000.0 ** (-(np.arange(0, 64, 2, dtype=np.float32) / np.float32(64)))).astype(np.float32)
    cp[:, C_INVF] = np.concatenate([invf, invf, invf, invf])
    return cp


_PROG_CACHE = {}


def _get_prog(layers):
    key = tuple(layers)
    if key not in _PROG_CACHE:
        _PROG_CACHE[key] = build_program(list(layers))
    return _PROG_CACHE[key]


def kernel(**inp):
    inp = {k_: np.asarray(v) for k_, v in inp.items()}
    x = inp["x"].astype(np.float32, copy=False)
    p = inp["p"].astype(np.float32, copy=False)
    positions = inp["positions"].astype(np.int32, copy=False)
    cp = _pack_consts(inp)
    shared = {
        "cp": cp,
        "b_s": np.ascontiguousarray(inp["gmlp_b_s"].reshape(2, 1024)),
        "w_sT": np.ascontiguousarray(inp["gmlp_w_s"].transpose(0, 3, 1, 2).reshape(2, 128, 1024)),
    }
    for n in ("mla_w_down", "mla_w_uq", "mla_w_ukv", "mla_w_out", "gmlp_w_in", "gmlp_w_out",
              "ffn_w_up", "ffn_w_down", "ple_w_gate", "ple_w_proj"):
        shared[n] = np.ascontiguousarray(inp[n], dtype=np.float32)
    per_core = []
    for c in range(N_CORES):
        b0 = c * NSEQ
        xs = x[b0:b0 + NSEQ].reshape(NSEQ * SEQ, D)
        ps = p[:, b0:b0 + NSEQ].reshape(DEPTH, NSEQ * SEQ, PLE)
        per_core.append({
            "xT": np.ascontiguousarray(xs.T),
            "pT": np.ascontiguousarray(ps.transpose(0, 2, 1).reshape(DEPTH * PLE, NSEQ * SEQ)),
            "pos": np.ascontiguousarray(positions[b0:b0 + NSEQ].reshape(1, NSEQ * SEQ)),
        })
    launches = [list(range(DEPTH))] if FUSED else [[l] for l in range(DEPTH)]
    for layers in launches:
        nc = _get_prog(layers)
        in_maps = [dict(shared, **pc) for pc in per_core]
        res = run_bass_kernel_spmd(nc, in_maps, core_ids=list(range(N_CORES)))
        for c in range(N_CORES):
            per_core[c]["xT"] = np.ascontiguousarray(res.results[c]["outT"])
    out = np.empty((BATCH, SEQ, D), np.float32)
    for c in range(N_CORES):
        out[c * NSEQ:(c + 1) * NSEQ] = per_core[c]["xT"].T.reshape(NSEQ, SEQ, D)
    return out
```
